# Optimizing a Trainium2 kernel written in Bass

```python
import math
import jax
import jax.numpy as jnp
from jax import lax
import numpy as np

D_MODEL = 1024
BATCH = 8
SEQ = 8192
DEPTH = 2

N_MIXERS = 4
GROUP_WIDTH = D_MODEL // N_MIXERS
MIX_WIDTH = N_MIXERS * GROUP_WIDTH
N_HEADS = 4
HEAD_DIM = GROUP_WIDTH // N_HEADS
Q_BLOCK = 128
NORM_EPS = 1e-6
DIFF_QK_DIM = HEAD_DIM // 2
ALIBI_MAX_EXP = 8.0
SSM_D_STATE = 128
SSM_BC_GROUPS = 2
SSM_CONV = 4
SSM_CHUNK = 128
SSM_CONV_DIM = GROUP_WIDTH + 2 * SSM_BC_GROUPS * SSM_D_STATE
RWKV_DECAY_RANK = 32
RWKV_A_RANK = 32
RWKV_GATE_RANK = 64
RWKV_PROJ = 3 * GROUP_WIDTH + RWKV_DECAY_RANK + RWKV_A_RANK + RWKV_GATE_RANK
RWKV_LN_EPS = 64e-5
D_FF = 2816
FFN_CONV = 3
IN_SPLITS = (2 * N_HEADS * DIFF_QK_DIM, 2 * N_HEADS * DIFF_QK_DIM, GROUP_WIDTH,
             GROUP_WIDTH, SSM_CONV_DIM, N_HEADS,
             GROUP_WIDTH, GROUP_WIDTH, GROUP_WIDTH, N_HEADS,
             RWKV_PROJ)
D_IN = sum(IN_SPLITS)

kernel_name = 'hymba_style_diff_ssd_fox_rwkv7_trunk'


def rms_norm(x, g, eps=NORM_EPS):
    xf = x.astype(jnp.float32)
    y = xf * lax.rsqrt(jnp.mean(xf * xf, axis=-1, keepdims=True) + eps)
    return (y * g.astype(jnp.float32)).astype(x.dtype)


def _split_cols(x, sizes):
    out, start = [], 0
    for n in sizes:
        out.append(x[..., start:start + n])
        start += n
    return out


def causal_dwconv(x, w, b):
    k = w.shape[0]
    y = lax.conv_general_dilated(x, w[:, None, :].astype(x.dtype), window_strides=(1,),
                                 padding=[(k - 1, 0)], dimension_numbers=('NWC', 'WIO', 'NWC'),
                                 feature_group_count=x.shape[-1])
    return y + b.astype(x.dtype)


def alibi_slopes(n):
    return jnp.exp2(-ALIBI_MAX_EXP * jnp.arange(1, n + 1, dtype=jnp.float32) / n)


def _query_blocks(t):
    b, m, s, d = t.shape
    return jnp.moveaxis(t.reshape(b, m, s // Q_BLOCK, Q_BLOCK, d), 2, 0)


def diff_attention(q, k, v, lam, lam_init, subln_g):
    b, s, _ = q.shape
    nb = s // Q_BLOCK
    q = (q.reshape(b, s, 2 * N_HEADS, DIFF_QK_DIM) * DIFF_QK_DIM ** -0.5).transpose(0, 2, 1, 3)
    k = k.reshape(b, s, 2 * N_HEADS, DIFF_QK_DIM).transpose(0, 2, 1, 3)
    v = v.reshape(b, s, N_HEADS, HEAD_DIM).transpose(0, 2, 1, 3)
    slopes = jnp.repeat(alibi_slopes(N_HEADS), 2)
    pos = jnp.arange(s)

    def block(args):
        q_blk, i = args
        t = i * Q_BLOCK + jnp.arange(Q_BLOCK)
        dist = (t[:, None] - pos[None, :]).astype(jnp.float32)
        logits = (jnp.einsum('bmqd,bmsd->bmqs', q_blk, k).astype(jnp.float32)
                  - slopes[:, None, None] * dist)
        logits = jnp.where(dist >= 0, logits, -jnp.inf)
        p = jax.nn.softmax(logits, axis=-1).reshape(b, N_HEADS, 2, Q_BLOCK, s)
        att = p[:, :, 0] - lam * p[:, :, 1]
        return jnp.einsum('bhqs,bhsd->bhqd', att.astype(v.dtype), v)

    o = lax.map(block, (_query_blocks(q), jnp.arange(nb)))
    o = jnp.moveaxis(o, 0, 2).reshape(b, N_HEADS, s, HEAD_DIM)
    o = rms_norm(o, subln_g) * (1.0 - lam_init)
    return o.transpose(0, 2, 1, 3).reshape(b, s, GROUP_WIDTH)


def forgetting_attention(q, k, v, f_logit, norm_g):
    b, s, _ = q.shape
    nb = s // Q_BLOCK
    q = (q.reshape(b, s, N_HEADS, HEAD_DIM) * HEAD_DIM ** -0.5).transpose(0, 2, 1, 3)
    k = k.reshape(b, s, N_HEADS, HEAD_DIM).transpose(0, 2, 1, 3)
    v = v.reshape(b, s, N_HEADS, HEAD_DIM).transpose(0, 2, 1, 3)
    c = jnp.cumsum(jax.nn.log_sigmoid(f_logit.astype(jnp.float32)), axis=1).transpose(0, 2, 1)
    c_blocks = jnp.moveaxis(c.reshape(b, N_HEADS, nb, Q_BLOCK), 2, 0)
    pos = jnp.arange(s)

    def block(args):
        q_blk, c_blk, i = args
        t = i * Q_BLOCK + jnp.arange(Q_BLOCK)
        logits = (jnp.einsum('bhqd,bhsd->bhqs', q_blk, k).astype(jnp.float32)
                  + c_blk[..., :, None] - c[:, :, None, :])
        logits = jnp.where(pos[None, :] <= t[:, None], logits, -jnp.inf)
        p = jax.nn.softmax(logits, axis=-1)
        return jnp.einsum('bhqs,bhsd->bhqd', p.astype(v.dtype), v)

    o = lax.map(block, (_query_blocks(q), c_blocks, jnp.arange(nb)))
    o = jnp.moveaxis(o, 0, 2).reshape(b, N_HEADS, s, HEAD_DIM)
    o = rms_norm(o, norm_g)
    return o.transpose(0, 2, 1, 3).reshape(b, s, GROUP_WIDTH)


def segsum(a):
    cs = jnp.cumsum(a, axis=-1)
    n = a.shape[-1]
    mask = jnp.tril(jnp.ones((n, n), dtype=bool))
    return jnp.where(mask, cs[..., :, None] - cs[..., None, :], -jnp.inf)


def ssd_chunked(x, dA, bh, ch):
    b, s, h, p = x.shape
    n = bh.shape[-1]
    c, l = s // SSM_CHUNK, SSM_CHUNK
    x = x.reshape(b, c, l, h, p)
    bh = bh.reshape(b, c, l, h, n)
    ch = ch.reshape(b, c, l, h, n)
    dA = dA.reshape(b, c, l, h).transpose(0, 3, 1, 2)
    a_cs = jnp.cumsum(dA, axis=-1)
    decay_in = jnp.exp(segsum(dA))
    scores = jnp.einsum('bclhn,bcshn->bhcls', ch, bh) * decay_in
    y_diag = jnp.einsum('bhcls,bcshp->bclhp', scores, x)
    decay_states = jnp.exp(a_cs[..., -1:] - a_cs)
    states = jnp.einsum('bclhn,bhcl,bclhp->bchpn', bh, decay_states, x)
    states = jnp.concatenate([jnp.zeros_like(states[:, :1]), states], axis=1)
    decay_chunk = jnp.exp(segsum(jnp.pad(a_cs[..., -1], ((0, 0), (0, 0), (1, 0)))))
    start_states = jnp.einsum('bhzc,bchpn->bzhpn', decay_chunk, states)[:, :-1]
    y_off = jnp.einsum('bclhn,bchpn,bhcl->bclhp', ch, start_states, jnp.exp(a_cs))
    return (y_diag + y_off).reshape(b, s, h, p)


def ssd_mixer(z, xbc, dt_raw, conv_w, conv_b, dt_bias, a_log, d_skip, norm_g):
    b, s, _ = z.shape
    xbc = jax.nn.silu(causal_dwconv(xbc, conv_w, conv_b)).astype(jnp.float32)
    xs, bm, cm = _split_cols(xbc, (GROUP_WIDTH, SSM_BC_GROUPS * SSM_D_STATE, SSM_BC_GROUPS * SSM_D_STATE))
    rep = N_HEADS // SSM_BC_GROUPS
    x_h = xs.reshape(b, s, N_HEADS, HEAD_DIM)
    bh = jnp.repeat(bm.reshape(b, s, SSM_BC_GROUPS, SSM_D_STATE), rep, axis=2)
    ch = jnp.repeat(cm.reshape(b, s, SSM_BC_GROUPS, SSM_D_STATE), rep, axis=2)
    dt = jax.nn.softplus(dt_raw.astype(jnp.float32) + dt_bias.astype(jnp.float32))
    a = -jnp.exp(a_log.astype(jnp.float32))
    y = ssd_chunked(x_h * dt[..., None], dt * a, bh, ch)
    y = (y + x_h * d_skip.astype(jnp.float32)[:, None]).reshape(b, s, GROUP_WIDTH)
    yg = (y * jax.nn.silu(z.astype(jnp.float32))).reshape(b, s, SSM_BC_GROUPS, GROUP_WIDTH // SSM_BC_GROUPS)
    yg = yg * lax.rsqrt(jnp.mean(yg * yg, axis=-1, keepdims=True) + NORM_EPS)
    return yg.reshape(b, s, GROUP_WIDTH) * norm_g.astype(jnp.float32)


def wkv7_scan(r, decay, k, v, kk, a):
    b, s, h, n = r.shape

    def step(state, inp):
        r_t, w_t, k_t, v_t, kk_t, a_t = inp
        sa = jnp.einsum('bhvk,bhk->bhv', state, -kk_t)
        state = (state * w_t[:, :, None, :] + sa[..., None] * (kk_t * a_t)[:, :, None, :]
                 + v_t[..., None] * k_t[:, :, None, :])
        return state, jnp.einsum('bhvk,bhk->bhv', state, r_t)

    xs = tuple(jnp.moveaxis(t, 1, 0) for t in (r, decay, k, v, kk, a))
    _, ys = lax.scan(step, jnp.zeros((b, h, n, n), jnp.float32), xs)
    return jnp.moveaxis(ys, 0, 1)


def rwkv7_time_mix(p, mu, w0, w2, a0, a2, g2, k_k, k_a, r_k, ln_w, ln_b):
    b, s, _ = p.shape
    p = p.astype(jnp.float32)
    p_prev = jnp.pad(p, ((0, 0), (1, 0), (0, 0)))[:, :-1]
    p = p + (p_prev - p) * mu
    r, k, v, xw, xa, xg = _split_cols(p, (GROUP_WIDTH, GROUP_WIDTH, GROUP_WIDTH,
                                          RWKV_DECAY_RANK, RWKV_A_RANK, RWKV_GATE_RANK))
    w = -jax.nn.softplus(-(w0 + jnp.tanh(xw) @ w2)) - 0.5
    decay = jnp.exp(-jnp.exp(w))
    a = jax.nn.sigmoid(a0 + xa @ a2)
    g = jax.nn.sigmoid(xg) @ g2
    heads = lambda t: t.reshape(b, s, N_HEADS, HEAD_DIM)
    kk = heads(k * k_k)
    kk = kk * lax.rsqrt(jnp.sum(kk * kk, axis=-1, keepdims=True) + 1e-12)
    k = k * (1.0 + (a - 1.0) * k_a)
    r, k, v, decay, a = heads(r), heads(k), heads(v), heads(decay), heads(a)
    o = wkv7_scan(r, decay, k, v, kk, a)
    mean = jnp.mean(o, axis=-1, keepdims=True)
    var = jnp.mean(jnp.square(o - mean), axis=-1, keepdims=True)
    o = ((o - mean) * lax.rsqrt(var + RWKV_LN_EPS)).reshape(b, s, GROUP_WIDTH) * ln_w + ln_b
    o = o + (jnp.sum(r * k * r_k, axis=-1, keepdims=True) * v).reshape(b, s, GROUP_WIDTH)
    return o * g


def conv_glu_ffn(h, w_gate, w_up, conv_w, conv_b, w_down):
    gate = causal_dwconv(h @ w_gate, conv_w, conv_b)
    return (jax.nn.gelu(gate, approximate=True) * (h @ w_up)) @ w_down


def setup_inputs(seed: int = 0) -> dict:
    key = jax.random.key(seed)
    keys = iter(jax.random.split(key, 48))
    f32 = jnp.float32
    L, G, H = DEPTH, GROUP_WIDTH, N_HEADS

    def normal(shape, scale):
        return jax.random.normal(next(keys), shape, f32) * scale

    def gain(shape):
        return 1.0 + normal(shape, 0.05)

    def uniform(shape, lo, hi):
        return jax.random.uniform(next(keys), shape, f32, lo, hi)

    dt_init = jnp.exp(uniform((L, H), math.log(1e-3), math.log(1e-1)))
    return {
        'x': normal((BATCH, SEQ, D_MODEL), 1.0),
        'norm_mix_pre': gain((L, D_MODEL)),
        'norm_mix_post': gain((L, D_MODEL)),
        'norm_ffn_pre': gain((L, D_MODEL)),
        'norm_ffn_post': gain((L, D_MODEL)),
        'w_in': normal((L, D_MODEL, D_IN), D_MODEL ** -0.5),
        'w_out': normal((L, MIX_WIDTH, D_MODEL), MIX_WIDTH ** -0.5),
        'diff_lambda_q1': normal((L, DIFF_QK_DIM), 0.1),
        'diff_lambda_k1': normal((L, DIFF_QK_DIM), 0.1),
        'diff_lambda_q2': normal((L, DIFF_QK_DIM), 0.1),
        'diff_lambda_k2': normal((L, DIFF_QK_DIM), 0.1),
        'diff_subln': gain((L, HEAD_DIM)),
        'ssm_conv_w': normal((L, SSM_CONV, SSM_CONV_DIM), SSM_CONV ** -0.5),
        'ssm_conv_b': normal((L, SSM_CONV_DIM), 0.02),
        'ssm_dt_bias': dt_init + jnp.log(-jnp.expm1(-dt_init)),
        'ssm_a_log': jnp.log(uniform((L, H), 1.0, 16.0)),
        'ssm_d': gain((L, H)),
        'ssm_norm': gain((L, G)),
        'fox_f_bias': jnp.linspace(1.0, 4.0, H)[None, :] + normal((L, H), 0.1),
        'fox_norm': gain((L, HEAD_DIM)),
        'rwkv_mu': uniform((L, RWKV_PROJ), 0.2, 0.8),
        'rwkv_w0': jnp.linspace(-6.0, -1.0, G)[None, :] + normal((L, G), 0.1),
        'rwkv_w2': normal((L, RWKV_DECAY_RANK, G), 0.5 * RWKV_DECAY_RANK ** -0.5),
        'rwkv_a0': normal((L, G), 0.1),
        'rwkv_a2': normal((L, RWKV_A_RANK, G), 0.5 * RWKV_A_RANK ** -0.5),
        'rwkv_g2': normal((L, RWKV_GATE_RANK, G), RWKV_GATE_RANK ** -0.5),
        'rwkv_k_k': 0.85 + normal((L, G), 0.02),
        'rwkv_k_a': gain((L, G)),
        'rwkv_r_k': normal((L, H, HEAD_DIM), 0.1),
        'rwkv_ln_w': gain((L, G)),
        'rwkv_ln_b': normal((L, G), 0.02),
        'ffn_w_gate': normal((L, D_MODEL, D_FF), D_MODEL ** -0.5),
        'ffn_w_up': normal((L, D_MODEL, D_FF), D_MODEL ** -0.5),
        'ffn_conv_w': normal((L, FFN_CONV, D_FF), FFN_CONV ** -0.5),
        'ffn_conv_b': normal((L, D_FF), 0.02),
        'ffn_w_down': normal((L, D_FF, D_MODEL), D_FF ** -0.5),
    }


def reference(x, norm_mix_pre, norm_mix_post, norm_ffn_pre, norm_ffn_post, w_in, w_out,
              diff_lambda_q1, diff_lambda_k1, diff_lambda_q2, diff_lambda_k2, diff_subln,
              ssm_conv_w, ssm_conv_b, ssm_dt_bias, ssm_a_log, ssm_d, ssm_norm,
              fox_f_bias, fox_norm,
              rwkv_mu, rwkv_w0, rwkv_w2, rwkv_a0, rwkv_a2, rwkv_g2, rwkv_k_k, rwkv_k_a, rwkv_r_k,
              rwkv_ln_w, rwkv_ln_b,
              ffn_w_gate, ffn_w_up, ffn_conv_w, ffn_conv_b, ffn_w_down):
    for l in range(DEPTH):
        h = rms_norm(x, norm_mix_pre[l])
        (dq, dk, dv, sz, sxbc, sdt, fq, fk, fv, ff, rp) = _split_cols(h @ w_in[l], IN_SPLITS)
        lam_init = 0.8 - 0.6 * math.exp(-0.3 * l)
        lam = (jnp.exp(jnp.sum(diff_lambda_q1[l] * diff_lambda_k1[l]).astype(jnp.float32))
               - jnp.exp(jnp.sum(diff_lambda_q2[l] * diff_lambda_k2[l]).astype(jnp.float32)) + lam_init)
        y_diff = diff_attention(dq, dk, dv, lam, lam_init, diff_subln[l])
        y_ssm = ssd_mixer(sz, sxbc, sdt, ssm_conv_w[l], ssm_conv_b[l], ssm_dt_bias[l], ssm_a_log[l],
                          ssm_d[l], ssm_norm[l])
        y_fox = forgetting_attention(fq, fk, fv, ff + fox_f_bias[l], fox_norm[l])
        y_rwkv = rwkv7_time_mix(rp, rwkv_mu[l], rwkv_w0[l], rwkv_w2[l], rwkv_a0[l], rwkv_a2[l],
                                rwkv_g2[l], rwkv_k_k[l], rwkv_k_a[l], rwkv_r_k[l],
                                rwkv_ln_w[l], rwkv_ln_b[l])
        y = jnp.concatenate([t.astype(x.dtype) for t in (y_diff, y_ssm, y_fox, y_rwkv)], axis=-1) @ w_out[l]
        x = x + rms_norm(y, norm_mix_post[l])
        h = rms_norm(x, norm_ffn_pre[l])
        f = conv_glu_ffn(h, ffn_w_gate[l], ffn_w_up[l], ffn_conv_w[l], ffn_conv_b[l], ffn_w_down[l])
        x = x + rms_norm(f, norm_ffn_post[l])
    return x
```

```python
import contextlib, math
import numpy as np
import ml_dtypes
from concourse.bass_utils import run_bass_kernel_spmd
import contextlib
import numpy as np
import concourse.bass as bass
import concourse.mybir as mybir

F32 = mybir.dt.float32
BF16 = mybir.dt.bfloat16
AF = mybir.ActivationFunctionType
ALU = mybir.AluOpType
AX = mybir.AxisListType

SAME_ENGINE_RAW_SYNC = True


class Reg:
    __slots__ = ("w", "r")

    def __init__(self):
        self.w = None
        self.r = {}


class V:
    __slots__ = ("t", "ap", "key")

    def __init__(self, t, ap, key):
        self.t = t
        self.ap = ap
        self.key = key

    def rearrange(self, pat, **kw):
        return V(self.t, self.ap.rearrange(pat, **kw), self.key)

    def partition_broadcast(self, n):
        return V(self.t, self.ap.partition_broadcast(n), self.key)

    def bitcast(self, dt):
        return V(self.t, self.ap.bitcast(dt), self.key)

    def __getitem__(self, idx):
        return V(self.t, self.ap[idx], self.key)


class _Keyed:
    def __init__(self, t, key):
        self.t = t
        self.key = key

    def __getitem__(self, idx):
        return V(self.t, self.t.h[idx], self.key)


class T:
    def __init__(self, h, name, excl=False):
        self.h = h
        self.name = name
        self.excl = excl
        self.regs = {None: Reg()}

    def __getitem__(self, idx):
        return V(self, self.h[idx], None)

    def k(self, key):
        return _Keyed(self, key)

    def regs_of(self, key):
        if key is None:
            return list(self.regs.values())
        if key not in self.regs:
            r = Reg()
            self.regs[key] = r
        return [self.regs[None], self.regs[key]]


class KB:
    COMPUTE = ("pe", "act", "dve", "pool")

    def __init__(self, nc, es, n_dma_sems=40):
        self.nc = nc
        self.es = es
        self.engs = {"pe": nc.tensor, "act": nc.scalar, "dve": nc.vector, "pool": nc.gpsimd, "sp": nc.sync}
        self.sem = {}
        self.tick = {}
        for e in self.COMPUTE:
            self.sem[e] = es.enter_context(nc.semaphore("sem_" + e))
            self.tick[e] = 0
        self.waited = {}
        self.dma_sems = []
        for i in range(n_dma_sems):
            self.dma_sems.append([es.enter_context(nc.semaphore("dsem%d" % i)), 0])
        self.dma_next = 0
        self.n_inst = 0
        self.n_wait = 0

    def sb(self, name, shape, dtype, es=None):
        es = es or self.es
        self._uid = getattr(self, "_uid", 0) + 1
        name = "%s_u%d" % (name, self._uid)
        return T(es.enter_context(self.nc.sbuf_tensor(name, list(shape), dtype)), name)

    def ps(self, name, shape, dtype, es=None):
        es = es or self.es
        return T(es.enter_context(self.nc.psum_tensor(name, list(shape), dtype)), name, excl=True)

    def dram(self, name, shape, dtype, kind="Internal"):
        return T(self.nc.dram_tensor(name, list(shape), dtype, kind=kind).ap(), name)

    def _collect(self, reads, writes):
        deps = {}

        def add(tok, raw):
            if tok is None:
                return
            k = tok[3]
            old = deps.get(k)
            if old is None or old[1] < tok[1]:
                deps[k] = (tok[0], tok[1], tok[2], old[3] or raw if old else raw)
            elif raw and not old[3]:
                deps[k] = (old[0], old[1], old[2], True)

        for v in reads:
            for reg in v.t.regs_of(v.key):
                add(reg.w, True)
        for v in writes:
            for reg in v.t.regs_of(v.key):
                add(reg.w, False)
                for tok in reg.r.values():
                    add(tok, False)
        return deps

    def _waits(self, eng, deps):
        for k, (sem, val, src, raw) in deps.items():
            if src == eng:
                if eng == "pe":
                    continue
                if not SAME_ENGINE_RAW_SYNC:
                    continue
            wk = (eng, k)
            if self.waited.get(wk, 0) >= val:
                continue
            self.engs[eng].wait_ge(sem, val)
            self.n_wait += 1
            self.waited[wk] = val

    def _update(self, tok, eng_key, reads, writes):
        for v in writes:
            if v.key is None:
                for reg in v.t.regs.values():
                    reg.w = tok
                    reg.r = {}
            else:
                regs = v.t.regs_of(v.key)
                regs[1].w = tok
                regs[1].r = {}
        for v in reads:
            if v.key is None:
                v.t.regs[None].r[eng_key] = tok
            else:
                v.t.regs_of(v.key)[1].r[eng_key] = tok

    def op(self, eng, fn, reads, writes):
        xr = [v for v in reads if v.t.excl]
        if xr:
            writes = list(writes) + [V(v.t, v.ap, None) for v in xr]
        deps = self._collect(reads, writes)
        self._waits(eng, deps)
        inst = fn()
        self.tick[eng] += 1
        inst.then_inc(self.sem[eng], 1)
        tok = (self.sem[eng], self.tick[eng], eng, eng)
        self._update(tok, eng, reads, writes)
        self.n_inst += 1
        return inst

    def dma(self, out, in_, q="sp", **kw):
        slot = self.dma_sems[self.dma_next % len(self.dma_sems)]
        self.dma_next += 1
        sem, cnt = slot
        kid = "d%d" % id(sem)
        if cnt > 0:
            wk = (q, kid)
            if self.waited.get(wk, 0) < cnt:
                self.engs[q].wait_ge(sem, cnt)
                self.waited[wk] = cnt
        reads, writes = [in_], [out]
        deps = self._collect(reads, writes)
        for k, (s, val, src, raw) in deps.items():
            wk = (q, k)
            if src == q and not k.startswith("d"):
                pass
            if self.waited.get(wk, 0) >= val:
                continue
            self.engs[q].wait_ge(s, val)
            self.n_wait += 1
            self.waited[wk] = val
        inst = self.engs[q].dma_start(out=out.ap, in_=in_.ap, **kw)
        inst.then_inc(sem, 16)
        slot[1] = cnt + 16
        tok = (sem, cnt + 16, "dma", kid)
        self._update(tok, kid, reads, writes)
        self.n_inst += 1
        return inst

    def barrier(self):
        for e in self.engs:
            for s in self.COMPUTE:
                if s == e or self.tick[s] == 0:
                    continue
                wk = (e, s)
                if self.waited.get(wk, 0) < self.tick[s]:
                    self.engs[e].wait_ge(self.sem[s], self.tick[s])
                    self.waited[wk] = self.tick[s]
            for sem, cnt in self.dma_sems:
                if cnt == 0:
                    continue
                wk = (e, "d%d" % id(sem))
                if self.waited.get(wk, 0) < cnt:
                    self.engs[e].wait_ge(sem, cnt)
                    self.waited[wk] = cnt

    def finish(self):
        self.barrier()

    @contextlib.contextmanager
    def scope(self):
        with contextlib.ExitStack() as es:
            yield es
            self.barrier()

    def mm(self, out, lhsT, rhs, start=True, stop=True, **kw):
        return self.op("pe", lambda: self.nc.tensor.matmul(out.ap, lhsT.ap, rhs.ap, start=start, stop=stop, **kw),
                       [lhsT, rhs], [out])

    def tr(self, out, in_, ident):
        return self.op("pe", lambda: self.nc.tensor.transpose(out.ap, in_.ap, ident.ap), [in_, ident], [out])

    def act(self, out, in_, func, bias=None, scale=None, accum=None, extra_reads=()):
        kw = {}
        reads = [in_] + list(extra_reads)
        if bias is not None:
            if isinstance(bias, V):
                kw["bias"] = bias.ap
                reads.append(bias)
            else:
                kw["bias"] = bias
        if scale is not None:
            if isinstance(scale, V):
                kw["scale"] = scale.ap
                reads.append(scale)
            else:
                kw["scale"] = scale
        writes = [out]
        if accum is not None:
            kw["accum_out"] = accum.ap
            writes.append(accum)
        return self.op("act", lambda: self.nc.scalar.activation(out=out.ap, in_=in_.ap, func=func, **kw), reads, writes)

    def _e(self, eng):
        return self.engs[eng]

    def tt(self, out, a, b, op, eng="dve"):
        return self.op(eng, lambda: self._e(eng).tensor_tensor(out.ap, a.ap, b.ap, op), [a, b], [out])

    def ts(self, out, a, s1, op0, s2=None, op1=None, eng="dve", accum=None):
        reads = [a]
        sa1 = s1
        if isinstance(s1, V):
            reads.append(s1)
            sa1 = s1.ap
        sa2 = s2
        if isinstance(s2, V):
            reads.append(s2)
            sa2 = s2.ap
        writes = [out]
        kw = {}
        if op1 is not None:
            kw["op1"] = op1
        if accum is not None:
            kw["accum_out"] = accum.ap
            writes.append(accum)
        return self.op(eng, lambda: self._e(eng).tensor_scalar(out.ap, a.ap, sa1, sa2, op0, **kw), reads, writes)

    def stt(self, out, a, s, b, op0, op1, eng="dve", accum=None):
        assert eng == "dve"
        reads = [a, b]
        sa = s
        if isinstance(s, V):
            reads.append(s)
            sa = s.ap
        writes = [out]
        kw = {}
        if accum is not None:
            kw["accum_out"] = accum.ap
            writes.append(accum)
        return self.op(eng, lambda: self._e(eng).scalar_tensor_tensor(out.ap, a.ap, sa, b.ap, op0, op1, **kw), reads, writes)

    def cp(self, out, in_, eng="dve"):
        if eng == "act":
            return self.op("act", lambda: self.nc.scalar.copy(out.ap, in_.ap), [in_], [out])
        return self.op(eng, lambda: self._e(eng).tensor_copy(out.ap, in_.ap), [in_], [out])

    def scan(self, out, d0, d1, init, op0, op1):
        reads = [d0, d1]
        ia = init
        if isinstance(init, V):
            reads.append(init)
            ia = init.ap
        return self.op("dve", lambda: self.nc.vector.tensor_tensor_scan(out.ap, d0.ap, d1.ap, ia, op0, op1), reads, [out])

    def red(self, out, in_, op, axis=AX.X, eng="dve"):
        return self.op(eng, lambda: self._e(eng).tensor_reduce(out.ap, in_.ap, axis, op), [in_], [out])

    def recip(self, out, in_):
        return self.op("dve", lambda: self.nc.vector.reciprocal(out.ap, in_.ap), [in_], [out])

    def memset(self, out, val, eng="dve"):
        return self.op(eng, lambda: self._e(eng).memset(out.ap, val), [], [out])


D = 1024
DIN = 3464
DFF = 2816
SEGS = [("dq", 0, 256), ("dk", 256, 256), ("dv", 512, 256), ("sz", 768, 256), ("sxbc", 1024, 768),
        ("fq", 1796, 256), ("fk", 2052, 256), ("fv", 2308, 256), ("rp", 2568, 896), ("sdt", 1792, 4), ("ff", 2564, 4)]
SOFF = {}
_o = 0
for _n, _c, _w in SEGS:
    SOFF[_n] = _o
    _o += _w
assert _o == DIN


class Ctx:
    pass


def run_pipelined(gen_iter, depth):
    active = []
    it = iter(gen_iter)
    done = False
    while True:
        if not done and len(active) < depth:
            try:
                active.append(next(it))
            except StopIteration:
                done = True
        if not active:
            if done:
                break
            continue
        for g in list(active):
            try:
                next(g)
            except StopIteration:
                active.remove(g)


def build(S, L, dbg=(), ext_in=()):
    nc = bass.Bass("TRN2", target_bir_lowering=False)
    es = contextlib.ExitStack()
    kb = KB(nc, es)
    _dram = kb.dram
    kb.dram = lambda name, shape, dt: _dram(name, shape, dt, kind=("ExternalOutput" if name in dbg else ("ExternalInput" if name in ext_in else "Internal")))
    c = Ctx()
    c.nc, c.kb, c.S, c.L = nc, kb, S, L
    NG = S // 512
    c.NG = NG
    def ext(name, shape, dt=F32):
        return T(nc.dram_tensor(name, list(shape), dt, kind="ExternalInput").ap(), name)

    c.x = ext("x", [S, D])
    c.w_in = ext("w_in", [L, D, DIN])
    c.norm_mix_pre = ext("norm_mix_pre", [L, D])
    c.ident = ext("ident", [128, 128])
    c.out = T(nc.dram_tensor("out", [S, D], F32, kind="ExternalOutput").ap(), "out")
    c.xT = kb.dram("xT", [D, S], F32)
    c.qd = kb.dram("qd", [256, S], BF16)
    c.kd = kb.dram("kd", [256, S], BF16)
    c.fq = kb.dram("fq", [256, S], BF16)
    c.fk = kb.dram("fk", [256, S], BF16)
    c.vd = kb.dram("vd", [S, 256], BF16)
    c.fv = kb.dram("fv", [S, 256], BF16)
    c.z = kb.dram("z", [S, 256], F32)
    c.xbc = kb.dram("xbc", [768, S], F32)
    c.rp = kb.dram("rp", [896, S], F32)
    c.dtffT = kb.dram("dtffT", [S, 8], F32)
    c.dtffF = kb.dram("dtffF", [8, S], F32)
    c.yT = kb.dram("yT", [D, S], BF16)
    c.identf = kb.sb("identf", [128, 128], F32)
    c.identb = kb.sb("identb", [128, 128], BF16)
    c.onesb = kb.sb("onesb", [128, 128], BF16)
    kb.dma(c.identf[:], c.ident[:, :])
    kb.cp(c.identb[:], c.identf[:])
    kb.memset(c.onesb[:], 1.0)
    c.psum = [kb.ps("psum%d" % i, [128, 512], F32) for i in range(7)]
    c.psT = kb.ps("psumT", [128, 1024], BF16)
    c.hmidT = kb.dram("hmidT", [S // 512, 128, DFF // 128, 512], BF16)
    c.triu = ext("triu", [128, 128])
    c.mSU = ext("mSU", [128, 128]); c.mSL = ext("mSL", [128, 128]); c.mIU = ext("mIU", [128, 128])
    c.bones = ext("bones", [128, 128]); c.rmask = ext("rmask", [128, 512]); c.hsel = ext("hsel", [128, 2])
    for nm, shp in (("rwkv_mu", [L, 896]), ("rwkv_w0", [L, 256]), ("rwkv_w2", [L, 32, 256]), ("rwkv_a0", [L, 256]),
                    ("rwkv_a2", [L, 32, 256]), ("rwkv_g2", [L, 64, 256]), ("rwkv_k_k", [L, 256]), ("rwkv_k_a", [L, 256]),
                    ("rwkv_r_k", [L, 4, 64]), ("rwkv_ln_w", [L, 256]), ("rwkv_ln_b", [L, 256])):
        setattr(c, nm, ext(nm, shp))
    c.strl = ext("strl", [128, 128])
    for nm, shp in (("ssm_conv_w", [L, 4, 768]), ("ssm_conv_b", [L, 768]), ("ssm_dt_bias", [L, 4]), ("ssm_a_log", [L, 4]),
                    ("ssm_d", [L, 4]), ("ssm_norm", [L, 256])):
        setattr(c, nm, ext(nm, shp))
    for nm, shp in (("norm_mix_post", [L, D]), ("norm_ffn_pre", [L, D]), ("norm_ffn_post", [L, D]),
                    ("w_out", [L, D, D]), ("ffn_w_gate", [L, D, DFF]), ("ffn_w_up", [L, D, DFF]),
                    ("ffn_conv_w", [L, 3, DFF]), ("ffn_conv_b", [L, DFF]), ("ffn_w_down", [L, DFF, D])):
        setattr(c, nm, ext(nm, shp))
    c.maskneg = ext("maskneg", [128, 128], BF16)
    c.alibi_k = ext("alibi_k", [4, 4, S], BF16)
    c.alibi_q = ext("alibi_q", [4, 4, S], BF16)
    for nm, shp in (("diff_lambda_q1", [L, 32]), ("diff_lambda_k1", [L, 32]), ("diff_lambda_q2", [L, 32]),
                    ("diff_lambda_k2", [L, 32]), ("diff_subln", [L, 64]), ("fox_f_bias", [L, 4]), ("fox_norm", [L, 64])):
        setattr(c, nm, ext(nm, shp))

    ph = "P1,A,B,C,D,E,F"
    if "P1" in ph:
        phase_P1(c)
    for l in range(L):
        if "A" in ph:
            phase_A(c, l)
        if "B" in ph:
            phase_B(c, l)
        if "C" in ph:
            phase_C(c, l)
        if "D" in ph:
            phase_D(c, l)
        if "E" in ph:
            phase_out(c, l, "E", c.yT, 8, c.w_out, c.norm_mix_post, False)
        if "F" in ph:
            phase_F1(c, l)
            phase_out(c, l, "G", c.hmidT, DFF // 128, c.ffn_w_down, c.norm_ffn_post, l == L - 1, grouped_src=True)
    kb.finish()
    es.close()
    print("n_inst", kb.n_inst, "n_wait", kb.n_wait)
    return nc


def phase_P1(c):
    kb, S = c.kb, c.S
    with kb.scope() as es:
        xin = [kb.sb("p1_xin%d" % i, [128, 4, D], F32, es) for i in range(2)]
        stg = [kb.sb("p1_stg%d" % i, [128, 8, 512], F32, es) for i in range(2)]
        for g in range(c.NG):
            xi = xin[g % 2]
            st = stg[g % 2]
            kb.dma(xi[:], c.x[g * 512:(g + 1) * 512, :].rearrange("(j p) d -> p j d", p=128))
            for kc in range(8):
                ps = c.psum[kc % 4]
                for j in range(4):
                    kb.tr(ps[:, j * 128:(j + 1) * 128], xi[:, j, kc * 128:(kc + 1) * 128], c.identf[:])
                if kc % 2 == 0:
                    kb.cp(st[:, kc, :], ps[:, :], eng="act")
                else:
                    kb.cp(st[:, kc, :], ps[:, :], eng="dve")
            kb.dma(c.xT[:, g * 512:(g + 1) * 512].rearrange("(kc p) t -> p kc t", p=128), st[:])


def load_w_bf16(c, es_, name, w_dram_rows, K, cols_plan, gscale=None, stage_cols=1024):
    kb = c.kb
    ncols = sum(p[2] for p in cols_plan)
    W = kb.sb(name, [128, K, ncols], BF16, es_)
    with kb.scope() as es:
        stg = [kb.sb(name + "_stg%d" % i, [128, stage_cols], F32, es) for i in range(3)]
        n = 0
        for kc in range(K):
            for (dst, src, w, mult) in cols_plan:
                for o in range(0, w, stage_cols):
                    ww = min(stage_cols, w - o)
                    st = stg[n % 3]
                    kb.dma(st[:, 0:ww], w_dram_rows(kc, src + o, ww))
                    use_act = (n % 2 == 1) and float(mult) == 1.0
                    if use_act:
                        if gscale is not None:
                            kb.act(W[:, kc, dst + o:dst + o + ww], st[:, 0:ww], AF.Copy, scale=gscale[:, kc:kc + 1])
                        else:
                            kb.cp(W[:, kc, dst + o:dst + o + ww], st[:, 0:ww], eng="act")
                    elif gscale is not None:
                        kb.ts(W[:, kc, dst + o:dst + o + ww], st[:, 0:ww], gscale[:, kc:kc + 1], ALU.mult,
                              float(mult), ALU.mult, eng="dve")
                    else:
                        kb.ts(W[:, kc, dst + o:dst + o + ww], st[:, 0:ww], float(mult), ALU.mult, eng="dve")
                    n += 1
    return W


def rstd_bc(c, xg, sq, rs, ps):
    kb = c.kb
    for kc in range(8):
        kb.act(sq[:, kc, :], xg[:, kc, :], AF.Square)
    for kc in range(8):
        kb.mm(ps[:, :], c.onesb[:], sq[:, kc, :], start=(kc == 0), stop=(kc == 7))
    kb.act(rs[:], ps[:, :], AF.Sqrt, bias=c.eps_t[:, 0:1], scale=1.0 / D)
    kb.recip(rs[:], rs[:])


def phase_A(c, l):
    kb, S = c.kb, c.S
    with kb.scope() as es:
        g_t = kb.sb("A_g", [128, 8], F32, es)
        c.eps_t = kb.sb("A_eps", [128, 1], F32, es)
        kb.memset(c.eps_t[:], 1e-6)
        kb.dma(g_t[:], c.norm_mix_pre[l, :].rearrange("(kc p) -> p kc", p=128), allow_slow_non_contiguous=True)
        plan = []
        for n_, oc, w in SEGS:
            mult = 32 ** -0.5 if n_ == "dq" else (64 ** -0.5 if n_ == "fq" else 1.0)
            plan.append((SOFF[n_], oc, w, mult))
        W = load_w_bf16(c, es, "A_W", lambda kc, so, ww: c.w_in[l, kc * 128:(kc + 1) * 128, so:so + ww], 8, plan,
                        gscale=g_t)
        ASTOP = 99
        if ASTOP <= 1:
            return
        xg = [kb.sb("A_xg%d" % i, [128, 8, 512], F32, es) for i in range(2)]
        sq = kb.sb("A_sq", [128, 8, 512], BF16, es)
        rs = kb.sb("A_rs", [128, 512], F32, es)
        xn = [kb.sb("A_xn%d" % i, [128, 8, 512], BF16, es) for i in range(2)]
        stF = [kb.sb("A_stF%d" % i, [128, 512], F32, es) for i in range(3)]
        stB = [kb.sb("A_stB%d" % i, [128, 512], BF16, es) for i in range(3)]
        stT = [kb.sb("A_stT%d" % i, [128, 1024], F32, es) for i in range(2)]
        stTb = [kb.sb("A_stTb%d" % i, [128, 512], BF16, es) for i in range(2)]
        fch = []
        for n_, dst, bf in (("dq", c.qd, True), ("dk", c.kd, True), ("fq", c.fq, True), ("fk", c.fk, True),
                            ("sxbc", c.xbc, False), ("rp", c.rp, False)):
            w = dict((a, cc) for a, b, cc in SEGS)[n_]
            for o in range(0, w, 128):
                fch.append((SOFF[n_] + o, 128, dst, o, bf))
        fch.append((SOFF["sdt"], 8, c.dtffF, 0, False))
        nev = 0
        for g in range(c.NG):
            ts_ = slice(g * 512, (g + 1) * 512)
            x_ = xg[g % 2]
            xn_ = xn[g % 2]
            def prologue(gg):
                xx_ = xg[gg % 2]
                kb.dma(xx_[:], c.xT[:, gg * 512:(gg + 1) * 512].rearrange("(kc p) t -> p kc t", p=128))
                rstd_bc(c, xx_, sq, rs, c.psum[0])
                for kc in range(8):
                    kb.tt(xn[gg % 2][:, kc, :], xx_[:, kc, :], rs[:], ALU.mult, eng=("dve", "pool")[kc % 2])
            if g == 0:
                prologue(0)
            for i, (so, w, dst, ro, bf) in enumerate(fch):
                if i == 8 and g + 1 < c.NG:
                    prologue(g + 1)
                ps = c.psum[1 + i % 4]
                for kc in range(8):
                    kb.mm(ps[0:w, :], W[:, kc, so:so + w], xn_[:, kc, :], start=(kc == 0), stop=(kc == 7))
                st = (stB if bf else stF)[nev % 3]
                nev += 1
                kb.cp(st[0:w, :], ps[0:w, :], eng=("act", "dve")[nev % 2])
                kb.dma(dst[ro:ro + w, ts_], st[0:w, :])
            if ASTOP <= 4:
                continue
            for j in range(4):
                tsj = slice(g * 512 + j * 128, g * 512 + (j + 1) * 128)
                p1, p2 = c.psum[5], c.psum[6]
                for kc in range(8):
                    kb.mm(p1[:, :], xn_[:, kc, j * 128:(j + 1) * 128], W[:, kc, SOFF["dv"]:SOFF["dv"] + 512],
                          start=(kc == 0), stop=(kc == 7))
                for kc in range(8):
                    kb.mm(p2[:, 0:256], xn_[:, kc, j * 128:(j + 1) * 128], W[:, kc, SOFF["fv"]:SOFF["fv"] + 256],
                          start=(kc == 0), stop=(kc == 7))
                for kc in range(8 if ASTOP > 5 else 0):
                    kb.mm(p2[:, 256:264], xn_[:, kc, j * 128:(j + 1) * 128], W[:, kc, SOFF["sdt"]:SOFF["sdt"] + 8],
                          start=(kc == 0), stop=(kc == 7))
                sb_ = stTb[j % 2]
                sf_ = stT[j % 2]
                kb.cp(sb_[:, 0:256], p1[:, 0:256], eng="act")
                kb.cp(sf_[:, 0:256], p1[:, 256:512], eng="dve")
                kb.cp(sb_[:, 256:512], p2[:, 0:256], eng="act")
                if ASTOP > 5:
                    kb.cp(sf_[:, 256:264], p2[:, 256:264], eng="dve")
                if ASTOP != 5:
                    kb.dma(c.vd[tsj, :], sb_[:, 0:256])
                    kb.dma(c.fv[tsj, :], sb_[:, 256:512])
                    kb.dma(c.z[tsj, :], sf_[:, 0:256])
                if ASTOP > 6:
                    kb.dma(c.dtffT[tsj, :], sf_[:, 256:264])


def phase_B(c, l):
    kb, S, NG = c.kb, c.S, c.NG
    NB = S // 128
    lam_init = 0.8 - 0.6 * math.exp(-0.3 * l)
    with kb.scope() as es:
        maskb = kb.sb("B_mask", [128, 128], BF16, es)
        kb.dma(maskb[:], c.maskneg[:, :])
        epsc = kb.sb("B_eps", [128, 1], F32, es)
        kb.memset(epsc[:], 1e-6)
        onec = kb.sb("B_one", [128, 1], F32, es)
        kb.memset(onec[:], 1.0)
        mhalfB = kb.sb("B_mhalf", [128, 1], F32, es)
        kb.memset(mhalfB[:], -0.5)
        lv = kb.sb("B_lv", [128, 4, 32], F32, es)
        for i, t in enumerate((c.diff_lambda_q1, c.diff_lambda_k1, c.diff_lambda_q2, c.diff_lambda_k2)):
            kb.dma(lv[:, i, :], t[l:l + 1, :].partition_broadcast(128))
        lj = kb.sb("B_lj", [128, 32], F32, es)
        ls = kb.sb("B_ls", [128, 4], F32, es)
        kb.stt(lj[:], lv[:, 0, :], 1.0, lv[:, 1, :], ALU.mult, ALU.mult, accum=ls[:, 0:1])
        kb.stt(lj[:], lv[:, 2, :], 1.0, lv[:, 3, :], ALU.mult, ALU.mult, accum=ls[:, 1:2])
        kb.act(ls[:, 0:2], ls[:, 0:2], AF.Exp)
        kb.tt(ls[:, 2:3], ls[:, 1:2], ls[:, 0:1], ALU.subtract)
        kb.ts(ls[:, 3:4], ls[:, 2:3], -lam_init, ALU.add)
        neg_lam = ls[:, 3:4]
        gd = kb.sb("B_gd", [128, 64], F32, es)
        gf = kb.sb("B_gf", [128, 64], F32, es)
        kb.dma(gd[:], c.diff_subln[l:l + 1, :].partition_broadcast(128))
        kb.dma(gf[:], c.fox_norm[l:l + 1, :].partition_broadcast(128))
        kb.ts(gd[:], gd[:], 1.0 - lam_init, ALU.mult)
        chi = kb.sb("B_chi", [4, S], BF16, es)
        cmid = kb.sb("B_cmid", [4, S], BF16, es)
        clo = kb.sb("B_clo", [4, S], BF16, es)
        nhi = kb.sb("B_nhi", [4, S], BF16, es)
        nmid = kb.sb("B_nmid", [4, S], BF16, es)
        nlo = kb.sb("B_nlo", [4, S], BF16, es)
        with kb.scope() as es2:
            cf = kb.sb("B_cf", [4, S], F32, es2)
            c1 = kb.sb("B_c1", [4, S], F32, es2)
            ones4 = kb.sb("B_ones4", [4, S], F32, es2)
            fb = kb.sb("B_fb", [4, 1], F32, es2)
            kb.dma(cf[:], c.dtffF[4:8, :])
            kb.dma(fb[:], c.fox_f_bias[l, :].rearrange("(p o) -> p o", o=1))
            kb.ts(fb[:], fb[:], -1.0, ALU.mult)
            kb.memset(ones4[:], 1.0)
            kb.act(c1[:], cf[:], AF.Exp, bias=fb[:, 0:1], scale=-1.0)
            kb.act(c1[:], c1[:], AF.Ln, bias=onec[0:4, 0:1], scale=1.0)
            kb.ts(c1[:], c1[:], -1.0, ALU.mult)
            kb.scan(cf[:], ones4[:], c1[:], 0.0, ALU.mult, ALU.add)
            kb.cp(chi[:], cf[:])
            kb.tt(c1[:], cf[:], chi[:], ALU.subtract)
            kb.cp(cmid[:], c1[:])
            kb.tt(c1[:], c1[:], cmid[:], ALU.subtract)
            kb.cp(clo[:], c1[:])
            for a, b in ((nhi, chi), (nmid, cmid), (nlo, clo)):
                kb.ts(a[:], b[:], -1.0, ALU.mult)
        Ka = [kb.sb("B_Ka%d" % i, [128, S], BF16, es) for i in range(2)]
        Qa = [kb.sb("B_Qa%d" % i, [128, S], BF16, es) for i in range(2)]
        for i in range(2):
            kb.memset(Ka[i][:], 0.0, eng="pool")
            kb.memset(Qa[i][:], 0.0, eng="dve")
        Va = [kb.sb("B_Va%d" % i, [128, NB, 65], BF16, es) for i in range(2)]
        o1 = kb.sb("B_o1", [128, NB, 64], F32, es)
        osb = [kb.sb("B_osb%d" % i, [65, 512], F32, es) for i in range(2)]
        pT = [kb.sb("B_pT%d" % i, [128, 512], BF16, es) for i in range(5)]
        SB = [c.psum[0], c.psum[1], c.psum[4], c.psum[5]]
        LA = 3
        sm = [kb.sb("B_sm%d" % i, [128, 8], F32, es) for i in range(4)]
        tmp = [kb.sb("B_tmp%d" % i, [128, 64], F32, es) for i in range(4)]
        junk = kb.sb("B_junk", [128, 64], F32, es)
        ybf = [kb.sb("B_ybf%d" % i, [128, 64], BF16, es) for i in range(4)]
        yst = [kb.sb("B_yst%d" % i, [64, 512], BF16, es) for i in range(2)]
        maps = [("d", m) for m in range(8)] + [("f", h) for h in range(4)]
        npt = 0
        nfin = 0
        ngrp = 0
        pending = []
        for mi, (kind, m) in enumerate(maps):
            K_, Q_, V_ = Ka[mi % 2], Qa[mi % 2], Va[mi % 2]
            if kind == "d":
                h = m // 2
                R = 36
                kb.dma(K_[0:32, :], c.kd[m * 32:(m + 1) * 32, :])
                kb.dma(Q_[0:32, :], c.qd[m * 32:(m + 1) * 32, :])
                kb.dma(K_[32:36, :], c.alibi_k[h, :, :])
                kb.dma(Q_[32:36, :], c.alibi_q[h, :, :])
                vsrc, yrow, gv = c.vd, h * 64, gd
            else:
                h = m
                R = 70
                kb.dma(K_[0:64, :], c.fk[h * 64:(h + 1) * 64, :])
                kb.dma(Q_[0:64, :], c.fq[h * 64:(h + 1) * 64, :])
                kb.memset(K_[64:70, :], 1.0)
                kb.memset(Q_[64:70, :], 1.0)
                for i, (a, b) in enumerate(((nhi, chi), (nmid, cmid), (nlo, clo))):
                    kb.dma(K_[64 + i:65 + i, :], a[h:h + 1, :])
                    kb.dma(Q_[67 + i:68 + i, :], b[h:h + 1, :])
                vsrc, yrow, gv = c.fv, 512 + h * 64, gf
            kb.dma(V_[:, :, 0:64], vsrc[:, h * 64:(h + 1) * 64].rearrange("(j p) d -> p j d", p=128))
            kb.memset(V_[:, :, 64:65], 1.0)
            for g in range(NG):
                tiles = list(range(4 * g + 4))
                Ob = c.psum[2 + (ngrp % 2)]
                ngrp += 1

                def emit_qk(j, g=g, K_=K_, Q_=Q_, R=R):
                    jj = max(0, j - 4 * g)
                    c0 = 128 * jj
                    diag = j >= 4 * g
                    ps_s = SB[j % 4]
                    kb.mm(ps_s[:, c0:512], K_[:, j * 128:(j + 1) * 128], Q_[:, g * 512 + c0:(g + 1) * 512],
                          start=True, stop=not diag)
                    if diag:
                        kb.mm(ps_s[:, c0:c0 + 128], c.identb[:], maskb[:], start=False, stop=True)

                for j0 in range(min(LA, len(tiles))):
                    emit_qk(j0)
                for j in tiles:
                    if j + LA < len(tiles):
                        emit_qk(j + LA)
                    jj = max(0, j - 4 * g)
                    c0 = 128 * jj
                    ps_s = SB[j % 4]
                    p_ = pT[npt % 5]
                    npt += 1
                    kb.act(p_[:, c0:512], ps_s[:, c0:512], AF.Exp)
                    kb.mm(Ob[0:65, c0:512], V_[:, j, :], p_[:, c0:512], start=(j == 0), stop=(j == 4 * g + 3),
                          skip_group_check=True)
                    if j == 1 and pending:
                        pending.pop(0)()

                def fin(g=g, Ob=Ob, kind=kind, m=m, yrow=yrow, gv=gv):
                    nonlocal nfin
                    osb_ = osb[g % 2]
                    kb.cp(osb_[:], Ob[0:65, :], eng="dve")
                    for tb in range(4):
                        kb.tr(c.psum[6][:, tb * 128:tb * 128 + 65], osb_[:, tb * 128:(tb + 1) * 128], c.identf[0:65, 0:65])
                    for tb in range(4):
                        O = c.psum[6][:, tb * 128:tb * 128 + 65]
                        ib = g * 4 + tb
                        s_ = sm[nfin % 4]
                        t_ = tmp[nfin % 4]
                        y_ = ybf[nfin % 4]
                        nfin += 1
                        kb.recip(s_[:, 0:1], O[:, 64:65])
                        if kind == "d" and m % 2 == 0:
                            kb.ts(o1[:, ib, :], O[:, 0:64], s_[:, 0:1], ALU.mult)
                            continue
                        kb.ts(t_[:], O[:, 0:64], s_[:, 0:1], ALU.mult)
                        if kind == "d":
                            kb.stt(t_[:], t_[:], neg_lam, o1[:, ib, :], ALU.mult, ALU.add)
                        kb.stt(junk[:], t_[:], 1.0, t_[:], ALU.mult, ALU.mult, accum=s_[:, 1:2])
                        kb.ts(s_[:, 2:3], s_[:, 1:2], 1.0 / 64, ALU.mult, 1e-6, ALU.add)
                        kb.tt(s_[:, 3:4], s_[:, 2:3], mhalfB[:, 0:1], ALU.pow, eng="pool")
                        kb.stt(y_[:], t_[:], s_[:, 3:4], gv[:], ALU.mult, ALU.mult)
                        kb.tr(c.psT[0:64, tb * 128:(tb + 1) * 128], y_[:], c.identb[:])
                    if not (kind == "d" and m % 2 == 0):
                        ys = yst[g % 2]
                        kb.cp(ys[:], c.psT[0:64, 0:512])
                        kb.dma(c.yT[yrow:yrow + 64, g * 512:(g + 1) * 512], ys[:])

                pending.append(fin)
                if len(tiles) < 2 or NG == 1:
                    while pending:
                        pending.pop(0)()
        while pending:
            pending.pop(0)()


def phase_out(c, l, nm, src, K, wsrc, gpost, last, grouped_src=False):
    kb, S, NG = c.kb, c.S, c.NG
    with kb.scope() as es:
        c.eps_t = kb.sb(nm + "_eps", [128, 1], F32, es)
        kb.memset(c.eps_t[:], 1e-6)
        gp = kb.sb(nm + "_gp", [128, 8], F32, es)
        kb.dma(gp[:], gpost[l, :].rearrange("(kc p) -> p kc", p=128), allow_slow_non_contiguous=True)
        W = load_w_bf16(c, es, nm + "_W", lambda kc, so, ww: wsrc[l, kc * 128:(kc + 1) * 128, so:so + ww], K,
                        [(0, 0, 1024, 1.0)])
        yg = [kb.sb(nm + "_yg%d" % i, [128, K, 512], BF16, es) for i in range(2)]
        xg = [kb.sb(nm + "_xg%d" % i, [128, 8, 512], F32, es) for i in range(2)]
        yos = [kb.sb(nm + "_yo%d" % i, [128, 8, 512], F32, es) for i in range(2)]
        sq = kb.sb(nm + "_sq", [128, 8, 512], BF16, es)
        rs = kb.sb(nm + "_rs", [128, 512], F32, es)
        tmpo = [kb.sb(nm + "_tmp%d" % i, [128, 512], F32, es) for i in range(2)]
        if last:
            ost = [kb.sb(nm + "_ost%d" % i, [128, 1024], F32, es) for i in range(2)]
        for g in range(NG):
            ts_ = slice(g * 512, (g + 1) * 512)
            y_ = yg[g % 2]
            x_ = xg[g % 2]
            yo = yos[g % 2]
            if grouped_src:
                kb.dma(y_[:], src[g, :, :, :])
            else:
                kb.dma(y_[:], src[:, ts_].rearrange("(kc p) t -> p kc t", p=128))
            kb.dma(x_[:], c.xT[:, ts_].rearrange("(kc p) t -> p kc t", p=128))
            for oc in range(8):
                ps = c.psum[1 + oc % 4]
                for kc in range(K):
                    kb.mm(ps[:, :], W[:, kc, oc * 128:(oc + 1) * 128], y_[:, kc, :], start=(kc == 0), stop=(kc == K - 1))
                kb.cp(yo[:, oc, :], ps[:, :], eng=("act", "dve")[oc % 2])
            rstd_bc(c, yo, sq, rs, c.psum[0])
            for oc in range(8):
                t_ = tmpo[oc % 2]
                kb.stt(t_[:], yo[:, oc, :], gp[:, oc:oc + 1], rs[:], ALU.mult, ALU.mult, eng="dve")
                kb.tt(x_[:, oc, :], x_[:, oc, :], t_[:], ALU.add, eng="pool")
            if not last:
                kb.dma(c.xT[:, ts_].rearrange("(kc p) t -> p kc t", p=128), x_[:])
            else:
                for tb in range(4):
                    o_ = ost[tb % 2]
                    for half in range(2):
                        ps = c.psum[5 + half]
                        for q in range(4):
                            oc = half * 4 + q
                            kb.tr(ps[:, q * 128:(q + 1) * 128], x_[:, oc, tb * 128:(tb + 1) * 128], c.identf[:])
                        kb.cp(o_[:, half * 512:(half + 1) * 512], ps[:, :], eng=("act", "dve")[half])
                    kb.dma(c.out[g * 512 + tb * 128:g * 512 + (tb + 1) * 128, :], o_[:])


def phase_F1(c, l):
    kb, S, NG = c.kb, c.S, c.NG
    NF = DFF // 128
    with kb.scope() as es:
        c.eps_t = kb.sb("F_eps", [128, 1], F32, es)
        kb.memset(c.eps_t[:], 1e-6)
        g_t = kb.sb("F_g", [128, 8], F32, es)
        kb.dma(g_t[:], c.norm_ffn_pre[l, :].rearrange("(kc p) -> p kc", p=128), allow_slow_non_contiguous=True)
        cw = kb.sb("F_cw", [128, 3, NF], F32, es)
        cb = kb.sb("F_cb", [128, NF], F32, es)
        for k in range(3):
            kb.dma(cw[:, k, :], c.ffn_conv_w[l, k, :].rearrange("(fc p) -> p fc", p=128), allow_slow_non_contiguous=True)
        kb.dma(cb[:], c.ffn_conv_b[l, :].rearrange("(fc p) -> p fc", p=128), allow_slow_non_contiguous=True)
        Wg = load_w_bf16(c, es, "F_Wg", lambda kc, so, ww: c.ffn_w_gate[l, kc * 128:(kc + 1) * 128, so:so + ww], 8,
                         [(0, 0, DFF, 1.0)], gscale=g_t)
        Wu = load_w_bf16(c, es, "F_Wu", lambda kc, so, ww: c.ffn_w_up[l, kc * 128:(kc + 1) * 128, so:so + ww], 8,
                         [(0, 0, DFF, 1.0)], gscale=g_t)
        xg = [kb.sb("F_xg%d" % i, [128, 8, 512], F32, es) for i in range(2)]
        sq = kb.sb("F_sq", [128, 8, 512], BF16, es)
        rs = kb.sb("F_rs", [128, 512], F32, es)
        xn = [kb.sb("F_xn%d" % i, [128, 8, 512], BF16, es) for i in range(2)]
        halo = kb.sb("F_halo", [128, NF, 2], F32, es)
        kb.memset(halo[:], 0.0)
        gb = [kb.sb("F_gb%d" % i, [128, 514], F32, es) for i in range(3)]
        acc = [kb.sb("F_acc%d" % i, [128, 512], F32, es) for i in range(3)]
        t1 = [kb.sb("F_t1%d" % i, [128, 512], F32, es) for i in range(3)]
        t2 = [kb.sb("F_t2%d" % i, [128, 512], F32, es) for i in range(3)]
        hst = [kb.sb("F_hst%d" % i, [128, 512], BF16, es) for i in range(3)]
        n = 0
        for g in range(NG):
            ts_ = slice(g * 512, (g + 1) * 512)
            x_ = xg[g % 2]
            xn_ = xn[g % 2]
            def prologue(gg):
                xx_ = xg[gg % 2]
                kb.dma(xx_[:], c.xT[:, gg * 512:(gg + 1) * 512].rearrange("(kc p) t -> p kc t", p=128))
                rstd_bc(c, xx_, sq, rs, c.psum[0])
                for kc in range(8):
                    kb.tt(xn[gg % 2][:, kc, :], xx_[:, kc, :], rs[:], ALU.mult, eng=("dve", "pool")[kc % 2])
            if g == 0:
                prologue(0)

            def fc_gen(fc, g=g, ts_=ts_, xn_=xn_):
                nonlocal n
                pg = c.psum[1 + (fc % 3)]
                pu = c.psum[4 + (fc % 3)]
                fs = slice(fc * 128, (fc + 1) * 128)
                gb_, a_, t1_, t2_ = gb[n % 3], acc[n % 3], t1[n % 3], t2[n % 3]
                h_ = hst[n % 3]
                n += 1
                for kc in range(8):
                    kb.mm(pg[:, :], Wg[:, kc, fs], xn_[:, kc, :], start=(kc == 0), stop=(kc == 7))
                kb.cp(gb_[:, 0:2], halo[:, fc, :], eng="pool")
                yield
                for kc in range(8):
                    kb.mm(pu[:, :], Wu[:, kc, fs], xn_[:, kc, :], start=(kc == 0), stop=(kc == 7))
                kb.cp(gb_[:, 2:514], pg[:, :], eng="act")
                yield
                kb.cp(halo[:, fc, :], gb_[:, 512:514], eng="pool")
                kb.ts(a_[:], gb_[:, 2:514], cw[:, 2, fc:fc + 1], ALU.mult, cb[:, fc:fc + 1], ALU.add, eng="pool")
                yield
                kb.stt(a_[:], gb_[:, 1:513], cw[:, 1, fc:fc + 1], a_[:], ALU.mult, ALU.add, eng="dve")
                kb.stt(a_[:], gb_[:, 0:512], cw[:, 0, fc:fc + 1], a_[:], ALU.mult, ALU.add, eng="dve")
                yield
                kb.act(t2_[:], a_[:], AF.Gelu_apprx_tanh)
                yield
                kb.tt(h_[:], t2_[:], pu[:, :], ALU.mult, eng="dve")
                kb.dma(c.hmidT[g, :, fc, :], h_[:])

            def fcs(g=g):
                for fc in range(NF):
                    if fc == 8 and g + 1 < NG:
                        prologue(g + 1)
                    yield fc_gen(fc)
            run_pipelined(fcs(), 3)


def phase_C(c, l):
    kb, S, NG = c.kb, c.S, c.NG
    with kb.scope() as es:
        triu = kb.sb("C_triu", [128, 128], F32, es)
        strl = kb.sb("C_strl", [128, 128], F32, es)
        onesf = kb.sb("C_onesf", [128, 128], F32, es)
        kb.dma(triu[:], c.triu[:, :])
        kb.dma(strl[:], c.strl[:, :])
        kb.memset(onesf[:], 1.0)
        epsc = kb.sb("C_eps", [128, 1], F32, es)
        kb.memset(epsc[:], 1e-6)
        onec = kb.sb("C_one", [128, 1], F32, es)
        kb.memset(onec[:], 1.0)
        cw = kb.sb("C_cw", [128, 4, 6], F32, es)
        cbias = kb.sb("C_cb", [128, 6], F32, es)
        for k in range(4):
            kb.dma(cw[:, k, :], c.ssm_conv_w[l, k, :].rearrange("(cc p) -> p cc", p=128), allow_slow_non_contiguous=True)
        kb.dma(cbias[:], c.ssm_conv_b[l, :].rearrange("(cc p) -> p cc", p=128), allow_slow_non_contiguous=True)
        prm = kb.sb("C_prm", [128, 3, 4], F32, es)
        kb.dma(prm[:, 0, :], c.ssm_dt_bias[l:l + 1, :].partition_broadcast(128))
        kb.dma(prm[:, 1, :], c.ssm_a_log[l:l + 1, :].partition_broadcast(128))
        kb.dma(prm[:, 2, :], c.ssm_d[l:l + 1, :].partition_broadcast(128))
        kb.act(prm[:, 1, :], prm[:, 1, :], AF.Exp)
        kb.ts(prm[:, 1, :], prm[:, 1, :], -1.0, ALU.mult)
        gn = kb.sb("C_gn", [128, 256], F32, es)
        kb.dma(gn[:], c.ssm_norm[l:l + 1, :].partition_broadcast(128))
        St = kb.sb("C_St", [128, 4, 64], F32, es)
        kb.memset(St[:], 0.0)
        Stb = kb.sb("C_Stb", [128, 4, 64], BF16, es)
        cbuf = [kb.sb("C_cbuf%d" % i, [128, 515], F32, es) for i in range(2)]
        acc = [kb.sb("C_acc%d" % i, [128, 512], F32, es) for i in range(2)]
        ux = [kb.sb("C_ux%d" % i, [128, 2, 512], F32, es) for i in range(2)]
        uB = [kb.sb("C_uB%d" % i, [128, 2, 512], BF16, es) for i in range(2)]
        uC = [kb.sb("C_uC%d" % i, [128, 2, 512], BF16, es) for i in range(2)]
        yst = [kb.sb("C_yst%d" % i, [128, 2, 512], BF16, es) for i in range(2)]
        NR = 2

        def ring(nm, shape, dt):
            return [kb.sb("C_%s%d" % (nm, i), shape, dt, es) for i in range(NR)]

        dtt = ring("dtt", [128, 8], F32)
        sm = ring("sm", [128, 32], F32)
        xtok = ring("xtok", [128, 256], F32)
        Btok = ring("Btok", [128, 256], BF16)
        xd = ring("xd", [128, 256], BF16)
        xde = ring("xde", [128, 256], BF16)
        zt = ring("zt", [128, 256], F32)
        MS = [kb.sb("C_MS%d" % i, [128, 128], F32, es) for i in range(8)]
        E = [kb.sb("C_E%d" % i, [128, 128], F32, es) for i in range(8)]
        MT = [kb.sb("C_MT%d" % i, [128, 128], BF16, es) for i in range(8)]
        y1 = ring("y1", [128, 256], F32)
        yy = ring("yy", [128, 256], F32)
        junk = ring("junk", [128, 128], F32)
        ybf = ring("ybf", [128, 256], BF16)
        nch = 0
        for g in range(NG):
            ux_, uB_, uC_ = ux[g % 2], uB[g % 2], uC[g % 2]
            for cc in range(6):
                cb_ = cbuf[cc % 2]
                a_ = acc[cc % 2]
                rows = slice(cc * 128, (cc + 1) * 128)
                if g == 0:
                    kb.memset(cb_[:, 0:3], 0.0)
                    kb.dma(cb_[:, 3:515], c.xbc[rows, 0:512])
                else:
                    kb.dma(cb_[:, 0:515], c.xbc[rows, g * 512 - 3:(g + 1) * 512])
                kb.ts(a_[:], cb_[:, 3:515], cw[:, 3, cc:cc + 1], ALU.mult, cbias[:, cc:cc + 1], ALU.add, eng="pool")
                for k in (2, 1, 0):
                    kb.stt(a_[:], cb_[:, k:k + 512], cw[:, k, cc:cc + 1], a_[:], ALU.mult, ALU.add, eng="dve")
                if cc < 2:
                    kb.act(ux_[:, cc, :], a_[:], AF.Silu)
                elif cc < 4:
                    kb.act(uB_[:, cc - 2, :], a_[:], AF.Silu)
                else:
                    kb.act(uC_[:, cc - 4, :], a_[:], AF.Silu)
            ys_ = yst[g % 2]

            def prep_gen(j, g=g, ux_=ux_, uB_=uB_):
                i = (g * 4 + j) % NR
                tsl = slice(g * 512 + j * 128, g * 512 + (j + 1) * 128)
                cs_ = slice(j * 128, (j + 1) * 128)
                dtt_, sm_, xtok_, Btok_, xd_, xde_, zt_ = dtt[i], sm[i], xtok[i], Btok[i], xd[i], xde[i], zt[i]
                kb.dma(dtt_[:], c.dtffT[tsl, :])
                kb.dma(zt_[:], c.z[tsl, :])
                xr, ax, ee, dtv, dA = sm_[:, 0:4], sm_[:, 4:8], sm_[:, 8:12], sm_[:, 12:16], sm_[:, 16:20]
                ecs, dte, etot = sm_[:, 20:24], sm_[:, 24:28], sm_[:, 28:32]
                px = c.psum[1]
                for q in range(2):
                    kb.tr(px[:, q * 128:(q + 1) * 128], ux_[:, q, cs_], c.identf[:])
                for q in range(2):
                    kb.tr(c.psT[:, q * 128:(q + 1) * 128], uB_[:, q, cs_], c.identb[:])
                kb.tt(xr, dtt_[:, 0:4], prm[:, 0, :], ALU.add)
                kb.ts(ax, xr, -1.0, ALU.mult)
                kb.tt(ax, ax, xr, ALU.max)
                yield
                kb.cp(xtok_[:], px[:, 0:256], eng="act")
                kb.cp(Btok_[:], c.psT[:, 0:256], eng="act")
                kb.act(ee, ax, AF.Exp, scale=-1.0)
                kb.act(ee, ee, AF.Ln, bias=onec[:, 0:1], scale=1.0)
                kb.act(zt_[:], zt_[:], AF.Silu)
                yield
                kb.stt(dtv, xr, 0.0, ee, ALU.max, ALU.add)
                kb.tt(dA, dtv, prm[:, 1, :], ALU.mult)
                yield
                pcs = c.psum[0]
                kb.mm(pcs[:, 0:4], triu[:], dA, start=True, stop=True)
                kb.mm(pcs[:, 4:8], onesf[:], dA, start=True, stop=True)
                yield
                kb.act(ecs, pcs[:, 0:4], AF.Exp)
                kb.act(etot, pcs[:, 4:8], AF.Exp)
                kb.cp(dte, pcs[:, 0:4])
                kb.tt(dte, pcs[:, 4:8], dte, ALU.subtract)
                yield
                kb.act(dte, dte, AF.Exp)
                yield
                for h in range(4):
                    hs = slice(h * 64, (h + 1) * 64)
                    e1, e2 = ("pool", "dve") if h % 2 == 0 else ("dve", "pool")
                    kb.ts(xd_[:, hs], xtok_[:, hs], dtv[:, h:h + 1], ALU.mult, eng=e1)
                    kb.ts(xde_[:, hs], xtok_[:, hs], dtv[:, h:h + 1], ALU.mult, dte[:, h:h + 1], ALU.mult, eng=e2)
                    kb.ts(MS[i * 4 + h][:], strl[:], dA[:, h:h + 1], ALU.mult, eng=e1)
                    yield

            def main_gen(j, g=g, uB_=uB_, uC_=uC_, ys_=ys_):
                i = (g * 4 + j) % NR
                cs_ = slice(j * 128, (j + 1) * 128)
                dtt_, sm_, xtok_, Btok_, xd_, xde_, zt_ = dtt[i], sm[i], xtok[i], Btok[i], xd[i], xde[i], zt[i]
                dA, ecs, etot = sm_[:, 16:20], sm_[:, 20:24], sm_[:, 28:32]
                y1_, yy_ = y1[i], yy[i]
                pL, pG, pY, pS = c.psum[2], c.psum[3], c.psum[4], c.psum[5]
                kb.cp(Stb[:], St[:], eng="pool")
                for grp in range(2):
                    kb.mm(pG[:, grp * 128:(grp + 1) * 128], uB_[:, grp, cs_], uC_[:, grp, cs_], start=True, stop=True)

                def chead(h):
                    grp = h // 2
                    hs = slice(h * 64, (h + 1) * 64)
                    MS_, E_, MT_ = MS[i * 4 + h], E[i * 4 + h], MT[i * 4 + h]
                    kb.mm(pL[:, h * 128:(h + 1) * 128], MS_[:], triu[:], start=True, stop=True)
                    yield
                    kb.act(E_[:], pL[:, h * 128:(h + 1) * 128], AF.Exp)
                    yield
                    kb.tt(E_[:], E_[:], triu[:], ALU.mult, eng=("pool", "dve")[h % 2])
                    yield
                    kb.tt(MT_[:], pG[:, grp * 128:(grp + 1) * 128], E_[:], ALU.mult)
                    yield
                    kb.mm(pY[:, h * 128:h * 128 + 64], MT_[:], xd_[:, hs], start=True, stop=True)
                    kb.mm(pY[:, h * 128 + 64:h * 128 + 128], uC_[:, grp, cs_], Stb[:, h, :], start=True, stop=True)
                    kb.mm(pS[:, hs], Btok_[:, grp * 128:(grp + 1) * 128], xde_[:, hs], start=True, stop=True)
                    yield
                    kb.stt(y1_[:, hs], xtok_[:, hs], prm[:, 2, h:h + 1], pY[:, h * 128:h * 128 + 64], ALU.mult, ALU.add)
                    kb.stt(yy_[:, hs], pY[:, h * 128 + 64:h * 128 + 128], ecs[:, h:h + 1], y1_[:, hs], ALU.mult, ALU.add)
                    kb.stt(St[:, h, :], St[:, h, :], etot[:, h:h + 1], pS[:, hs], ALU.mult, ALU.add)

                pend = [chead(h) for h in range(4)]
                gens = []
                while gens or pend:
                    if pend:
                        gens.append(pend.pop(0))
                    for g_ in list(gens):
                        try:
                            next(g_)
                        except StopIteration:
                            gens.remove(g_)
                    yield
                kb.tt(yy_[:], yy_[:], zt_[:], ALU.mult, eng="pool")
                yield
                s2 = dtt_
                for q in range(2):
                    qs = slice(q * 128, (q + 1) * 128)
                    kb.stt(junk[i][:], yy_[:, qs], 1.0, yy_[:, qs], ALU.mult, ALU.mult, accum=s2[:, 4 + q:5 + q])
                yield
                kb.act(s2[:, 4:6], s2[:, 4:6], AF.Ln, bias=epsc[:, 0:1], scale=1.0 / 128)
                kb.act(s2[:, 6:8], s2[:, 4:6], AF.Exp, scale=-0.5)
                yield
                yb_ = ybf[i]
                for q in range(2):
                    qs = slice(q * 128, (q + 1) * 128)
                    kb.stt(yb_[:, qs], yy_[:, qs], s2[:, 6 + q:7 + q], gn[:, qs], ALU.mult, ALU.mult)
                yield
                for q in range(2):
                    kb.tr(c.psT[:, 512 + q * 128:512 + (q + 1) * 128], yb_[:, q * 128:(q + 1) * 128], c.identb[:])
                yield
                for q in range(2):
                    kb.cp(ys_[:, q, cs_], c.psT[:, 512 + q * 128:512 + (q + 1) * 128], eng="act")

            def drain(gs):
                gs = list(gs)
                while gs:
                    for g_ in list(gs):
                        try:
                            next(g_)
                        except StopIteration:
                            gs.remove(g_)

            drain([prep_gen(0)])
            for j in range(4):
                gs = [main_gen(j)]
                if j + 1 < 4:
                    gs.append(prep_gen(j + 1))
                drain(gs)
            kb.dma(c.yT[256:512, g * 512:(g + 1) * 512].rearrange("(q p) t -> p q t", p=128), ys_[:])


def phase_D(c, l):
    kb, S, NG = c.kb, c.S, c.NG
    with kb.scope() as es:
        def cst(nm, src):
            t = kb.sb("D_" + nm, [128, 128], F32, es)
            kb.dma(t[:], src[:, :])
            return t
        SU, SL, IU, bones = cst("SU", c.mSU), cst("SL", c.mSL), cst("IU", c.mIU), cst("bones", c.bones)
        rmask = kb.sb("D_rmask", [128, 512], F32, es)
        kb.dma(rmask[:], c.rmask[:, :])
        hsel = kb.sb("D_hsel", [128, 2], F32, es)
        kb.dma(hsel[:], c.hsel[:, :])
        onec = kb.sb("D_one", [128, 1], F32, es)
        kb.memset(onec[:], 1.0)
        mhalf = kb.sb("D_mhalf", [128, 1], F32, es)
        kb.memset(mhalf[:], -0.5)
        eps12 = kb.sb("D_eps12", [128, 1], F32, es)
        kb.memset(eps12[:], 1e-12)
        epsln = kb.sb("D_epsln", [128, 1], F32, es)
        kb.memset(epsln[:], 64e-5)
        mu = kb.sb("D_mu", [128, 7], F32, es)
        omm = kb.sb("D_omm", [128, 7], F32, es)
        kb.dma(mu[:], c.rwkv_mu[l, :].rearrange("(cc p) -> p cc", p=128), allow_slow_non_contiguous=True)
        kb.ts(omm[:], mu[:], -1.0, ALU.mult, 1.0, ALU.add)
        pp = kb.sb("D_pp", [128, 5, 2], F32, es)
        for i, t in enumerate((c.rwkv_w0, c.rwkv_a0, c.rwkv_k_k, c.rwkv_k_a)):
            kb.dma(pp[:, i, :], t[l, :].rearrange("(q p) -> p q", p=128), allow_slow_non_contiguous=True)
        kb.dma(pp[:, 4, :], c.rwkv_r_k[l, :, :].rearrange("(q hh) d -> (hh d) q", q=2), allow_slow_non_contiguous=True)
        kb.ts(pp[:, 0, :], pp[:, 0, :], -1.0, ALU.mult)
        w2t = kb.sb("D_w2", [32, 256], F32, es)
        kb.dma(w2t[:], c.rwkv_w2[l, :, :])
        a2t = kb.sb("D_a2", [64, 256], F32, es)
        kb.dma(a2t[32:64, :], c.rwkv_a2[l, :, :])
        g2t = kb.sb("D_g2", [128, 256], F32, es)
        kb.dma(g2t[64:128, :], c.rwkv_g2[l, :, :])
        lnw = kb.sb("D_lnw", [128, 256], F32, es)
        lnb = kb.sb("D_lnb", [128, 256], F32, es)
        kb.dma(lnw[:], c.rwkv_ln_w[l:l + 1, :].partition_broadcast(128))
        kb.dma(lnb[:], c.rwkv_ln_b[l:l + 1, :].partition_broadcast(128))
        Hs = [[kb.sb("D_H%d_%d" % (h, i), [128, 64], F32, es) for i in range(3)] for h in range(4)]
        for h in range(4):
            kb.memset(Hs[h][0][:], 0.0)
        hidx = [0, 0, 0, 0]

        def garr(nm, n=2, cols=512, dt=F32):
            return [kb.sb("D_%s%d" % (nm, i), [128, cols], dt, es) for i in range(n)]
        buf = garr("buf", 2, 513)
        m1 = garr("m1", 2)
        rr, kk_, vv = garr("rr"), garr("kk"), garr("vv")
        xx = kb.sb("D_xx", [128, 512], F32, es)
        txw = kb.sb("D_txw", [32, 512], F32, es)
        sxg = kb.sb("D_sxg", [128, 512], F32, es)
        aa, nlw, ncl = garr("aa"), garr("nlw"), garr("ncl")
        pc = garr("pc")
        kat, kt, bt, rt = garr("kat", dt=BF16), garr("kt", dt=BF16), garr("bt", dt=BF16), garr("rt", dt=BF16)
        t1, t2, prod = garr("t1"), garr("t2"), garr("prod")
        gtok = kb.sb("D_gtok", [128, 4, 256], F32, es)
        vtok = garr("vtok", 2, 256)
        vtokb = garr("vtokb", 2, 256, BF16)
        otok = garr("otok", 2, 256)
        rkt = garr("rkt", 2, 4)
        ymid = garr("ymid", 2, 256)
        ybf = garr("ybf", 2, 256, BF16)
        st6 = garr("st6", 2, 32)
        yst = [kb.sb("D_yst%d" % i, [128, 2, 512], BF16, es) for i in range(2)]
        NS = 4
        def slot(nm, cols=128, dt=BF16):
            return [kb.sb("D_s_%s%d" % (nm, i), [128, cols], dt, es) for i in range(NS)]
        BAa, BAb, IA, Xa, Xb = slot("BAa", 256), slot("BAb", 256), slot("IA"), slot("Xa"), slot("Xb")
        A3 = slot("A3", 384)
        mSUSL = kb.sb("D_mSUSL", [128, 256], F32, es)
        mSUIUIU = kb.sb("D_mSUIUIU", [128, 384], F32, es)
        kb.cp(mSUSL[:, 0:128], SU[:], eng="pool")
        kb.cp(mSUSL[:, 128:256], SL[:], eng="pool")
        kb.cp(mSUIUIU[:, 0:128], SU[:], eng="pool")
        kb.cp(mSUIUIU[:, 128:256], IU[:], eng="pool")
        kb.cp(mSUIUIU[:, 256:384], IU[:], eng="pool")
        RH, WU, nU0 = slot("RH"), slot("WU"), slot("nU0", 64)
        Wpad, btTp, ktTp, R1, R2 = slot("Wpad"), slot("btTp"), slot("ktTp"), slot("R1"), slot("R2")
        MpT, Npp = slot("MpT", dt=F32), slot("Npp", 64, dt=F32)
        WpadF, btTpF = slot("WpadF", dt=F32), slot("btTpF", dt=F32)
        Hb = slot("Hb", 64)
        for s_ in range(NS):
            for t_ in (Wpad, btTp, ktTp, R1, R2, WpadF, btTpF):
                kb.memset(t_[s_][:], 0.0)
        P0, P1, P2, P3, P4, P5, P6 = c.psum
        for g in range(NG):
            def load_mix(cc, dst):
                b_ = buf[cc % 2]
                rows = slice(cc * 128, (cc + 1) * 128)
                if g == 0:
                    kb.memset(b_[:, 0:1], 0.0)
                    kb.dma(b_[:, 1:513], c.rp[rows, 0:512])
                else:
                    kb.dma(b_[:, 0:513], c.rp[rows, g * 512 - 1:(g + 1) * 512])
                m_ = m1[cc % 2]
                kb.ts(m_[:], b_[:, 1:513], omm[:, cc:cc + 1], ALU.mult, eng="pool")
                kb.stt(dst, b_[:, 0:512], mu[:, cc:cc + 1], m_[:], ALU.mult, ALU.add)
            for q in range(2):
                load_mix(q, rr[q][:])
                load_mix(2 + q, kk_[q][:])
                load_mix(4 + q, vv[q][:])
            load_mix(6, xx[:])
            kb.act(txw[:], xx[0:32, :], AF.Tanh)
            kb.act(sxg[64:128, :], xx[64:128, :], AF.Sigmoid)
            for q in range(2):
                qs = slice(q * 128, (q + 1) * 128)
                kb.mm(P0[:, :], a2t[32:64, qs], xx[32:64, :], start=True, stop=True)
                kb.act(aa[q][:], P0[:, :], AF.Sigmoid, bias=pp[:, 1, q:q + 1], scale=1.0)
                kb.mm(P1[:, :], w2t[:, qs], txw[:], start=True, stop=True)
                kb.act(t1[q][:], P1[:, :], AF.Exp, bias=pp[:, 0, q:q + 1], scale=-1.0)
                kb.act(t1[q][:], t1[q][:], AF.Ln, bias=onec[:, 0:1], scale=1.0)
                kb.act(nlw[q][:], t1[q][:], AF.Exp, bias=mhalf[:, 0:1], scale=-1.0)
                kb.scan(ncl[q][:], rmask[:], nlw[q][:], 0.0, ALU.mult, ALU.add)
                kb.ts(t1[q][:], kk_[q][:], pp[:, 2, q:q + 1], ALU.mult, eng="pool")
                kb.tt(t2[q][:], t1[q][:], t1[q][:], ALU.mult, eng="pool")
                kb.mm(P2[:, :], bones[:], t2[q][:], start=True, stop=True)
                kb.act(t2[q][:], P2[:, :], AF.Ln, bias=eps12[:, 0:1], scale=1.0)
                kb.act(t2[q][:], t2[q][:], AF.Exp, scale=-0.5)
                kb.tt(t1[q][:], t1[q][:], t2[q][:], ALU.mult)
                kb.tt(t2[q][:], nlw[q][:], ncl[q][:], ALU.subtract, eng="pool")
                kb.act(t2[q][:], t2[q][:], AF.Exp)
                kb.tt(kat[q][:], t1[q][:], t2[q][:], ALU.mult)
                kb.act(t2[q][:], ncl[q][:], AF.Exp)
                kb.tt(t1[q][:], t1[q][:], aa[q][:], ALU.mult, eng="pool")
                kb.tt(bt[q][:], t1[q][:], t2[q][:], ALU.mult)
                kb.ts(t1[q][:], aa[q][:], -1.0, ALU.add, pp[:, 3, q:q + 1], ALU.mult, eng="pool")
                kb.stt(t1[q][:], t1[q][:], 1.0, kk_[q][:], ALU.add, ALU.mult)
                kb.tt(kt[q][:], t1[q][:], t2[q][:], ALU.mult)
                kb.stt(prod[q][:], rr[q][:], pp[:, 4, q:q + 1], t1[q][:], ALU.mult, ALU.mult)
                kb.act(pc[q][:], ncl[q][:], AF.Exp, scale=-1.0)
                kb.tt(rt[q][:], rr[q][:], pc[q][:], ALU.mult)
            ys_ = yst[g % 2]
            for j4 in range(4):
                pg_ = c.psum[j4]
                kb.mm(pg_[:, 0:256], sxg[64:128, j4 * 128:(j4 + 1) * 128], g2t[64:128, :], start=True, stop=True)
                kb.cp(gtok[:, j4, :], pg_[:, 0:256], eng=("act", "dve")[j4 % 2])

            def prep_gen(j):
                cs_ = slice(j * 128, (j + 1) * 128)
                ti = j % 2
                vt_, rk_, vb_ = vtok[ti], rkt[ti], vtokb[ti]
                for q in range(2):
                    kb.tr(P4[:, q * 128:(q + 1) * 128], vv[q][:, cs_], c.identf[:])
                for q in range(2):
                    kb.mm(P4[:, 256 + 2 * q:258 + 2 * q], prod[q][:, cs_], hsel[:], start=True, stop=True)
                yield
                kb.cp(vt_[:], P4[:, 0:256], eng="act")
                kb.cp(vb_[:], P4[:, 0:256], eng="dve")
                kb.cp(rk_[:], P4[:, 256:260], eng="act")
                yield

            def post_gen(j, ys_=ys_):
                cs_ = slice(j * 128, (j + 1) * 128)
                ti = j % 2
                vt_, ot_, rk_ = vtok[ti], otok[ti], rkt[ti]
                s6, ym, yb = st6[ti], ymid[ti], ybf[ti]
                for h in range(4):
                    hs = slice(h * 64, (h + 1) * 64)
                    kb.op("dve", lambda: c.nc.vector.bn_stats(s6[:, h * 6:h * 6 + 6].ap, ot_[:, hs].ap),
                          [ot_[:, hs]], [s6[:, h * 6:h * 6 + 6]])
                yield
                for h in range(4):
                    kb.op("dve", lambda: c.nc.vector.bn_aggr(s6[:, 24 + 2 * h:26 + 2 * h].ap, s6[:, h * 6:h * 6 + 6].ap),
                          [s6[:, h * 6:h * 6 + 6]], [s6[:, 24 + 2 * h:26 + 2 * h]])
                yield
                for h in range(4):
                    var = s6[:, 25 + 2 * h:26 + 2 * h]
                    kb.ts(var, var, 64e-5, ALU.add)
                    kb.tt(var, var, mhalf[:, 0:1], ALU.pow, eng="pool")
                yield
                for h in range(4):
                    hs = slice(h * 64, (h + 1) * 64)
                    var = s6[:, 25 + 2 * h:26 + 2 * h]
                    kb.ts(ym[:, hs], ot_[:, hs], s6[:, 24 + 2 * h:25 + 2 * h], ALU.subtract, var, ALU.mult)
                yield
                kb.tt(ym[:], ym[:], lnw[:], ALU.mult, eng="pool")
                kb.tt(ym[:], ym[:], lnb[:], ALU.add, eng="pool")
                yield
                for h in range(4):
                    hs = slice(h * 64, (h + 1) * 64)
                    kb.stt(ym[:, hs], vt_[:, hs], rk_[:, h:h + 1], ym[:, hs], ALU.mult, ALU.add)
                yield
                kb.tt(yb[:], ym[:], gtok[:, j, :], ALU.mult, eng="pool")
                yield
                for q in range(2):
                    kb.tr(c.psT[:, q * 128:(q + 1) * 128], yb[:, q * 128:(q + 1) * 128], c.identb[:])
                yield
                for q in range(2):
                    kb.cp(ys_[:, q, cs_], c.psT[:, q * 128:(q + 1) * 128], eng="act")

            for _ in prep_gen(0):
                pass
            for j in range(4):
                cs_ = slice(j * 128, (j + 1) * 128)
                ti = j % 2
                vb_, ot_ = vtokb[ti], otok[ti]
                def head_gen(h, j=j, cs_=cs_, vt_=vb_, ot_=ot_):
                    q, s_ = h // 2, h
                    hp = slice((h % 2) * 64, (h % 2) * 64 + 64)
                    hs = slice(h * 64, (h + 1) * 64)
                    kat_, kt_, bt_, rt_ = kat[q][hp, cs_], kt[q][hp, cs_], bt[q][hp, cs_], rt[q][hp, cs_]
                    V_ = vt_[:, hs]
                    PH = c.psum[h]
                    PY = (P6, P5)[h % 2]
                    PT = c.psT if h % 2 == 0 else c.psum[4][:, :].bitcast(BF16)
                    kb.mm(PH[:, 0:128], bt_, kat_, start=True, stop=True)
                    kb.mm(PH[:, 128:256], kat_, bt_, start=True, stop=True)
                    yield
                    kb.tt(BAa[s_][:], PH[:, 0:256], mSUSL[:], ALU.mult)
                    yield
                    kb.tt(Xa[s_][:], c.identf[:], BAa[s_][:, 0:128], ALU.subtract, eng="pool")
                    BAc, BAn, Xc, Xn = BAa[s_], BAb[s_], Xa[s_], Xb[s_]
                    PN = PH
                    for lev in range(5):
                        kb.mm(PN[:, 128:256], BAc[:, 0:128], BAc[:, 128:256], start=True, stop=True)
                        if lev < 4:
                            kb.mm(PN[:, 0:128], BAc[:, 128:256], BAc[:, 0:128], start=True, stop=True)
                        yield
                        kb.tt(IA[s_][:], PN[:, 128:256], c.identf[:], ALU.add)
                        if lev < 4:
                            kb.cp(BAn[:], PN[:, 0:256], eng="act")
                        yield
                        kb.mm(PN[:, 256:384], IA[s_][:], Xc[:], start=True, stop=True)
                        yield
                        kb.cp(Xn[:], PN[:, 256:384], eng=("dve", "act")[lev % 2])
                        BAc, BAn = BAn, BAc
                        Xc, Xn = Xn, Xc
                        yield
                    TT = Xc
                    kb.mm(PH[:, 0:128], kt_, kat_, start=True, stop=True)
                    kb.mm(PH[:, 128:256], bt_, rt_, start=True, stop=True)
                    kb.mm(PH[:, 256:384], kt_, rt_, start=True, stop=True)
                    yield
                    kb.tt(A3[s_][:], PH[:, 0:384], mSUIUIU[:], ALU.mult)
                    yield
                    idb = c.identb[hp, hp]
                    TB = (256 + h * 192) if h % 2 == 0 else (520 + (h // 2) * 192)
                    kb.tr(PT[:, TB:TB + 64], kat_, idb)
                    kb.tr(PT[:, TB + 64:TB + 128], bt_, idb)
                    kb.tr(PT[:, TB + 128:TB + 192], kt_, idb)
                    yield
                    kb.cp(RH[s_][:, 0:64], PT[:, TB:TB + 64], eng="act")
                    kb.cp(btTp[s_][:, hp], PT[:, TB + 64:TB + 128], eng="act")
                    kb.cp(btTpF[s_][:, hp], PT[:, TB + 64:TB + 128], eng="dve")
                    kb.cp(ktTp[s_][:, hp], PT[:, TB + 128:TB + 192], eng="act")
                    yield
                    kb.mm(PH[:, 256:320], A3[s_][:, 0:128], V_, start=True, stop=True)
                    yield
                    kb.cp(RH[s_][:, 64:128], PH[:, 256:320], eng="dve")
                    yield
                    kb.mm(PH[:, 0:128], TT[:], RH[s_][:], start=True, stop=True)
                    yield
                    kb.cp(Wpad[s_][:, hp], PH[:, 0:64], eng="dve")
                    kb.cp(WpadF[s_][:, hp], PH[:, 0:64], eng="act")
                    kb.ts(nU0[s_][:], PH[:, 64:128], -1.0, ALU.mult)
                    yield
                    kb.mm(PH[:, 128:256], Wpad[s_][:], A3[s_][:, 128:256], start=True, stop=True)
                    yield
                    kb.tt(R1[s_][hp, 0:64], rt_[:, 0:64], PH[hp, 128:192], ALU.subtract)
                    kb.tt(R2[s_][hp, 64:128], rt_[:, 64:128], PH[hp, 192:256], ALU.subtract)
                    yield
                    kb.mm(PY[:, hs], A3[s_][:, 256:384], V_, start=(h < 2), stop=False, skip_group_check=True)
                    kb.mm(PY[:, hs], A3[s_][:, 128:256], nU0[s_][:], start=False, stop=False, skip_group_check=True)
                    yield
                    for ci in range(2):
                        cr = slice(ci * 64, ci * 64 + 64)
                        Hc = Hs[h][hidx[h] % 3]
                        Hn = Hs[h][(hidx[h] + 1) % 3]
                        hidx[h] += 1
                        Rm = (R1, R2)[ci][s_]
                        kb.cp(Hb[s_][hp, :], Hc[hp, :], eng="pool")
                        kb.mm(PY[:, hs], Rm[hp, :], Hb[s_][hp, :], start=False, stop=(ci == 1), skip_group_check=True)
                        kb.mm(PH[:, 256:320], ktTp[s_][cr, :], V_[cr, :], start=True, stop=False)
                        kb.mm(PH[:, 256:320], btTp[s_][cr, :], nU0[s_][cr, :], start=False, stop=True)
                        kb.mm(PH[:, 384:512], WpadF[s_][cr, :], btTpF[s_][cr, :], start=True, stop=True)
                        yield
                        PC = pc[q][hp, j * 128 + ci * 64 + 63:j * 128 + ci * 64 + 64]
                        kb.ts(Npp[s_][hp, :], PH[hp, 256:320], PC, ALU.mult)
                        kb.tt(MpT[s_][hp, :], c.identf[hp, :], PH[hp, 384:512], ALU.subtract)
                        yield
                        kb.mm(PH[:, 0:64], MpT[s_][hp, :], Hc[hp, :], start=True, stop=True)
                        yield
                        kb.stt(Hn[hp, :], PH[hp, 0:64], PC, Npp[s_][hp, :], ALU.mult, ALU.add)
                        yield
                    kb.cp(ot_[:, hs], PY[:, hs], eng="act")
                def pp_gen(j=j):
                    if j > 0:
                        yield from post_gen(j - 1)
                    if j + 1 < 4:
                        yield from prep_gen(j + 1)

                def allg(j=j):
                    yield pp_gen()
                    for h in range(4):
                        yield head_gen(h)
                run_pipelined(allg(), 5)
            for _ in post_gen(3):
                pass
            kb.dma(c.yT[768:1024, g * 512:(g + 1) * 512].rearrange("(q p) t -> p q t", p=128), ys_[:])


def consts(S):
    bf = ml_dtypes.bfloat16
    d = {}
    d["ident"] = np.eye(128, dtype=np.float32)
    s = np.arange(128)
    d["maskneg"] = np.where(s[:, None] > s[None, :], -30000.0, 0.0).astype(bf)
    d["triu"] = (s[:, None] <= s[None, :]).astype(np.float32)
    d["strl"] = (s[:, None] > s[None, :]).astype(np.float32)
    same = (s[:, None] // 64) == (s[None, :] // 64)
    d["mSU"] = ((s[:, None] < s[None, :]) & same).astype(np.float32)
    d["mSL"] = ((s[:, None] > s[None, :]) & same).astype(np.float32)
    d["mIU"] = ((s[:, None] <= s[None, :]) & same).astype(np.float32)
    d["bones"] = same.astype(np.float32)
    d["rmask"] = np.tile((np.arange(512) % 64 != 0).astype(np.float32)[None, :], (128, 1))
    d["hsel"] = np.stack([(s < 64), (s >= 64)], 1).astype(np.float32)
    pos = np.arange(S)
    hi = ((pos // 64) * 64).astype(np.float32)
    lo = (pos % 64).astype(np.float32)
    ak = np.zeros((4, 4, S), np.float32)
    aq = np.zeros((4, 4, S), np.float32)
    for h in range(4):
        sl = 2.0 ** (-8.0 * (h + 1) / 4)
        ak[h, 0] = sl * hi; ak[h, 1] = sl * lo; ak[h, 2] = 1; ak[h, 3] = 1
        aq[h, 0] = 1; aq[h, 1] = 1; aq[h, 2] = -sl * hi; aq[h, 3] = -sl * lo
    d["alibi_k"] = ak.astype(bf); d["alibi_q"] = aq.astype(bf)
    return d


SEQ = 8192
DEPTH = 2
_CACHE = {}


def kernel(**inputs):
    S, L = SEQ, DEPTH
    if "nc" not in _CACHE:
        _CACHE["nc"] = build(S, L)
        _CACHE["consts"] = consts(S)
    nc = _CACHE["nc"]
    x = np.ascontiguousarray(np.asarray(inputs["x"], dtype=np.float32))
    B = x.shape[0]
    shared = {k: np.ascontiguousarray(np.asarray(v, dtype=np.float32)) for k, v in inputs.items() if k != "x"}
    shared.update(_CACHE["consts"])
    in_maps = []
    for b in range(B):
        m = dict(shared)
        m["x"] = x[b]
        in_maps.append(m)
    res = run_bass_kernel_spmd(nc, in_maps, core_ids=list(range(B)))
    return np.stack([np.asarray(r["out"]) for r in res.results], axis=0).astype(np.float32)
```

```python
import contextlib, math
import numpy as np
import ml_dtypes
from concourse.bass_utils import run_bass_kernel_spmd
import contextlib
import numpy as np
import concourse.bass as bass
import concourse.mybir as mybir

F32 = mybir.dt.float32
BF16 = mybir.dt.bfloat16
AF = mybir.ActivationFunctionType
ALU = mybir.AluOpType
AX = mybir.AxisListType

SAME_ENGINE_RAW_SYNC = True


class Reg:
    __slots__ = ("w", "r")

    def __init__(self):
        self.w = None
        self.r = {}


class V:
    __slots__ = ("t", "ap", "key")

    def __init__(self, t, ap, key):
        self.t = t
        self.ap = ap
        self.key = key

    def rearrange(self, pat, **kw):
        return V(self.t, self.ap.rearrange(pat, **kw), self.key)

    def partition_broadcast(self, n):
        return V(self.t, self.ap.partition_broadcast(n), self.key)

    def bitcast(self, dt):
        return V(self.t, self.ap.bitcast(dt), self.key)

    def __getitem__(self, idx):
        return V(self.t, self.ap[idx], self.key)


class _Keyed:
    def __init__(self, t, key):
        self.t = t
        self.key = key

    def __getitem__(self, idx):
        return V(self.t, self.t.h[idx], self.key)


class T:
    def __init__(self, h, name, excl=False):
        self.h = h
        self.name = name
        self.excl = excl
        self.regs = {None: Reg()}

    def __getitem__(self, idx):
        return V(self, self.h[idx], None)

    def k(self, key):
        return _Keyed(self, key)

    def regs_of(self, key):
        if key is None:
            return list(self.regs.values())
        if key not in self.regs:
            r = Reg()
            self.regs[key] = r
        return [self.regs[None], self.regs[key]]


class KB:
    COMPUTE = ("pe", "act", "dve", "pool")

    def __init__(self, nc, es, n_dma_sems=40):
        self.nc = nc
        self.es = es
        self.engs = {"pe": nc.tensor, "act": nc.scalar, "dve": nc.vector, "pool": nc.gpsimd, "sp": nc.sync}
        self.sem = {}
        self.tick = {}
        for e in self.COMPUTE:
            self.sem[e] = es.enter_context(nc.semaphore("sem_" + e))
            self.tick[e] = 0
        self.waited = {}
        self.dma_sems = []
        for i in range(n_dma_sems):
            self.dma_sems.append([es.enter_context(nc.semaphore("dsem%d" % i)), 0])
        self.dma_next = 0
        self.n_inst = 0
        self.n_wait = 0

    def sb(self, name, shape, dtype, es=None):
        es = es or self.es
        self._uid = getattr(self, "_uid", 0) + 1
        name = "%s_u%d" % (name, self._uid)
        return T(es.enter_context(self.nc.sbuf_tensor(name, list(shape), dtype)), name)

    def ps(self, name, shape, dtype, es=None):
        es = es or self.es
        return T(es.enter_context(self.nc.psum_tensor(name, list(shape), dtype)), name, excl=True)

    def dram(self, name, shape, dtype, kind="Internal"):
        return T(self.nc.dram_tensor(name, list(shape), dtype, kind=kind).ap(), name)

    def _collect(self, reads, writes):
        deps = {}

        def add(tok, raw):
            if tok is None:
                return
            k = tok[3]
            old = deps.get(k)
            if old is None or old[1] < tok[1]:
                deps[k] = (tok[0], tok[1], tok[2], old[3] or raw if old else raw)
            elif raw and not old[3]:
                deps[k] = (old[0], old[1], old[2], True)

        for v in reads:
            for reg in v.t.regs_of(v.key):
                add(reg.w, True)
        for v in writes:
            for reg in v.t.regs_of(v.key):
                add(reg.w, False)
                for tok in reg.r.values():
                    add(tok, False)
        return deps

    def _waits(self, eng, deps):
        for k, (sem, val, src, raw) in deps.items():
            if src == eng:
                if eng == "pe":
                    continue
                if not SAME_ENGINE_RAW_SYNC:
                    continue
            wk = (eng, k)
            if self.waited.get(wk, 0) >= val:
                continue
            self.engs[eng].wait_ge(sem, val)
            self.n_wait += 1
            self.waited[wk] = val

    def _update(self, tok, eng_key, reads, writes):
        for v in writes:
            if v.key is None:
                for reg in v.t.regs.values():
                    reg.w = tok
                    reg.r = {}
            else:
                regs = v.t.regs_of(v.key)
                regs[1].w = tok
                regs[1].r = {}
        for v in reads:
            if v.key is None:
                v.t.regs[None].r[eng_key] = tok
            else:
                v.t.regs_of(v.key)[1].r[eng_key] = tok

    def op(self, eng, fn, reads, writes):
        xr = [v for v in reads if v.t.excl]
        if xr:
            writes = list(writes) + [V(v.t, v.ap, None) for v in xr]
        deps = self._collect(reads, writes)
        self._waits(eng, deps)
        inst = fn()
        self.tick[eng] += 1
        inst.then_inc(self.sem[eng], 1)
        tok = (self.sem[eng], self.tick[eng], eng, eng)
        self._update(tok, eng, reads, writes)
        self.n_inst += 1
        return inst

    def dma(self, out, in_, q="sp", **kw):
        slot = self.dma_sems[self.dma_next % len(self.dma_sems)]
        self.dma_next += 1
        sem, cnt = slot
        kid = "d%d" % id(sem)
        if cnt > 0:
            wk = (q, kid)
            if self.waited.get(wk, 0) < cnt:
                self.engs[q].wait_ge(sem, cnt)
                self.waited[wk] = cnt
        reads, writes = [in_], [out]
        deps = self._collect(reads, writes)
        for k, (s, val, src, raw) in deps.items():
            wk = (q, k)
            if src == q and not k.startswith("d"):
                pass
            if self.waited.get(wk, 0) >= val:
                continue
            self.engs[q].wait_ge(s, val)
            self.n_wait += 1
            self.waited[wk] = val
        inst = self.engs[q].dma_start(out=out.ap, in_=in_.ap, **kw)
        inst.then_inc(sem, 16)
        slot[1] = cnt + 16
        tok = (sem, cnt + 16, "dma", kid)
        self._update(tok, kid, reads, writes)
        self.n_inst += 1
        return inst

    def barrier(self):
        for e in self.engs:
            for s in self.COMPUTE:
                if s == e or self.tick[s] == 0:
                    continue
                wk = (e, s)
                if self.waited.get(wk, 0) < self.tick[s]:
                    self.engs[e].wait_ge(self.sem[s], self.tick[s])
                    self.waited[wk] = self.tick[s]
            for sem, cnt in self.dma_sems:
                if cnt == 0:
                    continue
                wk = (e, "d%d" % id(sem))
                if self.waited.get(wk, 0) < cnt:
                    self.engs[e].wait_ge(sem, cnt)
                    self.waited[wk] = cnt

    def finish(self):
        self.barrier()

    @contextlib.contextmanager
    def scope(self):
        with contextlib.ExitStack() as es:
            yield es
            self.barrier()

    def mm(self, out, lhsT, rhs, start=True, stop=True, **kw):
        return self.op("pe", lambda: self.nc.tensor.matmul(out.ap, lhsT.ap, rhs.ap, start=start, stop=stop, **kw),
                       [lhsT, rhs], [out])

    def tr(self, out, in_, ident):
        return self.op("pe", lambda: self.nc.tensor.transpose(out.ap, in_.ap, ident.ap), [in_, ident], [out])

    def act(self, out, in_, func, bias=None, scale=None, accum=None, extra_reads=()):
        kw = {}
        reads = [in_] + list(extra_reads)
        if bias is not None:
            if isinstance(bias, V):
                kw["bias"] = bias.ap
                reads.append(bias)
            else:
                kw["bias"] = bias
        if scale is not None:
            if isinstance(scale, V):
                kw["scale"] = scale.ap
                reads.append(scale)
            else:
                kw["scale"] = scale
        writes = [out]
        if accum is not None:
            kw["accum_out"] = accum.ap
            writes.append(accum)
        return self.op("act", lambda: self.nc.scalar.activation(out=out.ap, in_=in_.ap, func=func, **kw), reads, writes)

    def _e(self, eng):
        return self.engs[eng]

    def tt(self, out, a, b, op, eng="dve"):
        return self.op(eng, lambda: self._e(eng).tensor_tensor(out.ap, a.ap, b.ap, op), [a, b], [out])

    def ts(self, out, a, s1, op0, s2=None, op1=None, eng="dve", accum=None):
        reads = [a]
        sa1 = s1
        if isinstance(s1, V):
            reads.append(s1)
            sa1 = s1.ap
        sa2 = s2
        if isinstance(s2, V):
            reads.append(s2)
            sa2 = s2.ap
        writes = [out]
        kw = {}
        if op1 is not None:
            kw["op1"] = op1
        if accum is not None:
            kw["accum_out"] = accum.ap
            writes.append(accum)
        return self.op(eng, lambda: self._e(eng).tensor_scalar(out.ap, a.ap, sa1, sa2, op0, **kw), reads, writes)

    def stt(self, out, a, s, b, op0, op1, eng="dve", accum=None):
        assert eng == "dve"
        reads = [a, b]
        sa = s
        if isinstance(s, V):
            reads.append(s)
            sa = s.ap
        writes = [out]
        kw = {}
        if accum is not None:
            kw["accum_out"] = accum.ap
            writes.append(accum)
        return self.op(eng, lambda: self._e(eng).scalar_tensor_tensor(out.ap, a.ap, sa, b.ap, op0, op1, **kw), reads, writes)

    def cp(self, out, in_, eng="dve"):
        if eng == "act":
            return self.op("act", lambda: self.nc.scalar.copy(out.ap, in_.ap), [in_], [out])
        return self.op(eng, lambda: self._e(eng).tensor_copy(out.ap, in_.ap), [in_], [out])

    def scan(self, out, d0, d1, init, op0, op1):
        reads = [d0, d1]
        ia = init
        if isinstance(init, V):
            reads.append(init)
            ia = init.ap
        return self.op("dve", lambda: self.nc.vector.tensor_tensor_scan(out.ap, d0.ap, d1.ap, ia, op0, op1), reads, [out])

    def red(self, out, in_, op, axis=AX.X, eng="dve"):
        return self.op(eng, lambda: self._e(eng).tensor_reduce(out.ap, in_.ap, axis, op), [in_], [out])

    def recip(self, out, in_):
        return self.op("dve", lambda: self.nc.vector.reciprocal(out.ap, in_.ap), [in_], [out])

    def memset(self, out, val, eng="dve"):
        return self.op(eng, lambda: self._e(eng).memset(out.ap, val), [], [out])


D = 1024
DIN = 3464
DFF = 2816
SEGS = [("dq", 0, 256), ("dk", 256, 256), ("dv", 512, 256), ("sz", 768, 256), ("sxbc", 1024, 768),
        ("fq", 1796, 256), ("fk", 2052, 256), ("fv", 2308, 256), ("rp", 2568, 896), ("sdt", 1792, 4), ("ff", 2564, 4)]
SOFF = {}
_o = 0
for _n, _c, _w in SEGS:
    SOFF[_n] = _o
    _o += _w
assert _o == DIN


class Ctx:
    pass


def run_pipelined(gen_iter, depth):
    active = []
    it = iter(gen_iter)
    done = False
    while True:
        if not done and len(active) < depth:
            try:
                active.append(next(it))
            except StopIteration:
                done = True
        if not active:
            if done:
                break
            continue
        for g in list(active):
            try:
                next(g)
            except StopIteration:
                active.remove(g)


def build(S, L, dbg=(), ext_in=()):
    nc = bass.Bass("TRN2", target_bir_lowering=False)
    es = contextlib.ExitStack()
    kb = KB(nc, es)
    _dram = kb.dram
    kb.dram = lambda name, shape, dt: _dram(name, shape, dt, kind=("ExternalOutput" if name in dbg else ("ExternalInput" if name in ext_in else "Internal")))
    c = Ctx()
    c.nc, c.kb, c.S, c.L = nc, kb, S, L
    NG = S // 512
    c.NG = NG
    def ext(name, shape, dt=F32):
        return T(nc.dram_tensor(name, list(shape), dt, kind="ExternalInput").ap(), name)

    c.x = ext("x", [S, D])
    c.w_in = ext("w_in", [L, D, DIN])
    c.norm_mix_pre = ext("norm_mix_pre", [L, D])
    c.ident = ext("ident", [128, 128])
    c.out = T(nc.dram_tensor("out", [S, D], F32, kind="ExternalOutput").ap(), "out")
    c.xT = kb.dram("xT", [D, S], F32)
    c.qd = kb.dram("qd", [256, S], BF16)
    c.kd = kb.dram("kd", [256, S], BF16)
    c.fq = kb.dram("fq", [256, S], BF16)
    c.fk = kb.dram("fk", [256, S], BF16)
    c.vd = kb.dram("vd", [S, 256], BF16)
    c.fv = kb.dram("fv", [S, 256], BF16)
    c.z = kb.dram("z", [S, 256], F32)
    c.xbc = kb.dram("xbc", [768, S], F32)
    c.rp = kb.dram("rp", [896, S], F32)
    c.dtffT = kb.dram("dtffT", [S, 8], F32)
    c.dtffF = kb.dram("dtffF", [8, S], F32)
    c.yT = kb.dram("yT", [D, S], BF16)
    c.identf = kb.sb("identf", [128, 128], F32)
    c.identb = kb.sb("identb", [128, 128], BF16)
    c.onesb = kb.sb("onesb", [128, 128], BF16)
    kb.dma(c.identf[:], c.ident[:, :])
    kb.cp(c.identb[:], c.identf[:])
    kb.memset(c.onesb[:], 1.0)
    c.psum = [kb.ps("psum%d" % i, [128, 512], F32) for i in range(7)]
    c.psT = kb.ps("psumT", [128, 1024], BF16)
    c.hmidT = kb.dram("hmidT", [S // 512, 128, DFF // 128, 512], BF16)
    c.triu = ext("triu", [128, 128])
    c.mSU = ext("mSU", [128, 128]); c.mSL = ext("mSL", [128, 128]); c.mIU = ext("mIU", [128, 128])
    c.bones = ext("bones", [128, 128]); c.rmask = ext("rmask", [128, 512]); c.hsel = ext("hsel", [128, 2])
    for nm, shp in (("rwkv_mu", [L, 896]), ("rwkv_w0", [L, 256]), ("rwkv_w2", [L, 32, 256]), ("rwkv_a0", [L, 256]),
                    ("rwkv_a2", [L, 32, 256]), ("rwkv_g2", [L, 64, 256]), ("rwkv_k_k", [L, 256]), ("rwkv_k_a", [L, 256]),
                    ("rwkv_r_k", [L, 4, 64]), ("rwkv_ln_w", [L, 256]), ("rwkv_ln_b", [L, 256])):
        setattr(c, nm, ext(nm, shp))
    c.strl = ext("strl", [128, 128])
    for nm, shp in (("ssm_conv_w", [L, 4, 768]), ("ssm_conv_b", [L, 768]), ("ssm_dt_bias", [L, 4]), ("ssm_a_log", [L, 4]),
                    ("ssm_d", [L, 4]), ("ssm_norm", [L, 256])):
        setattr(c, nm, ext(nm, shp))
    for nm, shp in (("norm_mix_post", [L, D]), ("norm_ffn_pre", [L, D]), ("norm_ffn_post", [L, D]),
                    ("w_out", [L, D, D]), ("ffn_w_gate", [L, D, DFF]), ("ffn_w_up", [L, D, DFF]),
                    ("ffn_conv_w", [L, 3, DFF]), ("ffn_conv_b", [L, DFF]), ("ffn_w_down", [L, DFF, D])):
        setattr(c, nm, ext(nm, shp))
    c.maskneg = ext("maskneg", [128, 128], BF16)
    c.alibi_k = ext("alibi_k", [4, 4, S], BF16)
    c.alibi_q = ext("alibi_q", [4, 4, S], BF16)
    for nm, shp in (("diff_lambda_q1", [L, 32]), ("diff_lambda_k1", [L, 32]), ("diff_lambda_q2", [L, 32]),
                    ("diff_lambda_k2", [L, 32]), ("diff_subln", [L, 64]), ("fox_f_bias", [L, 4]), ("fox_norm", [L, 64])):
        setattr(c, nm, ext(nm, shp))

    ph = "P1,A,B,C,D,E,F"
    if "P1" in ph:
        phase_P1(c)
    for l in range(L):
        if "A" in ph:
            phase_A(c, l)
        if "B" in ph:
            phase_B(c, l)
        if "C" in ph:
            phase_C(c, l)
        if "D" in ph:
            phase_D(c, l)
        if "E" in ph:
            phase_out(c, l, "E", c.yT, 8, c.w_out, c.norm_mix_post, False)
        if "F" in ph:
            phase_F1(c, l)
            phase_out(c, l, "G", c.hmidT, DFF // 128, c.ffn_w_down, c.norm_ffn_post, l == L - 1, grouped_src=True)
    kb.finish()
    es.close()
    print("n_inst", kb.n_inst, "n_wait", kb.n_wait)
    return nc


def phase_P1(c):
    kb, S = c.kb, c.S
    with kb.scope() as es:
        xin = [kb.sb("p1_xin%d" % i, [128, 4, D], F32, es) for i in range(2)]
        stg = [kb.sb("p1_stg%d" % i, [128, 8, 512], F32, es) for i in range(2)]
        for g in range(c.NG):
            xi = xin[g % 2]
            st = stg[g % 2]
            kb.dma(xi[:], c.x[g * 512:(g + 1) * 512, :].rearrange("(j p) d -> p j d", p=128))
            for kc in range(8):
                ps = c.psum[kc % 4]
                for j in range(4):
                    kb.tr(ps[:, j * 128:(j + 1) * 128], xi[:, j, kc * 128:(kc + 1) * 128], c.identf[:])
                if kc % 2 == 0:
                    kb.cp(st[:, kc, :], ps[:, :], eng="act")
                else:
                    kb.cp(st[:, kc, :], ps[:, :], eng="dve")
            kb.dma(c.xT[:, g * 512:(g + 1) * 512].rearrange("(kc p) t -> p kc t", p=128), st[:])


def load_w_bf16(c, es_, name, w_dram_rows, K, cols_plan, gscale=None, stage_cols=1024):
    kb = c.kb
    ncols = sum(p[2] for p in cols_plan)
    W = kb.sb(name, [128, K, ncols], BF16, es_)
    with kb.scope() as es:
        stg = [kb.sb(name + "_stg%d" % i, [128, stage_cols], F32, es) for i in range(3)]
        n = 0
        for kc in range(K):
            for (dst, src, w, mult) in cols_plan:
                for o in range(0, w, stage_cols):
                    ww = min(stage_cols, w - o)
                    st = stg[n % 3]
                    kb.dma(st[:, 0:ww], w_dram_rows(kc, src + o, ww))
                    use_act = (n % 2 == 1) and float(mult) == 1.0
                    if use_act:
                        if gscale is not None:
                            kb.act(W[:, kc, dst + o:dst + o + ww], st[:, 0:ww], AF.Copy, scale=gscale[:, kc:kc + 1])
                        else:
                            kb.cp(W[:, kc, dst + o:dst + o + ww], st[:, 0:ww], eng="act")
                    elif gscale is not None:
                        kb.ts(W[:, kc, dst + o:dst + o + ww], st[:, 0:ww], gscale[:, kc:kc + 1], ALU.mult,
                              float(mult), ALU.mult, eng="dve")
                    else:
                        kb.ts(W[:, kc, dst + o:dst + o + ww], st[:, 0:ww], float(mult), ALU.mult, eng="dve")
                    n += 1
    return W


def rstd_bc(c, xg, sq, rs, ps):
    kb = c.kb
    for kc in range(8):
        kb.act(sq[:, kc, :], xg[:, kc, :], AF.Square)
    for kc in range(8):
        kb.mm(ps[:, :], c.onesb[:], sq[:, kc, :], start=(kc == 0), stop=(kc == 7))
    kb.act(rs[:], ps[:, :], AF.Sqrt, bias=c.eps_t[:, 0:1], scale=1.0 / D)
    kb.recip(rs[:], rs[:])


def phase_A(c, l):
    kb, S = c.kb, c.S
    with kb.scope() as es:
        g_t = kb.sb("A_g", [128, 8], F32, es)
        c.eps_t = kb.sb("A_eps", [128, 1], F32, es)
        kb.memset(c.eps_t[:], 1e-6)
        kb.dma(g_t[:], c.norm_mix_pre[l, :].rearrange("(kc p) -> p kc", p=128), allow_slow_non_contiguous=True)
        plan = []
        for n_, oc, w in SEGS:
            mult = 32 ** -0.5 if n_ == "dq" else (64 ** -0.5 if n_ == "fq" else 1.0)
            plan.append((SOFF[n_], oc, w, mult))
        W = load_w_bf16(c, es, "A_W", lambda kc, so, ww: c.w_in[l, kc * 128:(kc + 1) * 128, so:so + ww], 8, plan,
                        gscale=g_t)
        ASTOP = 99
        if ASTOP <= 1:
            return
        xg = [kb.sb("A_xg%d" % i, [128, 8, 512], F32, es) for i in range(2)]
        sq = kb.sb("A_sq", [128, 8, 512], BF16, es)
        rs = kb.sb("A_rs", [128, 512], F32, es)
        xn = [kb.sb("A_xn%d" % i, [128, 8, 512], BF16, es) for i in range(2)]
        stF = [kb.sb("A_stF%d" % i, [128, 512], F32, es) for i in range(3)]
        stB = [kb.sb("A_stB%d" % i, [128, 512], BF16, es) for i in range(3)]
        stT = [kb.sb("A_stT%d" % i, [128, 1024], F32, es) for i in range(2)]
        stTb = [kb.sb("A_stTb%d" % i, [128, 512], BF16, es) for i in range(2)]
        fch = []
        for n_, dst, bf in (("dq", c.qd, True), ("dk", c.kd, True), ("fq", c.fq, True), ("fk", c.fk, True),
                            ("sxbc", c.xbc, False), ("rp", c.rp, False)):
            w = dict((a, cc) for a, b, cc in SEGS)[n_]
            for o in range(0, w, 128):
                fch.append((SOFF[n_] + o, 128, dst, o, bf))
        fch.append((SOFF["sdt"], 8, c.dtffF, 0, False))
        nev = 0
        for g in range(c.NG):
            ts_ = slice(g * 512, (g + 1) * 512)
            x_ = xg[g % 2]
            xn_ = xn[g % 2]
            def prologue(gg):
                xx_ = xg[gg % 2]
                kb.dma(xx_[:], c.xT[:, gg * 512:(gg + 1) * 512].rearrange("(kc p) t -> p kc t", p=128))
                rstd_bc(c, xx_, sq, rs, c.psum[0])
                for kc in range(8):
                    kb.tt(xn[gg % 2][:, kc, :], xx_[:, kc, :], rs[:], ALU.mult, eng=("dve", "pool")[kc % 2])
            if g == 0:
                prologue(0)
            for i, (so, w, dst, ro, bf) in enumerate(fch):
                if i == 8 and g + 1 < c.NG:
                    prologue(g + 1)
                ps = c.psum[1 + i % 4]
                for kc in range(8):
                    kb.mm(ps[0:w, :], W[:, kc, so:so + w], xn_[:, kc, :], start=(kc == 0), stop=(kc == 7))
                st = (stB if bf else stF)[nev % 3]
                nev += 1
                kb.cp(st[0:w, :], ps[0:w, :], eng=("act", "dve")[nev % 2])
                kb.dma(dst[ro:ro + w, ts_], st[0:w, :])
            if ASTOP <= 4:
                continue
            for j in range(4):
                tsj = slice(g * 512 + j * 128, g * 512 + (j + 1) * 128)
                p1, p2 = c.psum[5], c.psum[6]
                for kc in range(8):
                    kb.mm(p1[:, :], xn_[:, kc, j * 128:(j + 1) * 128], W[:, kc, SOFF["dv"]:SOFF["dv"] + 512],
                          start=(kc == 0), stop=(kc == 7))
                for kc in range(8):
                    kb.mm(p2[:, 0:256], xn_[:, kc, j * 128:(j + 1) * 128], W[:, kc, SOFF["fv"]:SOFF["fv"] + 256],
                          start=(kc == 0), stop=(kc == 7))
                for kc in range(8 if ASTOP > 5 else 0):
                    kb.mm(p2[:, 256:264], xn_[:, kc, j * 128:(j + 1) * 128], W[:, kc, SOFF["sdt"]:SOFF["sdt"] + 8],
                          start=(kc == 0), stop=(kc == 7))
                sb_ = stTb[j % 2]
                sf_ = stT[j % 2]
                kb.cp(sb_[:, 0:256], p1[:, 0:256], eng="act")
                kb.cp(sf_[:, 0:256], p1[:, 256:512], eng="dve")
                kb.cp(sb_[:, 256:512], p2[:, 0:256], eng="act")
                if ASTOP > 5:
                    kb.cp(sf_[:, 256:264], p2[:, 256:264], eng="dve")
                if ASTOP != 5:
                    kb.dma(c.vd[tsj, :], sb_[:, 0:256])
                    kb.dma(c.fv[tsj, :], sb_[:, 256:512])
                    kb.dma(c.z[tsj, :], sf_[:, 0:256])
                if ASTOP > 6:
                    kb.dma(c.dtffT[tsj, :], sf_[:, 256:264])


def phase_B(c, l):
    kb, S, NG = c.kb, c.S, c.NG
    NB = S // 128
    lam_init = 0.8 - 0.6 * math.exp(-0.3 * l)
    with kb.scope() as es:
        maskb = kb.sb("B_mask", [128, 128], BF16, es)
        kb.dma(maskb[:], c.maskneg[:, :])
        epsc = kb.sb("B_eps", [128, 1], F32, es)
        kb.memset(epsc[:], 1e-6)
        onec = kb.sb("B_one", [128, 1], F32, es)
        kb.memset(onec[:], 1.0)
        mhalfB = kb.sb("B_mhalf", [128, 1], F32, es)
        kb.memset(mhalfB[:], -0.5)
        lv = kb.sb("B_lv", [128, 4, 32], F32, es)
        for i, t in enumerate((c.diff_lambda_q1, c.diff_lambda_k1, c.diff_lambda_q2, c.diff_lambda_k2)):
            kb.dma(lv[:, i, :], t[l:l + 1, :].partition_broadcast(128))
        lj = kb.sb("B_lj", [128, 32], F32, es)
        ls = kb.sb("B_ls", [128, 4], F32, es)
        kb.stt(lj[:], lv[:, 0, :], 1.0, lv[:, 1, :], ALU.mult, ALU.mult, accum=ls[:, 0:1])
        kb.stt(lj[:], lv[:, 2, :], 1.0, lv[:, 3, :], ALU.mult, ALU.mult, accum=ls[:, 1:2])
        kb.act(ls[:, 0:2], ls[:, 0:2], AF.Exp)
        kb.tt(ls[:, 2:3], ls[:, 1:2], ls[:, 0:1], ALU.subtract)
        kb.ts(ls[:, 3:4], ls[:, 2:3], -lam_init, ALU.add)
        neg_lam = ls[:, 3:4]
        gd = kb.sb("B_gd", [128, 64], F32, es)
        gf = kb.sb("B_gf", [128, 64], F32, es)
        kb.dma(gd[:], c.diff_subln[l:l + 1, :].partition_broadcast(128))
        kb.dma(gf[:], c.fox_norm[l:l + 1, :].partition_broadcast(128))
        kb.ts(gd[:], gd[:], 1.0 - lam_init, ALU.mult)
        chi = kb.sb("B_chi", [4, S], BF16, es)
        cmid = kb.sb("B_cmid", [4, S], BF16, es)
        clo = kb.sb("B_clo", [4, S], BF16, es)
        nhi = kb.sb("B_nhi", [4, S], BF16, es)
        nmid = kb.sb("B_nmid", [4, S], BF16, es)
        nlo = kb.sb("B_nlo", [4, S], BF16, es)
        with kb.scope() as es2:
            cf = kb.sb("B_cf", [4, S], F32, es2)
            c1 = kb.sb("B_c1", [4, S], F32, es2)
            ones4 = kb.sb("B_ones4", [4, S], F32, es2)
            fb = kb.sb("B_fb", [4, 1], F32, es2)
            kb.dma(cf[:], c.dtffF[4:8, :])
            kb.dma(fb[:], c.fox_f_bias[l, :].rearrange("(p o) -> p o", o=1))
            kb.ts(fb[:], fb[:], -1.0, ALU.mult)
            kb.memset(ones4[:], 1.0)
            kb.act(c1[:], cf[:], AF.Exp, bias=fb[:, 0:1], scale=-1.0)
            kb.act(c1[:], c1[:], AF.Ln, bias=onec[0:4, 0:1], scale=1.0)
            kb.ts(c1[:], c1[:], -1.0, ALU.mult)
            kb.scan(cf[:], ones4[:], c1[:], 0.0, ALU.mult, ALU.add)
            kb.cp(chi[:], cf[:])
            kb.tt(c1[:], cf[:], chi[:], ALU.subtract)
            kb.cp(cmid[:], c1[:])
            kb.tt(c1[:], c1[:], cmid[:], ALU.subtract)
            kb.cp(clo[:], c1[:])
            for a, b in ((nhi, chi), (nmid, cmid), (nlo, clo)):
                kb.ts(a[:], b[:], -1.0, ALU.mult)
        Ka = [kb.sb("B_Ka%d" % i, [128, S], BF16, es) for i in range(2)]
        Qa = [kb.sb("B_Qa%d" % i, [128, S], BF16, es) for i in range(2)]
        for i in range(2):
            kb.memset(Ka[i][:], 0.0, eng="pool")
            kb.memset(Qa[i][:], 0.0, eng="dve")
        Va = [kb.sb("B_Va%d" % i, [128, NB, 65], BF16, es) for i in range(2)]
        o1 = kb.sb("B_o1", [128, NB, 64], F32, es)
        pT = [kb.sb("B_pT%d" % i, [128, 512], BF16, es) for i in range(5)]
        SB = [c.psum[0], c.psum[1], c.psum[4], c.psum[5]]
        LA = 3
        sm = [kb.sb("B_sm%d" % i, [128, 8], F32, es) for i in range(4)]
        tmp = [kb.sb("B_tmp%d" % i, [128, 64], F32, es) for i in range(4)]
        junk = kb.sb("B_junk", [128, 64], F32, es)
        ybf = [kb.sb("B_ybf%d" % i, [128, 64], BF16, es) for i in range(4)]
        yst = [kb.sb("B_yst%d" % i, [64, 512], BF16, es) for i in range(2)]
        maps = [("d", m) for m in range(8)] + [("f", h) for h in range(4)]
        npt = 0
        nfin = 0
        ngrp = 0
        pending = []
        for mi, (kind, m) in enumerate(maps):
            K_, Q_, V_ = Ka[mi % 2], Qa[mi % 2], Va[mi % 2]
            if kind == "d":
                h = m // 2
                R = 36
                kb.dma(K_[0:32, :], c.kd[m * 32:(m + 1) * 32, :])
                kb.dma(Q_[0:32, :], c.qd[m * 32:(m + 1) * 32, :])
                kb.dma(K_[32:36, :], c.alibi_k[h, :, :])
                kb.dma(Q_[32:36, :], c.alibi_q[h, :, :])
                vsrc, yrow, gv = c.vd, h * 64, gd
            else:
                h = m
                R = 70
                kb.dma(K_[0:64, :], c.fk[h * 64:(h + 1) * 64, :])
                kb.dma(Q_[0:64, :], c.fq[h * 64:(h + 1) * 64, :])
                kb.memset(K_[64:70, :], 1.0)
                kb.memset(Q_[64:70, :], 1.0)
                for i, (a, b) in enumerate(((nhi, chi), (nmid, cmid), (nlo, clo))):
                    kb.dma(K_[64 + i:65 + i, :], a[h:h + 1, :])
                    kb.dma(Q_[67 + i:68 + i, :], b[h:h + 1, :])
                vsrc, yrow, gv = c.fv, 512 + h * 64, gf
            kb.dma(V_[:, :, 0:64], vsrc[:, h * 64:(h + 1) * 64].rearrange("(j p) d -> p j d", p=128))
            kb.memset(V_[:, :, 64:65], 1.0)
            for g in range(NG):
                tiles = list(range(4 * g + 4))
                Ob = c.psum[2 + (ngrp % 2)]
                ngrp += 1

                def emit_qk(j, g=g, K_=K_, Q_=Q_, R=R):
                    jj = max(0, j - 4 * g)
                    c0 = 128 * jj
                    diag = j >= 4 * g
                    ps_s = SB[j % 4]
                    kb.mm(ps_s[:, c0:512], K_[:, j * 128:(j + 1) * 128], Q_[:, g * 512 + c0:(g + 1) * 512],
                          start=True, stop=not diag)
                    if diag:
                        kb.mm(ps_s[:, c0:c0 + 128], c.identb[:], maskb[:], start=False, stop=True)

                for j0 in range(min(LA, len(tiles))):
                    emit_qk(j0)
                for j in tiles:
                    if j + LA < len(tiles):
                        emit_qk(j + LA)
                    jj = max(0, j - 4 * g)
                    c0 = 128 * jj
                    ps_s = SB[j % 4]
                    p_ = pT[npt % 5]
                    npt += 1
                    kb.act(p_[:, c0:512], ps_s[:, c0:512], AF.Exp)
                    for tb in range(jj, 4):
                        kb.mm(Ob[:, tb * 128:tb * 128 + 65], p_[:, tb * 128:(tb + 1) * 128], V_[:, j, :],
                              start=(j == 0 and tb == 0), stop=(j == 4 * g + tb), skip_group_check=True)
                    if j == 1 and pending:
                        pending.pop(0)()

                def fin(g=g, Ob=Ob, kind=kind, m=m, yrow=yrow, gv=gv):
                    nonlocal nfin
                    for tb in range(4):
                        O = Ob[:, tb * 128:tb * 128 + 65]
                        ib = g * 4 + tb
                        s_ = sm[nfin % 4]
                        t_ = tmp[nfin % 4]
                        y_ = ybf[nfin % 4]
                        nfin += 1
                        kb.recip(s_[:, 0:1], O[:, 64:65])
                        if kind == "d" and m % 2 == 0:
                            kb.ts(o1[:, ib, :], O[:, 0:64], s_[:, 0:1], ALU.mult)
                            continue
                        kb.ts(t_[:], O[:, 0:64], s_[:, 0:1], ALU.mult)
                        if kind == "d":
                            kb.stt(t_[:], t_[:], neg_lam, o1[:, ib, :], ALU.mult, ALU.add)
                        kb.stt(junk[:], t_[:], 1.0, t_[:], ALU.mult, ALU.mult, accum=s_[:, 1:2])
                        kb.ts(s_[:, 2:3], s_[:, 1:2], 1.0 / 64, ALU.mult, 1e-6, ALU.add)
                        kb.tt(s_[:, 3:4], s_[:, 2:3], mhalfB[:, 0:1], ALU.pow, eng="pool")
                        kb.stt(y_[:], t_[:], s_[:, 3:4], gv[:], ALU.mult, ALU.mult)
                        kb.tr(c.psT[0:64, tb * 128:(tb + 1) * 128], y_[:], c.identb[:])
                    if not (kind == "d" and m % 2 == 0):
                        ys = yst[g % 2]
                        kb.cp(ys[:], c.psT[0:64, 0:512])
                        kb.dma(c.yT[yrow:yrow + 64, g * 512:(g + 1) * 512], ys[:])

                pending.append(fin)
                if len(tiles) < 2 or NG == 1:
                    while pending:
                        pending.pop(0)()
        while pending:
            pending.pop(0)()


def phase_out(c, l, nm, src, K, wsrc, gpost, last, grouped_src=False):
    kb, S, NG = c.kb, c.S, c.NG
    with kb.scope() as es:
        c.eps_t = kb.sb(nm + "_eps", [128, 1], F32, es)
        kb.memset(c.eps_t[:], 1e-6)
        gp = kb.sb(nm + "_gp", [128, 8], F32, es)
        kb.dma(gp[:], gpost[l, :].rearrange("(kc p) -> p kc", p=128), allow_slow_non_contiguous=True)
        W = load_w_bf16(c, es, nm + "_W", lambda kc, so, ww: wsrc[l, kc * 128:(kc + 1) * 128, so:so + ww], K,
                        [(0, 0, 1024, 1.0)])
        yg = [kb.sb(nm + "_yg%d" % i, [128, K, 512], BF16, es) for i in range(2)]
        xg = [kb.sb(nm + "_xg%d" % i, [128, 8, 512], F32, es) for i in range(2)]
        yos = [kb.sb(nm + "_yo%d" % i, [128, 8, 512], F32, es) for i in range(2)]
        sq = kb.sb(nm + "_sq", [128, 8, 512], BF16, es)
        rs = kb.sb(nm + "_rs", [128, 512], F32, es)
        tmpo = [kb.sb(nm + "_tmp%d" % i, [128, 512], F32, es) for i in range(2)]
        if last:
            ost = [kb.sb(nm + "_ost%d" % i, [128, 1024], F32, es) for i in range(2)]
        def mm_part(g):
            ts_ = slice(g * 512, (g + 1) * 512)
            y_, x_, yo = yg[g % 2], xg[g % 2], yos[g % 2]
            if grouped_src:
                kb.dma(y_[:], src[g, :, :, :])
            else:
                kb.dma(y_[:], src[:, ts_].rearrange("(kc p) t -> p kc t", p=128))
            kb.dma(x_[:], c.xT[:, ts_].rearrange("(kc p) t -> p kc t", p=128))
            for oc in range(8):
                ps = c.psum[1 + oc % 4]
                for kc in range(K):
                    kb.mm(ps[:, :], W[:, kc, oc * 128:(oc + 1) * 128], y_[:, kc, :], start=(kc == 0), stop=(kc == K - 1))
                kb.cp(yo[:, oc, :], ps[:, :], eng=("act", "dve")[oc % 2])

        def post_part(g):
            ts_ = slice(g * 512, (g + 1) * 512)
            x_, yo = xg[g % 2], yos[g % 2]
            rstd_bc(c, yo, sq, rs, c.psum[0])
            for oc in range(8):
                t_ = tmpo[oc % 2]
                kb.stt(t_[:], yo[:, oc, :], gp[:, oc:oc + 1], rs[:], ALU.mult, ALU.mult, eng="dve")
                kb.tt(x_[:, oc, :], x_[:, oc, :], t_[:], ALU.add, eng="pool")
            if not last:
                kb.dma(c.xT[:, ts_].rearrange("(kc p) t -> p kc t", p=128), x_[:])
            else:
                for tb in range(4):
                    o_ = ost[tb % 2]
                    for half in range(2):
                        ps = c.psum[5 + half]
                        for q in range(4):
                            oc = half * 4 + q
                            kb.tr(ps[:, q * 128:(q + 1) * 128], x_[:, oc, tb * 128:(tb + 1) * 128], c.identf[:])
                        kb.cp(o_[:, half * 512:(half + 1) * 512], ps[:, :], eng=("act", "dve")[half])
                    kb.dma(c.out[g * 512 + tb * 128:g * 512 + (tb + 1) * 128, :], o_[:])

        mm_part(0)
        for g in range(NG):
            if g + 1 < NG:
                mm_part(g + 1)
            post_part(g)


def phase_F1(c, l):
    kb, S, NG = c.kb, c.S, c.NG
    NF = DFF // 128
    with kb.scope() as es:
        c.eps_t = kb.sb("F_eps", [128, 1], F32, es)
        kb.memset(c.eps_t[:], 1e-6)
        g_t = kb.sb("F_g", [128, 8], F32, es)
        kb.dma(g_t[:], c.norm_ffn_pre[l, :].rearrange("(kc p) -> p kc", p=128), allow_slow_non_contiguous=True)
        cw = kb.sb("F_cw", [128, 3, NF], F32, es)
        cb = kb.sb("F_cb", [128, NF], F32, es)
        for k in range(3):
            kb.dma(cw[:, k, :], c.ffn_conv_w[l, k, :].rearrange("(fc p) -> p fc", p=128), allow_slow_non_contiguous=True)
        kb.dma(cb[:], c.ffn_conv_b[l, :].rearrange("(fc p) -> p fc", p=128), allow_slow_non_contiguous=True)
        Wg = load_w_bf16(c, es, "F_Wg", lambda kc, so, ww: c.ffn_w_gate[l, kc * 128:(kc + 1) * 128, so:so + ww], 8,
                         [(0, 0, DFF, 1.0)], gscale=g_t)
        Wu = load_w_bf16(c, es, "F_Wu", lambda kc, so, ww: c.ffn_w_up[l, kc * 128:(kc + 1) * 128, so:so + ww], 8,
                         [(0, 0, DFF, 1.0)], gscale=g_t)
        xg = [kb.sb("F_xg%d" % i, [128, 8, 512], F32, es) for i in range(2)]
        sq = kb.sb("F_sq", [128, 8, 512], BF16, es)
        rs = kb.sb("F_rs", [128, 512], F32, es)
        xn = [kb.sb("F_xn%d" % i, [128, 8, 512], BF16, es) for i in range(2)]
        halo = kb.sb("F_halo", [128, NF, 2], F32, es)
        kb.memset(halo[:], 0.0)
        gb = [kb.sb("F_gb%d" % i, [128, 514], F32, es) for i in range(3)]
        acc = [kb.sb("F_acc%d" % i, [128, 512], F32, es) for i in range(3)]
        t1 = [kb.sb("F_t1%d" % i, [128, 512], F32, es) for i in range(3)]
        t2 = [kb.sb("F_t2%d" % i, [128, 512], F32, es) for i in range(3)]
        hst = [kb.sb("F_hst%d" % i, [128, 512], BF16, es) for i in range(3)]
        n = 0
        for g in range(NG):
            ts_ = slice(g * 512, (g + 1) * 512)
            x_ = xg[g % 2]
            xn_ = xn[g % 2]
            def prologue(gg):
                xx_ = xg[gg % 2]
                kb.dma(xx_[:], c.xT[:, gg * 512:(gg + 1) * 512].rearrange("(kc p) t -> p kc t", p=128))
                rstd_bc(c, xx_, sq, rs, c.psum[0])
                for kc in range(8):
                    kb.tt(xn[gg % 2][:, kc, :], xx_[:, kc, :], rs[:], ALU.mult, eng=("dve", "pool")[kc % 2])
            if g == 0:
                prologue(0)

            def fc_gen(fc, g=g, ts_=ts_, xn_=xn_):
                nonlocal n
                pg = c.psum[1 + (fc % 3)]
                pu = c.psum[4 + (fc % 3)]
                fs = slice(fc * 128, (fc + 1) * 128)
                gb_, a_, t1_, t2_ = gb[n % 3], acc[n % 3], t1[n % 3], t2[n % 3]
                h_ = hst[n % 3]
                n += 1
                for kc in range(8):
                    kb.mm(pg[:, :], Wg[:, kc, fs], xn_[:, kc, :], start=(kc == 0), stop=(kc == 7))
                kb.cp(gb_[:, 0:2], halo[:, fc, :], eng="pool")
                yield
                for kc in range(8):
                    kb.mm(pu[:, :], Wu[:, kc, fs], xn_[:, kc, :], start=(kc == 0), stop=(kc == 7))
                kb.cp(gb_[:, 2:514], pg[:, :], eng="act")
                yield
                kb.cp(halo[:, fc, :], gb_[:, 512:514], eng="pool")
                kb.ts(a_[:], gb_[:, 2:514], cw[:, 2, fc:fc + 1], ALU.mult, cb[:, fc:fc + 1], ALU.add, eng="pool")
                yield
                kb.stt(a_[:], gb_[:, 1:513], cw[:, 1, fc:fc + 1], a_[:], ALU.mult, ALU.add, eng="dve")
                kb.stt(a_[:], gb_[:, 0:512], cw[:, 0, fc:fc + 1], a_[:], ALU.mult, ALU.add, eng="dve")
                yield
                kb.act(t2_[:], a_[:], AF.Gelu_apprx_tanh)
                yield
                kb.tt(h_[:], t2_[:], pu[:, :], ALU.mult, eng="dve")
                kb.dma(c.hmidT[g, :, fc, :], h_[:])

            def fcs(g=g):
                for fc in range(NF):
                    if fc == 8 and g + 1 < NG:
                        prologue(g + 1)
                    yield fc_gen(fc)
            run_pipelined(fcs(), 3)


def phase_C(c, l):
    kb, S, NG = c.kb, c.S, c.NG
    with kb.scope() as es:
        triu = kb.sb("C_triu", [128, 128], F32, es)
        strl = kb.sb("C_strl", [128, 128], F32, es)
        onesf = kb.sb("C_onesf", [128, 128], F32, es)
        kb.dma(triu[:], c.triu[:, :])
        kb.dma(strl[:], c.strl[:, :])
        kb.memset(onesf[:], 1.0)
        epsc = kb.sb("C_eps", [128, 1], F32, es)
        kb.memset(epsc[:], 1e-6)
        onec = kb.sb("C_one", [128, 1], F32, es)
        kb.memset(onec[:], 1.0)
        cw = kb.sb("C_cw", [128, 4, 6], F32, es)
        cbias = kb.sb("C_cb", [128, 6], F32, es)
        for k in range(4):
            kb.dma(cw[:, k, :], c.ssm_conv_w[l, k, :].rearrange("(cc p) -> p cc", p=128), allow_slow_non_contiguous=True)
        kb.dma(cbias[:], c.ssm_conv_b[l, :].rearrange("(cc p) -> p cc", p=128), allow_slow_non_contiguous=True)
        prm = kb.sb("C_prm", [128, 3, 4], F32, es)
        kb.dma(prm[:, 0, :], c.ssm_dt_bias[l:l + 1, :].partition_broadcast(128))
        kb.dma(prm[:, 1, :], c.ssm_a_log[l:l + 1, :].partition_broadcast(128))
        kb.dma(prm[:, 2, :], c.ssm_d[l:l + 1, :].partition_broadcast(128))
        kb.act(prm[:, 1, :], prm[:, 1, :], AF.Exp)
        kb.ts(prm[:, 1, :], prm[:, 1, :], -1.0, ALU.mult)
        gn = kb.sb("C_gn", [128, 256], F32, es)
        kb.dma(gn[:], c.ssm_norm[l:l + 1, :].partition_broadcast(128))
        St = kb.sb("C_St", [128, 4, 64], F32, es)
        kb.memset(St[:], 0.0)
        Stb = kb.sb("C_Stb", [128, 4, 64], BF16, es)
        cbuf = [kb.sb("C_cbuf%d" % i, [128, 515], F32, es) for i in range(2)]
        acc = [kb.sb("C_acc%d" % i, [128, 512], F32, es) for i in range(2)]
        ux = [kb.sb("C_ux%d" % i, [128, 2, 512], F32, es) for i in range(2)]
        uB = [kb.sb("C_uB%d" % i, [128, 2, 512], BF16, es) for i in range(2)]
        uC = [kb.sb("C_uC%d" % i, [128, 2, 512], BF16, es) for i in range(2)]
        yst = [kb.sb("C_yst%d" % i, [128, 2, 512], BF16, es) for i in range(2)]
        NR = 2

        def ring(nm, shape, dt):
            return [kb.sb("C_%s%d" % (nm, i), shape, dt, es) for i in range(NR)]

        dtt = ring("dtt", [128, 8], F32)
        sm = ring("sm", [128, 32], F32)
        xtok = ring("xtok", [128, 256], F32)
        Btok = ring("Btok", [128, 256], BF16)
        xd = ring("xd", [128, 256], BF16)
        xde = ring("xde", [128, 256], BF16)
        zt = ring("zt", [128, 256], F32)
        MS = [kb.sb("C_MS%d" % i, [128, 128], F32, es) for i in range(8)]
        E = [kb.sb("C_E%d" % i, [128, 128], F32, es) for i in range(8)]
        MT = [kb.sb("C_MT%d" % i, [128, 128], BF16, es) for i in range(8)]
        y1 = ring("y1", [128, 256], F32)
        yy = ring("yy", [128, 256], F32)
        junk = ring("junk", [128, 128], F32)
        ybf = ring("ybf", [128, 256], BF16)
        nch = 0
        for g in range(NG):
            ux_, uB_, uC_ = ux[g % 2], uB[g % 2], uC[g % 2]
            for cc in range(6):
                cb_ = cbuf[cc % 2]
                a_ = acc[cc % 2]
                rows = slice(cc * 128, (cc + 1) * 128)
                if g == 0:
                    kb.memset(cb_[:, 0:3], 0.0)
                    kb.dma(cb_[:, 3:515], c.xbc[rows, 0:512])
                else:
                    kb.dma(cb_[:, 0:515], c.xbc[rows, g * 512 - 3:(g + 1) * 512])
                kb.ts(a_[:], cb_[:, 3:515], cw[:, 3, cc:cc + 1], ALU.mult, cbias[:, cc:cc + 1], ALU.add, eng="pool")
                for k in (2, 1, 0):
                    kb.stt(a_[:], cb_[:, k:k + 512], cw[:, k, cc:cc + 1], a_[:], ALU.mult, ALU.add, eng="dve")
                if cc < 2:
                    kb.act(ux_[:, cc, :], a_[:], AF.Silu)
                elif cc < 4:
                    kb.act(uB_[:, cc - 2, :], a_[:], AF.Silu)
                else:
                    kb.act(uC_[:, cc - 4, :], a_[:], AF.Silu)
            ys_ = yst[g % 2]

            def prep_gen(j, g=g, ux_=ux_, uB_=uB_):
                i = (g * 4 + j) % NR
                tsl = slice(g * 512 + j * 128, g * 512 + (j + 1) * 128)
                cs_ = slice(j * 128, (j + 1) * 128)
                dtt_, sm_, xtok_, Btok_, xd_, xde_, zt_ = dtt[i], sm[i], xtok[i], Btok[i], xd[i], xde[i], zt[i]
                kb.dma(dtt_[:], c.dtffT[tsl, :])
                kb.dma(zt_[:], c.z[tsl, :])
                xr, ax, ee, dtv, dA = sm_[:, 0:4], sm_[:, 4:8], sm_[:, 8:12], sm_[:, 12:16], sm_[:, 16:20]
                ecs, dte, etot = sm_[:, 20:24], sm_[:, 24:28], sm_[:, 28:32]
                px = c.psum[1]
                for q in range(2):
                    kb.tr(px[:, q * 128:(q + 1) * 128], ux_[:, q, cs_], c.identf[:])
                for q in range(2):
                    kb.tr(c.psT[:, q * 128:(q + 1) * 128], uB_[:, q, cs_], c.identb[:])
                kb.tt(xr, dtt_[:, 0:4], prm[:, 0, :], ALU.add)
                kb.ts(ax, xr, -1.0, ALU.mult)
                kb.tt(ax, ax, xr, ALU.max)
                yield
                kb.cp(xtok_[:], px[:, 0:256], eng="act")
                kb.cp(Btok_[:], c.psT[:, 0:256], eng="act")
                kb.act(ee, ax, AF.Exp, scale=-1.0)
                kb.act(ee, ee, AF.Ln, bias=onec[:, 0:1], scale=1.0)
                kb.act(zt_[:], zt_[:], AF.Silu)
                yield
                kb.stt(dtv, xr, 0.0, ee, ALU.max, ALU.add)
                kb.tt(dA, dtv, prm[:, 1, :], ALU.mult)
                yield
                pcs = c.psum[0]
                kb.mm(pcs[:, 0:4], triu[:], dA, start=True, stop=True)
                kb.mm(pcs[:, 4:8], onesf[:], dA, start=True, stop=True)
                yield
                kb.act(ecs, pcs[:, 0:4], AF.Exp)
                kb.act(etot, pcs[:, 4:8], AF.Exp)
                kb.cp(dte, pcs[:, 0:4])
                kb.tt(dte, pcs[:, 4:8], dte, ALU.subtract)
                yield
                kb.act(dte, dte, AF.Exp)
                yield
                for h in range(4):
                    hs = slice(h * 64, (h + 1) * 64)
                    e1, e2 = ("pool", "dve") if h % 2 == 0 else ("dve", "pool")
                    kb.ts(xd_[:, hs], xtok_[:, hs], dtv[:, h:h + 1], ALU.mult, eng=e1)
                    kb.ts(xde_[:, hs], xtok_[:, hs], dtv[:, h:h + 1], ALU.mult, dte[:, h:h + 1], ALU.mult, eng=e2)
                    kb.ts(MS[i * 4 + h][:], strl[:], dA[:, h:h + 1], ALU.mult, eng=e1)
                    yield

            def main_gen(j, g=g, uB_=uB_, uC_=uC_, ys_=ys_):
                i = (g * 4 + j) % NR
                cs_ = slice(j * 128, (j + 1) * 128)
                dtt_, sm_, xtok_, Btok_, xd_, xde_, zt_ = dtt[i], sm[i], xtok[i], Btok[i], xd[i], xde[i], zt[i]
                dA, ecs, etot = sm_[:, 16:20], sm_[:, 20:24], sm_[:, 28:32]
                y1_, yy_ = y1[i], yy[i]
                pL, pG, pY, pS = c.psum[2], c.psum[3], c.psum[4], c.psum[5]
                kb.cp(Stb[:], St[:], eng="pool")
                for grp in range(2):
                    kb.mm(pG[:, grp * 128:(grp + 1) * 128], uB_[:, grp, cs_], uC_[:, grp, cs_], start=True, stop=True)

                def chead(h):
                    grp = h // 2
                    hs = slice(h * 64, (h + 1) * 64)
                    MS_, E_, MT_ = MS[i * 4 + h], E[i * 4 + h], MT[i * 4 + h]
                    kb.mm(pL[:, h * 128:(h + 1) * 128], MS_[:], triu[:], start=True, stop=True)
                    yield
                    kb.act(E_[:], pL[:, h * 128:(h + 1) * 128], AF.Exp)
                    yield
                    kb.tt(E_[:], E_[:], triu[:], ALU.mult, eng=("pool", "dve")[h % 2])
                    yield
                    kb.tt(MT_[:], pG[:, grp * 128:(grp + 1) * 128], E_[:], ALU.mult)
                    yield
                    kb.mm(pY[:, h * 128:h * 128 + 64], MT_[:], xd_[:, hs], start=True, stop=True)
                    kb.mm(pY[:, h * 128 + 64:h * 128 + 128], uC_[:, grp, cs_], Stb[:, h, :], start=True, stop=True)
                    kb.mm(pS[:, hs], Btok_[:, grp * 128:(grp + 1) * 128], xde_[:, hs], start=True, stop=True)
                    yield
                    kb.stt(y1_[:, hs], xtok_[:, hs], prm[:, 2, h:h + 1], pY[:, h * 128:h * 128 + 64], ALU.mult, ALU.add)
                    kb.stt(yy_[:, hs], pY[:, h * 128 + 64:h * 128 + 128], ecs[:, h:h + 1], y1_[:, hs], ALU.mult, ALU.add)
                    kb.stt(St[:, h, :], St[:, h, :], etot[:, h:h + 1], pS[:, hs], ALU.mult, ALU.add)

                pend = [chead(h) for h in range(4)]
                gens = []
                while gens or pend:
                    if pend:
                        gens.append(pend.pop(0))
                    for g_ in list(gens):
                        try:
                            next(g_)
                        except StopIteration:
                            gens.remove(g_)
                    yield
                kb.tt(yy_[:], yy_[:], zt_[:], ALU.mult, eng="pool")
                yield
                s2 = dtt_
                for q in range(2):
                    qs = slice(q * 128, (q + 1) * 128)
                    kb.stt(junk[i][:], yy_[:, qs], 1.0, yy_[:, qs], ALU.mult, ALU.mult, accum=s2[:, 4 + q:5 + q])
                yield
                kb.act(s2[:, 4:6], s2[:, 4:6], AF.Ln, bias=epsc[:, 0:1], scale=1.0 / 128)
                kb.act(s2[:, 6:8], s2[:, 4:6], AF.Exp, scale=-0.5)
                yield
                yb_ = ybf[i]
                for q in range(2):
                    qs = slice(q * 128, (q + 1) * 128)
                    kb.stt(yb_[:, qs], yy_[:, qs], s2[:, 6 + q:7 + q], gn[:, qs], ALU.mult, ALU.mult)
                yield
                for q in range(2):
                    kb.tr(c.psT[:, 512 + q * 128:512 + (q + 1) * 128], yb_[:, q * 128:(q + 1) * 128], c.identb[:])
                yield
                for q in range(2):
                    kb.cp(ys_[:, q, cs_], c.psT[:, 512 + q * 128:512 + (q + 1) * 128], eng="act")

            def drain(gs):
                gs = list(gs)
                while gs:
                    for g_ in list(gs):
                        try:
                            next(g_)
                        except StopIteration:
                            gs.remove(g_)

            drain([prep_gen(0)])
            for j in range(4):
                gs = [main_gen(j)]
                if j + 1 < 4:
                    gs.append(prep_gen(j + 1))
                drain(gs)
            kb.dma(c.yT[256:512, g * 512:(g + 1) * 512].rearrange("(q p) t -> p q t", p=128), ys_[:])


def phase_D(c, l):
    kb, S, NG = c.kb, c.S, c.NG
    with kb.scope() as es:
        def cst(nm, src):
            t = kb.sb("D_" + nm, [128, 128], F32, es)
            kb.dma(t[:], src[:, :])
            return t
        SU, SL, IU, bones = cst("SU", c.mSU), cst("SL", c.mSL), cst("IU", c.mIU), cst("bones", c.bones)
        rmask = kb.sb("D_rmask", [128, 512], F32, es)
        kb.dma(rmask[:], c.rmask[:, :])
        hsel = kb.sb("D_hsel", [128, 2], F32, es)
        kb.dma(hsel[:], c.hsel[:, :])
        onec = kb.sb("D_one", [128, 1], F32, es)
        kb.memset(onec[:], 1.0)
        mhalf = kb.sb("D_mhalf", [128, 1], F32, es)
        kb.memset(mhalf[:], -0.5)
        eps12 = kb.sb("D_eps12", [128, 1], F32, es)
        kb.memset(eps12[:], 1e-12)
        epsln = kb.sb("D_epsln", [128, 1], F32, es)
        kb.memset(epsln[:], 64e-5)
        mu = kb.sb("D_mu", [128, 7], F32, es)
        omm = kb.sb("D_omm", [128, 7], F32, es)
        kb.dma(mu[:], c.rwkv_mu[l, :].rearrange("(cc p) -> p cc", p=128), allow_slow_non_contiguous=True)
        kb.ts(omm[:], mu[:], -1.0, ALU.mult, 1.0, ALU.add)
        pp = kb.sb("D_pp", [128, 5, 2], F32, es)
        for i, t in enumerate((c.rwkv_w0, c.rwkv_a0, c.rwkv_k_k, c.rwkv_k_a)):
            kb.dma(pp[:, i, :], t[l, :].rearrange("(q p) -> p q", p=128), allow_slow_non_contiguous=True)
        kb.dma(pp[:, 4, :], c.rwkv_r_k[l, :, :].rearrange("(q hh) d -> (hh d) q", q=2), allow_slow_non_contiguous=True)
        kb.ts(pp[:, 0, :], pp[:, 0, :], -1.0, ALU.mult)
        w2t = kb.sb("D_w2", [32, 256], F32, es)
        kb.dma(w2t[:], c.rwkv_w2[l, :, :])
        a2t = kb.sb("D_a2", [64, 256], F32, es)
        kb.dma(a2t[32:64, :], c.rwkv_a2[l, :, :])
        g2t = kb.sb("D_g2", [128, 256], F32, es)
        kb.dma(g2t[64:128, :], c.rwkv_g2[l, :, :])
        lnw = kb.sb("D_lnw", [128, 256], F32, es)
        lnb = kb.sb("D_lnb", [128, 256], F32, es)
        kb.dma(lnw[:], c.rwkv_ln_w[l:l + 1, :].partition_broadcast(128))
        kb.dma(lnb[:], c.rwkv_ln_b[l:l + 1, :].partition_broadcast(128))
        Hs = [[kb.sb("D_H%d_%d" % (h, i), [128, 64], F32, es) for i in range(3)] for h in range(4)]
        for h in range(4):
            kb.memset(Hs[h][0][:], 0.0)
        hidx = [0, 0, 0, 0]

        def garr(nm, n=2, cols=512, dt=F32):
            return [kb.sb("D_%s%d" % (nm, i), [128, cols], dt, es) for i in range(n)]
        buf = garr("buf", 2, 513)
        m1 = garr("m1", 2)
        rr, kk_, vv = garr("rr"), garr("kk"), garr("vv")
        xx = kb.sb("D_xx", [128, 512], F32, es)
        txw = kb.sb("D_txw", [32, 512], F32, es)
        sxg = kb.sb("D_sxg", [128, 512], F32, es)
        aa, nlw, ncl = garr("aa"), garr("nlw"), garr("ncl")
        pc = garr("pc")
        kat, kt, bt, rt = garr("kat", dt=BF16), garr("kt", dt=BF16), garr("bt", dt=BF16), garr("rt", dt=BF16)
        t1, t2, prod = garr("t1"), garr("t2"), garr("prod")
        gtok = kb.sb("D_gtok", [128, 4, 256], F32, es)
        vtok = garr("vtok", 2, 256)
        vtokb = garr("vtokb", 2, 256, BF16)
        otok = garr("otok", 2, 256)
        rkt = garr("rkt", 2, 4)
        ymid = garr("ymid", 2, 256)
        ybf = garr("ybf", 2, 256, BF16)
        st6 = garr("st6", 2, 32)
        yst = [kb.sb("D_yst%d" % i, [128, 2, 512], BF16, es) for i in range(2)]
        NS = 4
        def slot(nm, cols=128, dt=BF16):
            return [kb.sb("D_s_%s%d" % (nm, i), [128, cols], dt, es) for i in range(NS)]
        BAa, BAb, IA, Xa, Xb = slot("BAa", 256), slot("BAb", 256), slot("IA"), slot("Xa"), slot("Xb")
        A3 = slot("A3", 384)
        mSUSL = kb.sb("D_mSUSL", [128, 256], F32, es)
        mSUIUIU = kb.sb("D_mSUIUIU", [128, 384], F32, es)
        kb.cp(mSUSL[:, 0:128], SU[:], eng="pool")
        kb.cp(mSUSL[:, 128:256], SL[:], eng="pool")
        kb.cp(mSUIUIU[:, 0:128], SU[:], eng="pool")
        kb.cp(mSUIUIU[:, 128:256], IU[:], eng="pool")
        kb.cp(mSUIUIU[:, 256:384], IU[:], eng="pool")
        RH, WU, nU0 = slot("RH"), slot("WU"), slot("nU0", 64)
        Wpad, btTp, ktTp, R1, R2 = slot("Wpad"), slot("btTp"), slot("ktTp"), slot("R1"), slot("R2")
        MpT, Npp = slot("MpT", dt=F32), slot("Npp", 64, dt=F32)
        WpadF, btTpF = slot("WpadF", dt=F32), slot("btTpF", dt=F32)
        Hb = slot("Hb", 64)
        for s_ in range(NS):
            for t_ in (Wpad, btTp, ktTp, R1, R2, WpadF, btTpF):
                kb.memset(t_[s_][:], 0.0)
        P0, P1, P2, P3, P4, P5, P6 = c.psum
        for g in range(NG):
            def load_mix(cc, dst):
                b_ = buf[cc % 2]
                rows = slice(cc * 128, (cc + 1) * 128)
                if g == 0:
                    kb.memset(b_[:, 0:1], 0.0)
                    kb.dma(b_[:, 1:513], c.rp[rows, 0:512])
                else:
                    kb.dma(b_[:, 0:513], c.rp[rows, g * 512 - 1:(g + 1) * 512])
                m_ = m1[cc % 2]
                kb.ts(m_[:], b_[:, 1:513], omm[:, cc:cc + 1], ALU.mult, eng="pool")
                kb.stt(dst, b_[:, 0:512], mu[:, cc:cc + 1], m_[:], ALU.mult, ALU.add)
            for q in range(2):
                load_mix(q, rr[q][:])
                load_mix(2 + q, kk_[q][:])
                load_mix(4 + q, vv[q][:])
            load_mix(6, xx[:])
            kb.act(txw[:], xx[0:32, :], AF.Tanh)
            kb.act(sxg[64:128, :], xx[64:128, :], AF.Sigmoid)
            for q in range(2):
                qs = slice(q * 128, (q + 1) * 128)
                kb.mm(P0[:, :], a2t[32:64, qs], xx[32:64, :], start=True, stop=True)
                kb.act(aa[q][:], P0[:, :], AF.Sigmoid, bias=pp[:, 1, q:q + 1], scale=1.0)
                kb.mm(P1[:, :], w2t[:, qs], txw[:], start=True, stop=True)
                kb.act(t1[q][:], P1[:, :], AF.Exp, bias=pp[:, 0, q:q + 1], scale=-1.0)
                kb.act(t1[q][:], t1[q][:], AF.Ln, bias=onec[:, 0:1], scale=1.0)
                kb.act(nlw[q][:], t1[q][:], AF.Exp, bias=mhalf[:, 0:1], scale=-1.0)
                kb.scan(ncl[q][:], rmask[:], nlw[q][:], 0.0, ALU.mult, ALU.add)
                kb.ts(t1[q][:], kk_[q][:], pp[:, 2, q:q + 1], ALU.mult, eng="pool")
                kb.tt(t2[q][:], t1[q][:], t1[q][:], ALU.mult, eng="pool")
                kb.mm(P2[:, :], bones[:], t2[q][:], start=True, stop=True)
                kb.act(t2[q][:], P2[:, :], AF.Ln, bias=eps12[:, 0:1], scale=1.0)
                kb.act(t2[q][:], t2[q][:], AF.Exp, scale=-0.5)
                kb.tt(t1[q][:], t1[q][:], t2[q][:], ALU.mult)
                kb.tt(t2[q][:], nlw[q][:], ncl[q][:], ALU.subtract, eng="pool")
                kb.act(t2[q][:], t2[q][:], AF.Exp)
                kb.tt(kat[q][:], t1[q][:], t2[q][:], ALU.mult)
                kb.act(t2[q][:], ncl[q][:], AF.Exp)
                kb.tt(t1[q][:], t1[q][:], aa[q][:], ALU.mult, eng="pool")
                kb.tt(bt[q][:], t1[q][:], t2[q][:], ALU.mult)
                kb.ts(t1[q][:], aa[q][:], -1.0, ALU.add, pp[:, 3, q:q + 1], ALU.mult, eng="pool")
                kb.stt(t1[q][:], t1[q][:], 1.0, kk_[q][:], ALU.add, ALU.mult)
                kb.tt(kt[q][:], t1[q][:], t2[q][:], ALU.mult)
                kb.stt(prod[q][:], rr[q][:], pp[:, 4, q:q + 1], t1[q][:], ALU.mult, ALU.mult)
                kb.act(pc[q][:], ncl[q][:], AF.Exp, scale=-1.0)
                kb.tt(rt[q][:], rr[q][:], pc[q][:], ALU.mult)
            ys_ = yst[g % 2]
            for j4 in range(4):
                pg_ = c.psum[j4]
                kb.mm(pg_[:, 0:256], sxg[64:128, j4 * 128:(j4 + 1) * 128], g2t[64:128, :], start=True, stop=True)
                kb.cp(gtok[:, j4, :], pg_[:, 0:256], eng=("act", "dve")[j4 % 2])

            def prep_gen(j):
                cs_ = slice(j * 128, (j + 1) * 128)
                ti = j % 2
                vt_, rk_, vb_ = vtok[ti], rkt[ti], vtokb[ti]
                for q in range(2):
                    kb.tr(P4[:, q * 128:(q + 1) * 128], vv[q][:, cs_], c.identf[:])
                for q in range(2):
                    kb.mm(P4[:, 256 + 2 * q:258 + 2 * q], prod[q][:, cs_], hsel[:], start=True, stop=True)
                yield
                kb.cp(vt_[:], P4[:, 0:256], eng="act")
                kb.cp(vb_[:], P4[:, 0:256], eng="dve")
                kb.cp(rk_[:], P4[:, 256:260], eng="act")
                yield

            def post_gen(j, ys_=ys_):
                cs_ = slice(j * 128, (j + 1) * 128)
                ti = j % 2
                vt_, ot_, rk_ = vtok[ti], otok[ti], rkt[ti]
                s6, ym, yb = st6[ti], ymid[ti], ybf[ti]
                for h in range(4):
                    hs = slice(h * 64, (h + 1) * 64)
                    kb.op("dve", lambda: c.nc.vector.bn_stats(s6[:, h * 6:h * 6 + 6].ap, ot_[:, hs].ap),
                          [ot_[:, hs]], [s6[:, h * 6:h * 6 + 6]])
                yield
                for h in range(4):
                    kb.op("dve", lambda: c.nc.vector.bn_aggr(s6[:, 24 + 2 * h:26 + 2 * h].ap, s6[:, h * 6:h * 6 + 6].ap),
                          [s6[:, h * 6:h * 6 + 6]], [s6[:, 24 + 2 * h:26 + 2 * h]])
                yield
                for h in range(4):
                    var = s6[:, 25 + 2 * h:26 + 2 * h]
                    kb.ts(var, var, 64e-5, ALU.add)
                    kb.tt(var, var, mhalf[:, 0:1], ALU.pow, eng="pool")
                yield
                for h in range(4):
                    hs = slice(h * 64, (h + 1) * 64)
                    var = s6[:, 25 + 2 * h:26 + 2 * h]
                    kb.ts(ym[:, hs], ot_[:, hs], s6[:, 24 + 2 * h:25 + 2 * h], ALU.subtract, var, ALU.mult)
                yield
                kb.tt(ym[:], ym[:], lnw[:], ALU.mult, eng="pool")
                kb.tt(ym[:], ym[:], lnb[:], ALU.add, eng="pool")
                yield
                for h in range(4):
                    hs = slice(h * 64, (h + 1) * 64)
                    kb.stt(ym[:, hs], vt_[:, hs], rk_[:, h:h + 1], ym[:, hs], ALU.mult, ALU.add)
                yield
                kb.tt(yb[:], ym[:], gtok[:, j, :], ALU.mult, eng="pool")
                yield
                for q in range(2):
                    kb.tr(c.psT[:, q * 128:(q + 1) * 128], yb[:, q * 128:(q + 1) * 128], c.identb[:])
                yield
                for q in range(2):
                    kb.cp(ys_[:, q, cs_], c.psT[:, q * 128:(q + 1) * 128], eng="act")

            for _ in prep_gen(0):
                pass
            for j in range(4):
                cs_ = slice(j * 128, (j + 1) * 128)
                ti = j % 2
                vb_, ot_ = vtokb[ti], otok[ti]
                def head_gen(h, j=j, cs_=cs_, vt_=vb_, ot_=ot_):
                    q, s_ = h // 2, h
                    hp = slice((h % 2) * 64, (h % 2) * 64 + 64)
                    hs = slice(h * 64, (h + 1) * 64)
                    kat_, kt_, bt_, rt_ = kat[q][hp, cs_], kt[q][hp, cs_], bt[q][hp, cs_], rt[q][hp, cs_]
                    V_ = vt_[:, hs]
                    PH = c.psum[h]
                    PY = (P6, P5)[h % 2]
                    PT = c.psT if h % 2 == 0 else c.psum[4][:, :].bitcast(BF16)
                    kb.mm(PH[:, 0:128], bt_, kat_, start=True, stop=True)
                    kb.mm(PH[:, 128:256], kat_, bt_, start=True, stop=True)
                    yield
                    kb.tt(BAa[s_][:], PH[:, 0:256], mSUSL[:], ALU.mult)
                    yield
                    kb.tt(Xa[s_][:], c.identf[:], BAa[s_][:, 0:128], ALU.subtract, eng="pool")
                    BAc, BAn, Xc, Xn = BAa[s_], BAb[s_], Xa[s_], Xb[s_]
                    PN = PH
                    for lev in range(5):
                        kb.mm(PN[:, 128:256], BAc[:, 0:128], BAc[:, 128:256], start=True, stop=True)
                        if lev < 4:
                            kb.mm(PN[:, 0:128], BAc[:, 128:256], BAc[:, 0:128], start=True, stop=True)
                        yield
                        kb.tt(IA[s_][:], PN[:, 128:256], c.identf[:], ALU.add)
                        if lev < 4:
                            kb.cp(BAn[:], PN[:, 0:256], eng="act")
                        yield
                        kb.mm(PN[:, 256:384], IA[s_][:], Xc[:], start=True, stop=True)
                        yield
                        kb.cp(Xn[:], PN[:, 256:384], eng="dve")
                        BAc, BAn = BAn, BAc
                        Xc, Xn = Xn, Xc
                        yield
                    TT = Xc
                    kb.mm(PH[:, 0:128], kt_, kat_, start=True, stop=True)
                    kb.mm(PH[:, 128:256], bt_, rt_, start=True, stop=True)
                    kb.mm(PH[:, 256:384], kt_, rt_, start=True, stop=True)
                    yield
                    kb.tt(A3[s_][:], PH[:, 0:384], mSUIUIU[:], ALU.mult)
                    yield
                    idb = c.identb[hp, hp]
                    TB = (256 + h * 192) if h % 2 == 0 else (520 + (h // 2) * 192)
                    kb.tr(PT[:, TB:TB + 64], kat_, idb)
                    kb.tr(PT[:, TB + 64:TB + 128], bt_, idb)
                    kb.tr(PT[:, TB + 128:TB + 192], kt_, idb)
                    yield
                    kb.cp(RH[s_][:, 0:64], PT[:, TB:TB + 64], eng="act")
                    kb.cp(btTp[s_][:, hp], PT[:, TB + 64:TB + 128], eng="act")
                    kb.cp(btTpF[s_][:, hp], PT[:, TB + 64:TB + 128], eng="dve")
                    kb.cp(ktTp[s_][:, hp], PT[:, TB + 128:TB + 192], eng="act")
                    yield
                    kb.mm(PH[:, 256:320], A3[s_][:, 0:128], V_, start=True, stop=True)
                    yield
                    kb.cp(RH[s_][:, 64:128], PH[:, 256:320], eng="dve")
                    yield
                    kb.mm(PH[:, 0:128], TT[:], RH[s_][:], start=True, stop=True)
                    yield
                    kb.cp(Wpad[s_][:, hp], PH[:, 0:64], eng="dve")
                    kb.cp(WpadF[s_][:, hp], PH[:, 0:64], eng="dve")
                    kb.ts(nU0[s_][:], PH[:, 64:128], -1.0, ALU.mult)
                    yield
                    kb.mm(PH[:, 128:256], Wpad[s_][:], A3[s_][:, 128:256], start=True, stop=True)
                    yield
                    kb.tt(R1[s_][hp, 0:64], rt_[:, 0:64], PH[hp, 128:192], ALU.subtract)
                    kb.tt(R2[s_][hp, 64:128], rt_[:, 64:128], PH[hp, 192:256], ALU.subtract)
                    yield
                    kb.mm(PY[:, hs], A3[s_][:, 256:384], V_, start=(h < 2), stop=False, skip_group_check=True)
                    kb.mm(PY[:, hs], A3[s_][:, 128:256], nU0[s_][:], start=False, stop=False, skip_group_check=True)
                    yield
                    for ci in range(2):
                        cr = slice(ci * 64, ci * 64 + 64)
                        Hc = Hs[h][hidx[h] % 3]
                        Hn = Hs[h][(hidx[h] + 1) % 3]
                        hidx[h] += 1
                        Rm = (R1, R2)[ci][s_]
                        kb.cp(Hb[s_][hp, :], Hc[hp, :], eng="pool")
                        kb.mm(PY[:, hs], Rm[hp, :], Hb[s_][hp, :], start=False, stop=(ci == 1), skip_group_check=True)
                        kb.mm(PH[:, 256:320], ktTp[s_][cr, :], V_[cr, :], start=True, stop=False)
                        kb.mm(PH[:, 256:320], btTp[s_][cr, :], nU0[s_][cr, :], start=False, stop=True)
                        kb.mm(PH[:, 384:512], WpadF[s_][cr, :], btTpF[s_][cr, :], start=True, stop=True)
                        yield
                        PC = pc[q][hp, j * 128 + ci * 64 + 63:j * 128 + ci * 64 + 64]
                        kb.ts(Npp[s_][hp, :], PH[hp, 256:320], PC, ALU.mult)
                        kb.tt(MpT[s_][hp, :], c.identf[hp, :], PH[hp, 384:512], ALU.subtract)
                        yield
                        kb.mm(PH[:, 0:64], MpT[s_][hp, :], Hc[hp, :], start=True, stop=True)
                        yield
                        kb.stt(Hn[hp, :], PH[hp, 0:64], PC, Npp[s_][hp, :], ALU.mult, ALU.add)
                        yield
                    kb.cp(ot_[:, hs], PY[:, hs], eng="act")
                def pp_gen(j=j):
                    if j > 0:
                        yield from post_gen(j - 1)
                    if j + 1 < 4:
                        yield from prep_gen(j + 1)

                def allg(j=j):
                    yield pp_gen()
                    for h in range(4):
                        yield head_gen(h)
                run_pipelined(allg(), 5)
            for _ in post_gen(3):
                pass
            kb.dma(c.yT[768:1024, g * 512:(g + 1) * 512].rearrange("(q p) t -> p q t", p=128), ys_[:])


def consts(S):
    bf = ml_dtypes.bfloat16
    d = {}
    d["ident"] = np.eye(128, dtype=np.float32)
    s = np.arange(128)
    d["maskneg"] = np.where(s[:, None] > s[None, :], -30000.0, 0.0).astype(bf)
    d["triu"] = (s[:, None] <= s[None, :]).astype(np.float32)
    d["strl"] = (s[:, None] > s[None, :]).astype(np.float32)
    same = (s[:, None] // 64) == (s[None, :] // 64)
    d["mSU"] = ((s[:, None] < s[None, :]) & same).astype(np.float32)
    d["mSL"] = ((s[:, None] > s[None, :]) & same).astype(np.float32)
    d["mIU"] = ((s[:, None] <= s[None, :]) & same).astype(np.float32)
    d["bones"] = same.astype(np.float32)
    d["rmask"] = np.tile((np.arange(512) % 64 != 0).astype(np.float32)[None, :], (128, 1))
    d["hsel"] = np.stack([(s < 64), (s >= 64)], 1).astype(np.float32)
    pos = np.arange(S)
    hi = ((pos // 64) * 64).astype(np.float32)
    lo = (pos % 64).astype(np.float32)
    ak = np.zeros((4, 4, S), np.float32)
    aq = np.zeros((4, 4, S), np.float32)
    for h in range(4):
        sl = 2.0 ** (-8.0 * (h + 1) / 4)
        ak[h, 0] = sl * hi; ak[h, 1] = sl * lo; ak[h, 2] = 1; ak[h, 3] = 1
        aq[h, 0] = 1; aq[h, 1] = 1; aq[h, 2] = -sl * hi; aq[h, 3] = -sl * lo
    d["alibi_k"] = ak.astype(bf); d["alibi_q"] = aq.astype(bf)
    return d


SEQ = 8192
DEPTH = 2
_CACHE = {}


def kernel(**inputs):
    S, L = SEQ, DEPTH
    if "nc" not in _CACHE:
        _CACHE["nc"] = build(S, L)
        _CACHE["consts"] = consts(S)
    nc = _CACHE["nc"]
    x = np.ascontiguousarray(np.asarray(inputs["x"], dtype=np.float32))
    B = x.shape[0]
    shared = {k: np.ascontiguousarray(np.asarray(v, dtype=np.float32)) for k, v in inputs.items() if k != "x"}
    shared.update(_CACHE["consts"])
    in_maps = []
    for b in range(B):
        m = dict(shared)
        m["x"] = x[b]
        in_maps.append(m)
    res = run_bass_kernel_spmd(nc, in_maps, core_ids=list(range(B)))
    return np.stack([np.asarray(r["out"]) for r in res.results], axis=0).astype(np.float32)
```

```python
import contextlib, math
import numpy as np
import ml_dtypes
from concourse.bass_utils import run_bass_kernel_spmd
import contextlib
import numpy as np
import concourse.bass as bass
import concourse.mybir as mybir

F32 = mybir.dt.float32
BF16 = mybir.dt.bfloat16
AF = mybir.ActivationFunctionType
ALU = mybir.AluOpType
AX = mybir.AxisListType

SAME_ENGINE_RAW_SYNC = True


class Reg:
    __slots__ = ("w", "r")

    def __init__(self):
        self.w = None
        self.r = {}


class V:
    __slots__ = ("t", "ap", "key")

    def __init__(self, t, ap, key):
        self.t = t
        self.ap = ap
        self.key = key

    def rearrange(self, pat, **kw):
        return V(self.t, self.ap.rearrange(pat, **kw), self.key)

    def partition_broadcast(self, n):
        return V(self.t, self.ap.partition_broadcast(n), self.key)

    def bitcast(self, dt):
        return V(self.t, self.ap.bitcast(dt), self.key)

    def __getitem__(self, idx):
        return V(self.t, self.ap[idx], self.key)


class _Keyed:
    def __init__(self, t, key):
        self.t = t
        self.key = key

    def __getitem__(self, idx):
        return V(self.t, self.t.h[idx], self.key)


class T:
    def __init__(self, h, name, excl=False):
        self.h = h
        self.name = name
        self.excl = excl
        self.regs = {None: Reg()}

    def __getitem__(self, idx):
        return V(self, self.h[idx], None)

    def k(self, key):
        return _Keyed(self, key)

    def regs_of(self, key):
        if key is None:
            return list(self.regs.values())
        if key not in self.regs:
            r = Reg()
            self.regs[key] = r
        return [self.regs[None], self.regs[key]]


class KB:
    COMPUTE = ("pe", "act", "dve", "pool")

    def __init__(self, nc, es, n_dma_sems=40):
        self.nc = nc
        self.es = es
        self.engs = {"pe": nc.tensor, "act": nc.scalar, "dve": nc.vector, "pool": nc.gpsimd, "sp": nc.sync}
        self.sem = {}
        self.tick = {}
        for e in self.COMPUTE:
            self.sem[e] = es.enter_context(nc.semaphore("sem_" + e))
            self.tick[e] = 0
        self.waited = {}
        self.dma_sems = []
        for i in range(n_dma_sems):
            self.dma_sems.append([es.enter_context(nc.semaphore("dsem%d" % i)), 0])
        self.dma_next = 0
        self.n_inst = 0
        self.n_wait = 0

    def sb(self, name, shape, dtype, es=None):
        es = es or self.es
        self._uid = getattr(self, "_uid", 0) + 1
        name = "%s_u%d" % (name, self._uid)
        return T(es.enter_context(self.nc.sbuf_tensor(name, list(shape), dtype)), name)

    def ps(self, name, shape, dtype, es=None):
        es = es or self.es
        return T(es.enter_context(self.nc.psum_tensor(name, list(shape), dtype)), name, excl=True)

    def dram(self, name, shape, dtype, kind="Internal"):
        return T(self.nc.dram_tensor(name, list(shape), dtype, kind=kind).ap(), name)

    def _collect(self, reads, writes):
        deps = {}

        def add(tok, raw):
            if tok is None:
                return
            k = tok[3]
            old = deps.get(k)
            if old is None or old[1] < tok[1]:
                deps[k] = (tok[0], tok[1], tok[2], old[3] or raw if old else raw)
            elif raw and not old[3]:
                deps[k] = (old[0], old[1], old[2], True)

        for v in reads:
            for reg in v.t.regs_of(v.key):
                add(reg.w, True)
        for v in writes:
            for reg in v.t.regs_of(v.key):
                add(reg.w, False)
                for tok in reg.r.values():
                    add(tok, False)
        return deps

    def _waits(self, eng, deps):
        for k, (sem, val, src, raw) in deps.items():
            if src == eng:
                if eng == "pe":
                    continue
                if not SAME_ENGINE_RAW_SYNC:
                    continue
            wk = (eng, k)
            if self.waited.get(wk, 0) >= val:
                continue
            self.engs[eng].wait_ge(sem, val)
            self.n_wait += 1
            self.waited[wk] = val

    def _update(self, tok, eng_key, reads, writes):
        for v in writes:
            if v.key is None:
                for reg in v.t.regs.values():
                    reg.w = tok
                    reg.r = {}
            else:
                regs = v.t.regs_of(v.key)
                regs[1].w = tok
                regs[1].r = {}
        for v in reads:
            if v.key is None:
                v.t.regs[None].r[eng_key] = tok
            else:
                v.t.regs_of(v.key)[1].r[eng_key] = tok

    def op(self, eng, fn, reads, writes):
        xr = [v for v in reads if v.t.excl]
        if xr:
            writes = list(writes) + [V(v.t, v.ap, None) for v in xr]
        deps = self._collect(reads, writes)
        self._waits(eng, deps)
        inst = fn()
        self.tick[eng] += 1
        inst.then_inc(self.sem[eng], 1)
        tok = (self.sem[eng], self.tick[eng], eng, eng)
        self._update(tok, eng, reads, writes)
        self.n_inst += 1
        return inst

    def dma(self, out, in_, q="sp", **kw):
        slot = self.dma_sems[self.dma_next % len(self.dma_sems)]
        self.dma_next += 1
        sem, cnt = slot
        kid = "d%d" % id(sem)
        if cnt > 0:
            wk = (q, kid)
            if self.waited.get(wk, 0) < cnt:
                self.engs[q].wait_ge(sem, cnt)
                self.waited[wk] = cnt
        reads, writes = [in_], [out]
        deps = self._collect(reads, writes)
        for k, (s, val, src, raw) in deps.items():
            wk = (q, k)
            if src == q and not k.startswith("d"):
                pass
            if self.waited.get(wk, 0) >= val:
                continue
            self.engs[q].wait_ge(s, val)
            self.n_wait += 1
            self.waited[wk] = val
        inst = self.engs[q].dma_start(out=out.ap, in_=in_.ap, **kw)
        inst.then_inc(sem, 16)
        slot[1] = cnt + 16
        tok = (sem, cnt + 16, "dma", kid)
        self._update(tok, kid, reads, writes)
        self.n_inst += 1
        return inst

    def barrier(self):
        for e in self.engs:
            for s in self.COMPUTE:
                if s == e or self.tick[s] == 0:
                    continue
                wk = (e, s)
                if self.waited.get(wk, 0) < self.tick[s]:
                    self.engs[e].wait_ge(self.sem[s], self.tick[s])
                    self.waited[wk] = self.tick[s]
            for sem, cnt in self.dma_sems:
                if cnt == 0:
                    continue
                wk = (e, "d%d" % id(sem))
                if self.waited.get(wk, 0) < cnt:
                    self.engs[e].wait_ge(sem, cnt)
                    self.waited[wk] = cnt

    def finish(self):
        self.barrier()

    @contextlib.contextmanager
    def scope(self):
        with contextlib.ExitStack() as es:
            yield es
            self.barrier()

    def mm(self, out, lhsT, rhs, start=True, stop=True, **kw):
        return self.op("pe", lambda: self.nc.tensor.matmul(out.ap, lhsT.ap, rhs.ap, start=start, stop=stop, **kw),
                       [lhsT, rhs], [out])

    def tr(self, out, in_, ident):
        return self.op("pe", lambda: self.nc.tensor.transpose(out.ap, in_.ap, ident.ap), [in_, ident], [out])

    def act(self, out, in_, func, bias=None, scale=None, accum=None, extra_reads=()):
        kw = {}
        reads = [in_] + list(extra_reads)
        if bias is not None:
            if isinstance(bias, V):
                kw["bias"] = bias.ap
                reads.append(bias)
            else:
                kw["bias"] = bias
        if scale is not None:
            if isinstance(scale, V):
                kw["scale"] = scale.ap
                reads.append(scale)
            else:
                kw["scale"] = scale
        writes = [out]
        if accum is not None:
            kw["accum_out"] = accum.ap
            writes.append(accum)
        return self.op("act", lambda: self.nc.scalar.activation(out=out.ap, in_=in_.ap, func=func, **kw), reads, writes)

    def _e(self, eng):
        return self.engs[eng]

    def tt(self, out, a, b, op, eng="dve"):
        return self.op(eng, lambda: self._e(eng).tensor_tensor(out.ap, a.ap, b.ap, op), [a, b], [out])

    def ts(self, out, a, s1, op0, s2=None, op1=None, eng="dve", accum=None):
        reads = [a]
        sa1 = s1
        if isinstance(s1, V):
            reads.append(s1)
            sa1 = s1.ap
        sa2 = s2
        if isinstance(s2, V):
            reads.append(s2)
            sa2 = s2.ap
        writes = [out]
        kw = {}
        if op1 is not None:
            kw["op1"] = op1
        if accum is not None:
            kw["accum_out"] = accum.ap
            writes.append(accum)
        return self.op(eng, lambda: self._e(eng).tensor_scalar(out.ap, a.ap, sa1, sa2, op0, **kw), reads, writes)

    def stt(self, out, a, s, b, op0, op1, eng="dve", accum=None):
        assert eng == "dve"
        reads = [a, b]
        sa = s
        if isinstance(s, V):
            reads.append(s)
            sa = s.ap
        writes = [out]
        kw = {}
        if accum is not None:
            kw["accum_out"] = accum.ap
            writes.append(accum)
        return self.op(eng, lambda: self._e(eng).scalar_tensor_tensor(out.ap, a.ap, sa, b.ap, op0, op1, **kw), reads, writes)

    def cp(self, out, in_, eng="dve"):
        if eng == "act":
            return self.op("act", lambda: self.nc.scalar.copy(out.ap, in_.ap), [in_], [out])
        return self.op(eng, lambda: self._e(eng).tensor_copy(out.ap, in_.ap), [in_], [out])

    def scan(self, out, d0, d1, init, op0, op1):
        reads = [d0, d1]
        ia = init
        if isinstance(init, V):
            reads.append(init)
            ia = init.ap
        return self.op("dve", lambda: self.nc.vector.tensor_tensor_scan(out.ap, d0.ap, d1.ap, ia, op0, op1), reads, [out])

    def red(self, out, in_, op, axis=AX.X, eng="dve"):
        return self.op(eng, lambda: self._e(eng).tensor_reduce(out.ap, in_.ap, axis, op), [in_], [out])

    def recip(self, out, in_):
        return self.op("dve", lambda: self.nc.vector.reciprocal(out.ap, in_.ap), [in_], [out])

    def memset(self, out, val, eng="dve"):
        return self.op(eng, lambda: self._e(eng).memset(out.ap, val), [], [out])


D = 1024
DIN = 3464
DFF = 2816
SEGS = [("dq", 0, 256), ("dk", 256, 256), ("dv", 512, 256), ("sz", 768, 256), ("sxbc", 1024, 768),
        ("fq", 1796, 256), ("fk", 2052, 256), ("fv", 2308, 256), ("rp", 2568, 896), ("sdt", 1792, 4), ("ff", 2564, 4)]
SOFF = {}
_o = 0
for _n, _c, _w in SEGS:
    SOFF[_n] = _o
    _o += _w
assert _o == DIN


class Ctx:
    pass


def run_pipelined(gen_iter, depth):
    active = []
    it = iter(gen_iter)
    done = False
    while True:
        if not done and len(active) < depth:
            try:
                active.append(next(it))
            except StopIteration:
                done = True
        if not active:
            if done:
                break
            continue
        for g in list(active):
            try:
                next(g)
            except StopIteration:
                active.remove(g)


def build(S, L, dbg=(), ext_in=()):
    nc = bass.Bass("TRN2", target_bir_lowering=False)
    es = contextlib.ExitStack()
    kb = KB(nc, es)
    _dram = kb.dram
    kb.dram = lambda name, shape, dt: _dram(name, shape, dt, kind=("ExternalOutput" if name in dbg else ("ExternalInput" if name in ext_in else "Internal")))
    c = Ctx()
    c.nc, c.kb, c.S, c.L = nc, kb, S, L
    NG = S // 512
    c.NG = NG
    def ext(name, shape, dt=F32):
        return T(nc.dram_tensor(name, list(shape), dt, kind="ExternalInput").ap(), name)

    c.x = ext("x", [S, D])
    c.w_in = ext("w_in", [L, D, DIN])
    c.norm_mix_pre = ext("norm_mix_pre", [L, D])
    c.ident = ext("ident", [128, 128])
    c.out = T(nc.dram_tensor("out", [S, D], F32, kind="ExternalOutput").ap(), "out")
    c.xT = kb.dram("xT", [D, S], F32)
    c.qd = kb.dram("qd", [256, S], BF16)
    c.kd = kb.dram("kd", [256, S], BF16)
    c.fq = kb.dram("fq", [256, S], BF16)
    c.fk = kb.dram("fk", [256, S], BF16)
    c.vd = kb.dram("vd", [S, 256], BF16)
    c.fv = kb.dram("fv", [S, 256], BF16)
    c.z = kb.dram("z", [S, 256], F32)
    c.xbc = kb.dram("xbc", [768, S], F32)
    c.rp = kb.dram("rp", [896, S], F32)
    c.dtffT = kb.dram("dtffT", [S, 8], F32)
    c.dtffF = kb.dram("dtffF", [8, S], F32)
    c.yT = kb.dram("yT", [D, S], BF16)
    c.identf = kb.sb("identf", [128, 128], F32)
    c.identb = kb.sb("identb", [128, 128], BF16)
    c.onesb = kb.sb("onesb", [128, 128], BF16)
    kb.dma(c.identf[:], c.ident[:, :])
    kb.cp(c.identb[:], c.identf[:])
    kb.memset(c.onesb[:], 1.0)
    c.psum = [kb.ps("psum%d" % i, [128, 512], F32) for i in range(7)]
    c.psT = kb.ps("psumT", [128, 1024], BF16)
    c.hmidT = kb.dram("hmidT", [S // 512, 128, DFF // 128, 512], BF16)
    c.triu = ext("triu", [128, 128])
    c.mSU = ext("mSU", [128, 128]); c.mSL = ext("mSL", [128, 128]); c.mIU = ext("mIU", [128, 128])
    c.bones = ext("bones", [128, 128]); c.rmask = ext("rmask", [128, 512]); c.hsel = ext("hsel", [128, 2])
    for nm, shp in (("rwkv_mu", [L, 896]), ("rwkv_w0", [L, 256]), ("rwkv_w2", [L, 32, 256]), ("rwkv_a0", [L, 256]),
                    ("rwkv_a2", [L, 32, 256]), ("rwkv_g2", [L, 64, 256]), ("rwkv_k_k", [L, 256]), ("rwkv_k_a", [L, 256]),
                    ("rwkv_r_k", [L, 4, 64]), ("rwkv_ln_w", [L, 256]), ("rwkv_ln_b", [L, 256])):
        setattr(c, nm, ext(nm, shp))
    c.strl = ext("strl", [128, 128])
    for nm, shp in (("ssm_conv_w", [L, 4, 768]), ("ssm_conv_b", [L, 768]), ("ssm_dt_bias", [L, 4]), ("ssm_a_log", [L, 4]),
                    ("ssm_d", [L, 4]), ("ssm_norm", [L, 256])):
        setattr(c, nm, ext(nm, shp))
    for nm, shp in (("norm_mix_post", [L, D]), ("norm_ffn_pre", [L, D]), ("norm_ffn_post", [L, D]),
                    ("w_out", [L, D, D]), ("ffn_w_gate", [L, D, DFF]), ("ffn_w_up", [L, D, DFF]),
                    ("ffn_conv_w", [L, 3, DFF]), ("ffn_conv_b", [L, DFF]), ("ffn_w_down", [L, DFF, D])):
        setattr(c, nm, ext(nm, shp))
    c.maskneg = ext("maskneg", [128, 128], BF16)
    c.alibi_k = ext("alibi_k", [4, 4, S], BF16)
    c.alibi_q = ext("alibi_q", [4, 4, S], BF16)
    for nm, shp in (("diff_lambda_q1", [L, 32]), ("diff_lambda_k1", [L, 32]), ("diff_lambda_q2", [L, 32]),
                    ("diff_lambda_k2", [L, 32]), ("diff_subln", [L, 64]), ("fox_f_bias", [L, 4]), ("fox_norm", [L, 64])):
        setattr(c, nm, ext(nm, shp))

    ph = "P1,A,B,C,D,E,F"
    if "P1" in ph:
        phase_P1(c)
    for l in range(L):
        if "A" in ph:
            phase_A(c, l)
        if "B" in ph:
            phase_B(c, l)
        if "C" in ph:
            phase_C(c, l)
        if "D" in ph:
            phase_D(c, l)
        if "E" in ph:
            phase_out(c, l, "E", c.yT, 8, c.w_out, c.norm_mix_post, False)
        if "F" in ph:
            phase_F1(c, l)
            phase_out(c, l, "G", c.hmidT, DFF // 128, c.ffn_w_down, c.norm_ffn_post, l == L - 1, grouped_src=True)
    kb.finish()
    es.close()
    print("n_inst", kb.n_inst, "n_wait", kb.n_wait)
    return nc


def phase_P1(c):
    kb, S = c.kb, c.S
    with kb.scope() as es:
        xin = [kb.sb("p1_xin%d" % i, [128, 4, D], F32, es) for i in range(2)]
        stg = [kb.sb("p1_stg%d" % i, [128, 8, 512], F32, es) for i in range(2)]
        for g in range(c.NG):
            xi = xin[g % 2]
            st = stg[g % 2]
            kb.dma(xi[:], c.x[g * 512:(g + 1) * 512, :].rearrange("(j p) d -> p j d", p=128))
            for kc in range(8):
                ps = c.psum[kc % 4]
                for j in range(4):
                    kb.tr(ps[:, j * 128:(j + 1) * 128], xi[:, j, kc * 128:(kc + 1) * 128], c.identf[:])
                if kc % 2 == 0:
                    kb.cp(st[:, kc, :], ps[:, :], eng="act")
                else:
                    kb.cp(st[:, kc, :], ps[:, :], eng="dve")
            kb.dma(c.xT[:, g * 512:(g + 1) * 512].rearrange("(kc p) t -> p kc t", p=128), st[:])


def load_w_bf16(c, es_, name, w_dram_rows, K, cols_plan, gscale=None, stage_cols=1024):
    kb = c.kb
    ncols = sum(p[2] for p in cols_plan)
    W = kb.sb(name, [128, K, ncols], BF16, es_)
    with kb.scope() as es:
        stg = [kb.sb(name + "_stg%d" % i, [128, stage_cols], F32, es) for i in range(3)]
        n = 0
        for kc in range(K):
            for (dst, src, w, mult) in cols_plan:
                for o in range(0, w, stage_cols):
                    ww = min(stage_cols, w - o)
                    st = stg[n % 3]
                    kb.dma(st[:, 0:ww], w_dram_rows(kc, src + o, ww))
                    use_act = (n % 2 == 1) and float(mult) == 1.0
                    if use_act:
                        if gscale is not None:
                            kb.act(W[:, kc, dst + o:dst + o + ww], st[:, 0:ww], AF.Copy, scale=gscale[:, kc:kc + 1])
                        else:
                            kb.cp(W[:, kc, dst + o:dst + o + ww], st[:, 0:ww], eng="act")
                    elif gscale is not None:
                        kb.ts(W[:, kc, dst + o:dst + o + ww], st[:, 0:ww], gscale[:, kc:kc + 1], ALU.mult,
                              float(mult), ALU.mult, eng="dve")
                    else:
                        kb.ts(W[:, kc, dst + o:dst + o + ww], st[:, 0:ww], float(mult), ALU.mult, eng="dve")
                    n += 1
    return W


def rstd_bc(c, xg, sq, rs, ps):
    kb = c.kb
    for kc in range(8):
        kb.act(sq[:, kc, :], xg[:, kc, :], AF.Square)
    for kc in range(8):
        kb.mm(ps[:, :], c.onesb[:], sq[:, kc, :], start=(kc == 0), stop=(kc == 7))
    kb.act(rs[:], ps[:, :], AF.Sqrt, bias=c.eps_t[:, 0:1], scale=1.0 / D)
    kb.recip(rs[:], rs[:])


def phase_A(c, l):
    kb, S = c.kb, c.S
    with kb.scope() as es:
        g_t = kb.sb("A_g", [128, 8], F32, es)
        c.eps_t = kb.sb("A_eps", [128, 1], F32, es)
        kb.memset(c.eps_t[:], 1e-6)
        kb.dma(g_t[:], c.norm_mix_pre[l, :].rearrange("(kc p) -> p kc", p=128), allow_slow_non_contiguous=True)
        plan = []
        for n_, oc, w in SEGS:
            mult = 32 ** -0.5 if n_ == "dq" else (64 ** -0.5 if n_ == "fq" else 1.0)
            plan.append((SOFF[n_], oc, w, mult))
        W = load_w_bf16(c, es, "A_W", lambda kc, so, ww: c.w_in[l, kc * 128:(kc + 1) * 128, so:so + ww], 8, plan,
                        gscale=g_t)
        ASTOP = 99
        if ASTOP <= 1:
            return
        xg = [kb.sb("A_xg%d" % i, [128, 8, 512], F32, es) for i in range(2)]
        sq = kb.sb("A_sq", [128, 8, 512], BF16, es)
        rs = kb.sb("A_rs", [128, 512], F32, es)
        xn = [kb.sb("A_xn%d" % i, [128, 8, 512], BF16, es) for i in range(2)]
        stF = [kb.sb("A_stF%d" % i, [128, 512], F32, es) for i in range(3)]
        stB = [kb.sb("A_stB%d" % i, [128, 512], BF16, es) for i in range(3)]
        stT = [kb.sb("A_stT%d" % i, [128, 1024], F32, es) for i in range(2)]
        stTb = [kb.sb("A_stTb%d" % i, [128, 512], BF16, es) for i in range(2)]
        fch = []
        for n_, dst, bf in (("dq", c.qd, True), ("dk", c.kd, True), ("fq", c.fq, True), ("fk", c.fk, True),
                            ("sxbc", c.xbc, False), ("rp", c.rp, False)):
            w = dict((a, cc) for a, b, cc in SEGS)[n_]
            for o in range(0, w, 128):
                fch.append((SOFF[n_] + o, 128, dst, o, bf))
        fch.append((SOFF["sdt"], 8, c.dtffF, 0, False))
        nev = 0
        for g in range(c.NG):
            ts_ = slice(g * 512, (g + 1) * 512)
            x_ = xg[g % 2]
            xn_ = xn[g % 2]
            def prologue(gg):
                xx_ = xg[gg % 2]
                kb.dma(xx_[:], c.xT[:, gg * 512:(gg + 1) * 512].rearrange("(kc p) t -> p kc t", p=128))
                rstd_bc(c, xx_, sq, rs, c.psum[0])
                for kc in range(8):
                    kb.tt(xn[gg % 2][:, kc, :], xx_[:, kc, :], rs[:], ALU.mult, eng=("dve", "pool")[kc % 2])
            if g == 0:
                prologue(0)
            for i, (so, w, dst, ro, bf) in enumerate(fch):
                if i == 8 and g + 1 < c.NG:
                    prologue(g + 1)
                ps = c.psum[1 + i % 4]
                for kc in range(8):
                    kb.mm(ps[0:w, :], W[:, kc, so:so + w], xn_[:, kc, :], start=(kc == 0), stop=(kc == 7))
                st = (stB if bf else stF)[nev % 3]
                nev += 1
                kb.cp(st[0:w, :], ps[0:w, :], eng=("act", "dve")[nev % 2])
                kb.dma(dst[ro:ro + w, ts_], st[0:w, :])
            if ASTOP <= 4:
                continue
            for j in range(4):
                tsj = slice(g * 512 + j * 128, g * 512 + (j + 1) * 128)
                p1, p2 = c.psum[5], c.psum[6]
                for kc in range(8):
                    kb.mm(p1[:, :], xn_[:, kc, j * 128:(j + 1) * 128], W[:, kc, SOFF["dv"]:SOFF["dv"] + 512],
                          start=(kc == 0), stop=(kc == 7))
                for kc in range(8):
                    kb.mm(p2[:, 0:256], xn_[:, kc, j * 128:(j + 1) * 128], W[:, kc, SOFF["fv"]:SOFF["fv"] + 256],
                          start=(kc == 0), stop=(kc == 7))
                for kc in range(8 if ASTOP > 5 else 0):
                    kb.mm(p2[:, 256:264], xn_[:, kc, j * 128:(j + 1) * 128], W[:, kc, SOFF["sdt"]:SOFF["sdt"] + 8],
                          start=(kc == 0), stop=(kc == 7))
                sb_ = stTb[j % 2]
                sf_ = stT[j % 2]
                kb.cp(sb_[:, 0:256], p1[:, 0:256], eng="act")
                kb.cp(sf_[:, 0:256], p1[:, 256:512], eng="dve")
                kb.cp(sb_[:, 256:512], p2[:, 0:256], eng="act")
                if ASTOP > 5:
                    kb.cp(sf_[:, 256:264], p2[:, 256:264], eng="dve")
                if ASTOP != 5:
                    kb.dma(c.vd[tsj, :], sb_[:, 0:256])
                    kb.dma(c.fv[tsj, :], sb_[:, 256:512])
                    kb.dma(c.z[tsj, :], sf_[:, 0:256])
                if ASTOP > 6:
                    kb.dma(c.dtffT[tsj, :], sf_[:, 256:264])


def phase_B(c, l):
    kb, S, NG = c.kb, c.S, c.NG
    NB = S // 128
    lam_init = 0.8 - 0.6 * math.exp(-0.3 * l)
    with kb.scope() as es:
        maskb = kb.sb("B_mask", [128, 128], BF16, es)
        kb.dma(maskb[:], c.maskneg[:, :])
        epsc = kb.sb("B_eps", [128, 1], F32, es)
        kb.memset(epsc[:], 1e-6)
        onec = kb.sb("B_one", [128, 1], F32, es)
        kb.memset(onec[:], 1.0)
        mhalfB = kb.sb("B_mhalf", [128, 1], F32, es)
        kb.memset(mhalfB[:], -0.5)
        lv = kb.sb("B_lv", [128, 4, 32], F32, es)
        for i, t in enumerate((c.diff_lambda_q1, c.diff_lambda_k1, c.diff_lambda_q2, c.diff_lambda_k2)):
            kb.dma(lv[:, i, :], t[l:l + 1, :].partition_broadcast(128))
        lj = kb.sb("B_lj", [128, 32], F32, es)
        ls = kb.sb("B_ls", [128, 4], F32, es)
        kb.stt(lj[:], lv[:, 0, :], 1.0, lv[:, 1, :], ALU.mult, ALU.mult, accum=ls[:, 0:1])
        kb.stt(lj[:], lv[:, 2, :], 1.0, lv[:, 3, :], ALU.mult, ALU.mult, accum=ls[:, 1:2])
        kb.act(ls[:, 0:2], ls[:, 0:2], AF.Exp)
        kb.tt(ls[:, 2:3], ls[:, 1:2], ls[:, 0:1], ALU.subtract)
        kb.ts(ls[:, 3:4], ls[:, 2:3], -lam_init, ALU.add)
        neg_lam = ls[:, 3:4]
        gd = kb.sb("B_gd", [128, 64], F32, es)
        gf = kb.sb("B_gf", [128, 64], F32, es)
        kb.dma(gd[:], c.diff_subln[l:l + 1, :].partition_broadcast(128))
        kb.dma(gf[:], c.fox_norm[l:l + 1, :].partition_broadcast(128))
        kb.ts(gd[:], gd[:], 1.0 - lam_init, ALU.mult)
        chi = kb.sb("B_chi", [4, S], BF16, es)
        cmid = kb.sb("B_cmid", [4, S], BF16, es)
        clo = kb.sb("B_clo", [4, S], BF16, es)
        nhi = kb.sb("B_nhi", [4, S], BF16, es)
        nmid = kb.sb("B_nmid", [4, S], BF16, es)
        nlo = kb.sb("B_nlo", [4, S], BF16, es)
        with kb.scope() as es2:
            cf = kb.sb("B_cf", [4, S], F32, es2)
            c1 = kb.sb("B_c1", [4, S], F32, es2)
            ones4 = kb.sb("B_ones4", [4, S], F32, es2)
            fb = kb.sb("B_fb", [4, 1], F32, es2)
            kb.dma(cf[:], c.dtffF[4:8, :])
            kb.dma(fb[:], c.fox_f_bias[l, :].rearrange("(p o) -> p o", o=1))
            kb.ts(fb[:], fb[:], -1.0, ALU.mult)
            kb.memset(ones4[:], 1.0)
            kb.act(c1[:], cf[:], AF.Exp, bias=fb[:, 0:1], scale=-1.0)
            kb.act(c1[:], c1[:], AF.Ln, bias=onec[0:4, 0:1], scale=1.0)
            kb.ts(c1[:], c1[:], -1.0, ALU.mult)
            kb.scan(cf[:], ones4[:], c1[:], 0.0, ALU.mult, ALU.add)
            kb.cp(chi[:], cf[:])
            kb.tt(c1[:], cf[:], chi[:], ALU.subtract)
            kb.cp(cmid[:], c1[:])
            kb.tt(c1[:], c1[:], cmid[:], ALU.subtract)
            kb.cp(clo[:], c1[:])
            for a, b in ((nhi, chi), (nmid, cmid), (nlo, clo)):
                kb.ts(a[:], b[:], -1.0, ALU.mult)
        Ka = [kb.sb("B_Ka%d" % i, [128, S], BF16, es) for i in range(2)]
        Qa = [kb.sb("B_Qa%d" % i, [128, S], BF16, es) for i in range(2)]
        for i in range(2):
            kb.memset(Ka[i][:], 0.0, eng="pool")
            kb.memset(Qa[i][:], 0.0, eng="dve")
        Va = [kb.sb("B_Va%d" % i, [128, NB, 65], BF16, es) for i in range(2)]
        o1 = kb.sb("B_o1", [128, NB, 64], F32, es)
        pT = [kb.sb("B_pT%d" % i, [128, 512], BF16, es) for i in range(5)]
        SB = [c.psum[0], c.psum[1], c.psum[4], c.psum[5]]
        LA = 3
        sm = [kb.sb("B_sm%d" % i, [128, 8], F32, es) for i in range(4)]
        tmp = [kb.sb("B_tmp%d" % i, [128, 64], F32, es) for i in range(4)]
        junk = kb.sb("B_junk", [128, 64], F32, es)
        ybf = [kb.sb("B_ybf%d" % i, [128, 64], BF16, es) for i in range(4)]
        yst = [kb.sb("B_yst%d" % i, [64, 512], BF16, es) for i in range(2)]
        maps = [("d", m) for m in range(8)] + [("f", h) for h in range(4)]
        npt = 0
        nfin = 0
        ngrp = 0
        pending = []
        for mi, (kind, m) in enumerate(maps):
            K_, Q_, V_ = Ka[mi % 2], Qa[mi % 2], Va[mi % 2]
            if kind == "d":
                h = m // 2
                R = 36
                kb.dma(K_[0:32, :], c.kd[m * 32:(m + 1) * 32, :])
                kb.dma(Q_[0:32, :], c.qd[m * 32:(m + 1) * 32, :])
                kb.dma(K_[32:36, :], c.alibi_k[h, :, :])
                kb.dma(Q_[32:36, :], c.alibi_q[h, :, :])
                vsrc, yrow, gv = c.vd, h * 64, gd
            else:
                h = m
                R = 70
                kb.dma(K_[0:64, :], c.fk[h * 64:(h + 1) * 64, :])
                kb.dma(Q_[0:64, :], c.fq[h * 64:(h + 1) * 64, :])
                kb.memset(K_[64:70, :], 1.0)
                kb.memset(Q_[64:70, :], 1.0)
                for i, (a, b) in enumerate(((nhi, chi), (nmid, cmid), (nlo, clo))):
                    kb.dma(K_[64 + i:65 + i, :], a[h:h + 1, :])
                    kb.dma(Q_[67 + i:68 + i, :], b[h:h + 1, :])
                vsrc, yrow, gv = c.fv, 512 + h * 64, gf
            kb.dma(V_[:, :, 0:64], vsrc[:, h * 64:(h + 1) * 64].rearrange("(j p) d -> p j d", p=128))
            kb.memset(V_[:, :, 64:65], 1.0)
            for g in range(NG):
                tiles = list(range(4 * g + 4))
                Ob = c.psum[2 + (ngrp % 2)]
                ngrp += 1

                def emit_qk(j, g=g, K_=K_, Q_=Q_, R=R):
                    jj = max(0, j - 4 * g)
                    c0 = 128 * jj
                    diag = j >= 4 * g
                    ps_s = SB[j % 4]
                    kb.mm(ps_s[:, c0:512], K_[:, j * 128:(j + 1) * 128], Q_[:, g * 512 + c0:(g + 1) * 512],
                          start=True, stop=not diag)
                    if diag:
                        kb.mm(ps_s[:, c0:c0 + 128], c.identb[:], maskb[:], start=False, stop=True)

                for j0 in range(min(LA, len(tiles))):
                    emit_qk(j0)
                for j in tiles:
                    if j + LA < len(tiles):
                        emit_qk(j + LA)
                    jj = max(0, j - 4 * g)
                    c0 = 128 * jj
                    ps_s = SB[j % 4]
                    p_ = pT[npt % 5]
                    npt += 1
                    kb.act(p_[:, c0:512], ps_s[:, c0:512], AF.Exp)
                    for tb in range(jj, 4):
                        kb.mm(Ob[:, tb * 128:tb * 128 + 65], p_[:, tb * 128:(tb + 1) * 128], V_[:, j, :],
                              start=(j == 0 and tb == 0), stop=(j == 4 * g + tb), skip_group_check=True)
                    if j == 1 and pending:
                        pending.pop(0)()

                def fin(g=g, Ob=Ob, kind=kind, m=m, yrow=yrow, gv=gv):
                    nonlocal nfin
                    for tb in range(4):
                        O = Ob[:, tb * 128:tb * 128 + 65]
                        ib = g * 4 + tb
                        s_ = sm[nfin % 4]
                        t_ = tmp[nfin % 4]
                        y_ = ybf[nfin % 4]
                        nfin += 1
                        kb.recip(s_[:, 0:1], O[:, 64:65])
                        if kind == "d" and m % 2 == 0:
                            kb.ts(o1[:, ib, :], O[:, 0:64], s_[:, 0:1], ALU.mult)
                            continue
                        kb.ts(t_[:], O[:, 0:64], s_[:, 0:1], ALU.mult)
                        if kind == "d":
                            kb.stt(t_[:], t_[:], neg_lam, o1[:, ib, :], ALU.mult, ALU.add)
                        kb.stt(junk[:], t_[:], 1.0, t_[:], ALU.mult, ALU.mult, accum=s_[:, 1:2])
                        kb.ts(s_[:, 2:3], s_[:, 1:2], 1.0 / 64, ALU.mult, 1e-6, ALU.add)
                        kb.tt(s_[:, 3:4], s_[:, 2:3], mhalfB[:, 0:1], ALU.pow, eng="pool")
                        kb.stt(y_[:], t_[:], s_[:, 3:4], gv[:], ALU.mult, ALU.mult)
                        kb.tr(c.psT[0:64, tb * 128:(tb + 1) * 128], y_[:], c.identb[:])
                    if not (kind == "d" and m % 2 == 0):
                        ys = yst[g % 2]
                        kb.cp(ys[:], c.psT[0:64, 0:512])
                        kb.dma(c.yT[yrow:yrow + 64, g * 512:(g + 1) * 512], ys[:])

                pending.append(fin)
                if len(tiles) < 2 or NG == 1:
                    while pending:
                        pending.pop(0)()
        while pending:
            pending.pop(0)()


def phase_out(c, l, nm, src, K, wsrc, gpost, last, grouped_src=False):
    kb, S, NG = c.kb, c.S, c.NG
    with kb.scope() as es:
        c.eps_t = kb.sb(nm + "_eps", [128, 1], F32, es)
        kb.memset(c.eps_t[:], 1e-6)
        gp = kb.sb(nm + "_gp", [128, 8], F32, es)
        kb.dma(gp[:], gpost[l, :].rearrange("(kc p) -> p kc", p=128), allow_slow_non_contiguous=True)
        W = load_w_bf16(c, es, nm + "_W", lambda kc, so, ww: wsrc[l, kc * 128:(kc + 1) * 128, so:so + ww], K,
                        [(0, 0, 1024, 1.0)])
        yg = [kb.sb(nm + "_yg%d" % i, [128, K, 512], BF16, es) for i in range(2)]
        xg = [kb.sb(nm + "_xg%d" % i, [128, 8, 512], F32, es) for i in range(2)]
        yos = [kb.sb(nm + "_yo%d" % i, [128, 8, 512], F32, es) for i in range(2)]
        sq = kb.sb(nm + "_sq", [128, 8, 512], BF16, es)
        rs = kb.sb(nm + "_rs", [128, 512], F32, es)
        tmpo = [kb.sb(nm + "_tmp%d" % i, [128, 512], F32, es) for i in range(2)]
        if last:
            ost = [kb.sb(nm + "_ost%d" % i, [128, 1024], F32, es) for i in range(2)]
        def mm_part(g):
            ts_ = slice(g * 512, (g + 1) * 512)
            y_, x_, yo = yg[g % 2], xg[g % 2], yos[g % 2]
            if grouped_src:
                kb.dma(y_[:], src[g, :, :, :])
            else:
                kb.dma(y_[:], src[:, ts_].rearrange("(kc p) t -> p kc t", p=128))
            kb.dma(x_[:], c.xT[:, ts_].rearrange("(kc p) t -> p kc t", p=128))
            for oc in range(8):
                ps = c.psum[1 + oc % 4]
                for kc in range(K):
                    kb.mm(ps[:, :], W[:, kc, oc * 128:(oc + 1) * 128], y_[:, kc, :], start=(kc == 0), stop=(kc == K - 1))
                kb.cp(yo[:, oc, :], ps[:, :], eng=("act", "dve")[oc % 2])

        def post_part(g):
            ts_ = slice(g * 512, (g + 1) * 512)
            x_, yo = xg[g % 2], yos[g % 2]
            rstd_bc(c, yo, sq, rs, c.psum[0])
            for oc in range(8):
                t_ = tmpo[oc % 2]
                kb.stt(t_[:], yo[:, oc, :], gp[:, oc:oc + 1], rs[:], ALU.mult, ALU.mult, eng="dve")
                kb.tt(x_[:, oc, :], x_[:, oc, :], t_[:], ALU.add, eng="pool")
            if not last:
                kb.dma(c.xT[:, ts_].rearrange("(kc p) t -> p kc t", p=128), x_[:])
            else:
                for tb in range(4):
                    o_ = ost[tb % 2]
                    for half in range(2):
                        ps = c.psum[5 + half]
                        for q in range(4):
                            oc = half * 4 + q
                            kb.tr(ps[:, q * 128:(q + 1) * 128], x_[:, oc, tb * 128:(tb + 1) * 128], c.identf[:])
                        kb.cp(o_[:, half * 512:(half + 1) * 512], ps[:, :], eng=("act", "dve")[half])
                    kb.dma(c.out[g * 512 + tb * 128:g * 512 + (tb + 1) * 128, :], o_[:])

        mm_part(0)
        for g in range(NG):
            if g + 1 < NG:
                mm_part(g + 1)
            post_part(g)


def phase_F1(c, l):
    kb, S, NG = c.kb, c.S, c.NG
    NF = DFF // 128
    with kb.scope() as es:
        c.eps_t = kb.sb("F_eps", [128, 1], F32, es)
        kb.memset(c.eps_t[:], 1e-6)
        g_t = kb.sb("F_g", [128, 8], F32, es)
        kb.dma(g_t[:], c.norm_ffn_pre[l, :].rearrange("(kc p) -> p kc", p=128), allow_slow_non_contiguous=True)
        cw = kb.sb("F_cw", [128, 3, NF], F32, es)
        cb = kb.sb("F_cb", [128, NF], F32, es)
        for k in range(3):
            kb.dma(cw[:, k, :], c.ffn_conv_w[l, k, :].rearrange("(fc p) -> p fc", p=128), allow_slow_non_contiguous=True)
        kb.dma(cb[:], c.ffn_conv_b[l, :].rearrange("(fc p) -> p fc", p=128), allow_slow_non_contiguous=True)
        Wg = load_w_bf16(c, es, "F_Wg", lambda kc, so, ww: c.ffn_w_gate[l, kc * 128:(kc + 1) * 128, so:so + ww], 8,
                         [(0, 0, DFF, 1.0)], gscale=g_t)
        Wu = load_w_bf16(c, es, "F_Wu", lambda kc, so, ww: c.ffn_w_up[l, kc * 128:(kc + 1) * 128, so:so + ww], 8,
                         [(0, 0, DFF, 1.0)], gscale=g_t)
        xg = [kb.sb("F_xg%d" % i, [128, 8, 512], F32, es) for i in range(2)]
        sq = kb.sb("F_sq", [128, 8, 512], BF16, es)
        rs = kb.sb("F_rs", [128, 512], F32, es)
        xn = [kb.sb("F_xn%d" % i, [128, 8, 512], BF16, es) for i in range(2)]
        halo = kb.sb("F_halo", [128, NF, 2], F32, es)
        kb.memset(halo[:], 0.0)
        gb = [kb.sb("F_gb%d" % i, [128, 514], F32, es) for i in range(3)]
        acc = [kb.sb("F_acc%d" % i, [128, 512], F32, es) for i in range(3)]
        t1 = [kb.sb("F_t1%d" % i, [128, 512], F32, es) for i in range(3)]
        t2 = [kb.sb("F_t2%d" % i, [128, 512], F32, es) for i in range(3)]
        hst = [kb.sb("F_hst%d" % i, [128, 512], BF16, es) for i in range(3)]
        n = 0
        for g in range(NG):
            ts_ = slice(g * 512, (g + 1) * 512)
            x_ = xg[g % 2]
            xn_ = xn[g % 2]
            def prologue(gg):
                xx_ = xg[gg % 2]
                kb.dma(xx_[:], c.xT[:, gg * 512:(gg + 1) * 512].rearrange("(kc p) t -> p kc t", p=128))
                rstd_bc(c, xx_, sq, rs, c.psum[0])
                for kc in range(8):
                    kb.tt(xn[gg % 2][:, kc, :], xx_[:, kc, :], rs[:], ALU.mult, eng=("dve", "pool")[kc % 2])
            if g == 0:
                prologue(0)

            def fc_gen(fc, g=g, ts_=ts_, xn_=xn_):
                nonlocal n
                pg = c.psum[1 + (fc % 3)]
                pu = c.psum[4 + (fc % 3)]
                fs = slice(fc * 128, (fc + 1) * 128)
                gb_, a_, t1_, t2_ = gb[n % 3], acc[n % 3], t1[n % 3], t2[n % 3]
                h_ = hst[n % 3]
                n += 1
                for kc in range(8):
                    kb.mm(pg[:, :], Wg[:, kc, fs], xn_[:, kc, :], start=(kc == 0), stop=(kc == 7))
                kb.cp(gb_[:, 0:2], halo[:, fc, :], eng="pool")
                yield
                for kc in range(8):
                    kb.mm(pu[:, :], Wu[:, kc, fs], xn_[:, kc, :], start=(kc == 0), stop=(kc == 7))
                kb.cp(gb_[:, 2:514], pg[:, :], eng="act")
                yield
                kb.cp(halo[:, fc, :], gb_[:, 512:514], eng="pool")
                kb.ts(a_[:], gb_[:, 2:514], cw[:, 2, fc:fc + 1], ALU.mult, cb[:, fc:fc + 1], ALU.add, eng="pool")
                yield
                kb.stt(a_[:], gb_[:, 1:513], cw[:, 1, fc:fc + 1], a_[:], ALU.mult, ALU.add, eng="dve")
                kb.stt(a_[:], gb_[:, 0:512], cw[:, 0, fc:fc + 1], a_[:], ALU.mult, ALU.add, eng="dve")
                yield
                kb.act(t2_[:], a_[:], AF.Gelu_apprx_tanh)
                yield
                kb.tt(h_[:], t2_[:], pu[:, :], ALU.mult, eng="dve")
                kb.dma(c.hmidT[g, :, fc, :], h_[:])

            def fcs(g=g):
                for fc in range(NF):
                    if fc == 8 and g + 1 < NG:
                        prologue(g + 1)
                    yield fc_gen(fc)
            run_pipelined(fcs(), 3)


def phase_C(c, l):
    kb, S, NG = c.kb, c.S, c.NG
    with kb.scope() as es:
        triu = kb.sb("C_triu", [128, 128], F32, es)
        strl = kb.sb("C_strl", [128, 128], F32, es)
        onesf = kb.sb("C_onesf", [128, 128], F32, es)
        kb.dma(triu[:], c.triu[:, :])
        kb.dma(strl[:], c.strl[:, :])
        kb.memset(onesf[:], 1.0)
        epsc = kb.sb("C_eps", [128, 1], F32, es)
        kb.memset(epsc[:], 1e-6)
        onec = kb.sb("C_one", [128, 1], F32, es)
        kb.memset(onec[:], 1.0)
        cw = kb.sb("C_cw", [128, 4, 6], F32, es)
        cbias = kb.sb("C_cb", [128, 6], F32, es)
        for k in range(4):
            kb.dma(cw[:, k, :], c.ssm_conv_w[l, k, :].rearrange("(cc p) -> p cc", p=128), allow_slow_non_contiguous=True)
        kb.dma(cbias[:], c.ssm_conv_b[l, :].rearrange("(cc p) -> p cc", p=128), allow_slow_non_contiguous=True)
        prm = kb.sb("C_prm", [128, 3, 4], F32, es)
        kb.dma(prm[:, 0, :], c.ssm_dt_bias[l:l + 1, :].partition_broadcast(128))
        kb.dma(prm[:, 1, :], c.ssm_a_log[l:l + 1, :].partition_broadcast(128))
        kb.dma(prm[:, 2, :], c.ssm_d[l:l + 1, :].partition_broadcast(128))
        kb.act(prm[:, 1, :], prm[:, 1, :], AF.Exp)
        kb.ts(prm[:, 1, :], prm[:, 1, :], -1.0, ALU.mult)
        gn = kb.sb("C_gn", [128, 256], F32, es)
        kb.dma(gn[:], c.ssm_norm[l:l + 1, :].partition_broadcast(128))
        St = kb.sb("C_St", [128, 4, 64], F32, es)
        kb.memset(St[:], 0.0)
        Stb = kb.sb("C_Stb", [128, 4, 64], BF16, es)
        cbuf = [kb.sb("C_cbuf%d" % i, [128, 515], F32, es) for i in range(2)]
        acc = [kb.sb("C_acc%d" % i, [128, 512], F32, es) for i in range(2)]
        ux = [kb.sb("C_ux%d" % i, [128, 2, 512], F32, es) for i in range(2)]
        uB = [kb.sb("C_uB%d" % i, [128, 2, 512], BF16, es) for i in range(2)]
        uC = [kb.sb("C_uC%d" % i, [128, 2, 512], BF16, es) for i in range(2)]
        yst = [kb.sb("C_yst%d" % i, [128, 2, 512], BF16, es) for i in range(2)]
        NR = 2

        def ring(nm, shape, dt):
            return [kb.sb("C_%s%d" % (nm, i), shape, dt, es) for i in range(NR)]

        dtt = ring("dtt", [128, 8], F32)
        sm = ring("sm", [128, 32], F32)
        xtok = ring("xtok", [128, 256], F32)
        Btok = ring("Btok", [128, 256], BF16)
        xd = ring("xd", [128, 256], BF16)
        xde = ring("xde", [128, 256], BF16)
        zt = ring("zt", [128, 256], F32)
        MS = [kb.sb("C_MS%d" % i, [128, 128], F32, es) for i in range(8)]
        E = [kb.sb("C_E%d" % i, [128, 128], F32, es) for i in range(8)]
        MT = [kb.sb("C_MT%d" % i, [128, 128], BF16, es) for i in range(8)]
        y1 = ring("y1", [128, 256], F32)
        yy = ring("yy", [128, 256], F32)
        junk = ring("junk", [128, 128], F32)
        ybf = ring("ybf", [128, 256], BF16)
        nch = 0
        def conv_gen(g):
            ux_, uB_, uC_ = ux[g % 2], uB[g % 2], uC[g % 2]
            for cc in range(6):
                cb_ = cbuf[cc % 2]
                a_ = acc[cc % 2]
                rows = slice(cc * 128, (cc + 1) * 128)
                if g == 0:
                    kb.memset(cb_[:, 0:3], 0.0)
                    kb.dma(cb_[:, 3:515], c.xbc[rows, 0:512])
                else:
                    kb.dma(cb_[:, 0:515], c.xbc[rows, g * 512 - 3:(g + 1) * 512])
                kb.ts(a_[:], cb_[:, 3:515], cw[:, 3, cc:cc + 1], ALU.mult, cbias[:, cc:cc + 1], ALU.add, eng="pool")
                for k in (2, 1, 0):
                    kb.stt(a_[:], cb_[:, k:k + 512], cw[:, k, cc:cc + 1], a_[:], ALU.mult, ALU.add, eng="dve")
                if cc < 2:
                    kb.act(ux_[:, cc, :], a_[:], AF.Silu)
                elif cc < 4:
                    kb.act(uB_[:, cc - 2, :], a_[:], AF.Silu)
                else:
                    kb.act(uC_[:, cc - 4, :], a_[:], AF.Silu)
                yield

        for _ in conv_gen(0):
            pass
        for g in range(NG):
            ux_, uB_, uC_ = ux[g % 2], uB[g % 2], uC[g % 2]
            cg = conv_gen(g + 1) if g + 1 < NG else iter(())
            ys_ = yst[g % 2]

            def prep_gen(j, g=g, ux_=ux_, uB_=uB_):
                i = (g * 4 + j) % NR
                tsl = slice(g * 512 + j * 128, g * 512 + (j + 1) * 128)
                cs_ = slice(j * 128, (j + 1) * 128)
                dtt_, sm_, xtok_, Btok_, xd_, xde_, zt_ = dtt[i], sm[i], xtok[i], Btok[i], xd[i], xde[i], zt[i]
                kb.dma(dtt_[:], c.dtffT[tsl, :])
                kb.dma(zt_[:], c.z[tsl, :])
                xr, ax, ee, dtv, dA = sm_[:, 0:4], sm_[:, 4:8], sm_[:, 8:12], sm_[:, 12:16], sm_[:, 16:20]
                ecs, dte, etot = sm_[:, 20:24], sm_[:, 24:28], sm_[:, 28:32]
                px = c.psum[1]
                for q in range(2):
                    kb.tr(px[:, q * 128:(q + 1) * 128], ux_[:, q, cs_], c.identf[:])
                for q in range(2):
                    kb.tr(c.psT[:, q * 128:(q + 1) * 128], uB_[:, q, cs_], c.identb[:])
                kb.tt(xr, dtt_[:, 0:4], prm[:, 0, :], ALU.add)
                kb.ts(ax, xr, -1.0, ALU.mult)
                kb.tt(ax, ax, xr, ALU.max)
                yield
                kb.cp(xtok_[:], px[:, 0:256], eng="act")
                kb.cp(Btok_[:], c.psT[:, 0:256], eng="act")
                kb.act(ee, ax, AF.Exp, scale=-1.0)
                kb.act(ee, ee, AF.Ln, bias=onec[:, 0:1], scale=1.0)
                kb.act(zt_[:], zt_[:], AF.Silu)
                yield
                kb.stt(dtv, xr, 0.0, ee, ALU.max, ALU.add)
                kb.tt(dA, dtv, prm[:, 1, :], ALU.mult)
                yield
                pcs = c.psum[0]
                kb.mm(pcs[:, 0:4], triu[:], dA, start=True, stop=True)
                kb.mm(pcs[:, 4:8], onesf[:], dA, start=True, stop=True)
                yield
                kb.act(ecs, pcs[:, 0:4], AF.Exp)
                kb.act(etot, pcs[:, 4:8], AF.Exp)
                kb.cp(dte, pcs[:, 0:4])
                kb.tt(dte, pcs[:, 4:8], dte, ALU.subtract)
                yield
                kb.act(dte, dte, AF.Exp)
                yield
                for h in range(4):
                    hs = slice(h * 64, (h + 1) * 64)
                    e1, e2 = ("pool", "dve") if h % 2 == 0 else ("dve", "pool")
                    kb.ts(xd_[:, hs], xtok_[:, hs], dtv[:, h:h + 1], ALU.mult, eng=e1)
                    kb.ts(xde_[:, hs], xtok_[:, hs], dtv[:, h:h + 1], ALU.mult, dte[:, h:h + 1], ALU.mult, eng=e2)
                    kb.ts(MS[i * 4 + h][:], strl[:], dA[:, h:h + 1], ALU.mult, eng=e1)
                    yield

            def main_gen(j, g=g, uB_=uB_, uC_=uC_, ys_=ys_):
                i = (g * 4 + j) % NR
                cs_ = slice(j * 128, (j + 1) * 128)
                dtt_, sm_, xtok_, Btok_, xd_, xde_, zt_ = dtt[i], sm[i], xtok[i], Btok[i], xd[i], xde[i], zt[i]
                dA, ecs, etot = sm_[:, 16:20], sm_[:, 20:24], sm_[:, 28:32]
                y1_, yy_ = y1[i], yy[i]
                pL, pG, pY, pS = c.psum[2], c.psum[3], c.psum[4], c.psum[5]
                kb.cp(Stb[:], St[:], eng="pool")
                for grp in range(2):
                    kb.mm(pG[:, grp * 128:(grp + 1) * 128], uB_[:, grp, cs_], uC_[:, grp, cs_], start=True, stop=True)

                def chead(h):
                    grp = h // 2
                    hs = slice(h * 64, (h + 1) * 64)
                    MS_, E_, MT_ = MS[i * 4 + h], E[i * 4 + h], MT[i * 4 + h]
                    kb.mm(pL[:, h * 128:(h + 1) * 128], MS_[:], triu[:], start=True, stop=True)
                    yield
                    kb.act(E_[:], pL[:, h * 128:(h + 1) * 128], AF.Exp)
                    yield
                    kb.tt(E_[:], E_[:], triu[:], ALU.mult, eng=("pool", "dve")[h % 2])
                    yield
                    kb.tt(MT_[:], pG[:, grp * 128:(grp + 1) * 128], E_[:], ALU.mult)
                    yield
                    kb.mm(pY[:, h * 128:h * 128 + 64], MT_[:], xd_[:, hs], start=True, stop=True)
                    kb.mm(pY[:, h * 128 + 64:h * 128 + 128], uC_[:, grp, cs_], Stb[:, h, :], start=True, stop=True)
                    kb.mm(pS[:, hs], Btok_[:, grp * 128:(grp + 1) * 128], xde_[:, hs], start=True, stop=True)
                    yield
                    kb.stt(y1_[:, hs], xtok_[:, hs], prm[:, 2, h:h + 1], pY[:, h * 128:h * 128 + 64], ALU.mult, ALU.add)
                    kb.stt(yy_[:, hs], pY[:, h * 128 + 64:h * 128 + 128], ecs[:, h:h + 1], y1_[:, hs], ALU.mult, ALU.add)
                    kb.stt(St[:, h, :], St[:, h, :], etot[:, h:h + 1], pS[:, hs], ALU.mult, ALU.add)

                pend = [chead(h) for h in range(4)]
                gens = []
                while gens or pend:
                    if pend:
                        gens.append(pend.pop(0))
                    for g_ in list(gens):
                        try:
                            next(g_)
                        except StopIteration:
                            gens.remove(g_)
                    yield
                kb.tt(yy_[:], yy_[:], zt_[:], ALU.mult, eng="pool")
                yield
                s2 = dtt_
                for q in range(2):
                    qs = slice(q * 128, (q + 1) * 128)
                    kb.stt(junk[i][:], yy_[:, qs], 1.0, yy_[:, qs], ALU.mult, ALU.mult, accum=s2[:, 4 + q:5 + q])
                yield
                kb.act(s2[:, 4:6], s2[:, 4:6], AF.Ln, bias=epsc[:, 0:1], scale=1.0 / 128)
                kb.act(s2[:, 6:8], s2[:, 4:6], AF.Exp, scale=-0.5)
                yield
                yb_ = ybf[i]
                for q in range(2):
                    qs = slice(q * 128, (q + 1) * 128)
                    kb.stt(yb_[:, qs], yy_[:, qs], s2[:, 6 + q:7 + q], gn[:, qs], ALU.mult, ALU.mult)
                yield
                for q in range(2):
                    kb.tr(c.psT[:, 512 + q * 128:512 + (q + 1) * 128], yb_[:, q * 128:(q + 1) * 128], c.identb[:])
                yield
                for q in range(2):
                    kb.cp(ys_[:, q, cs_], c.psT[:, 512 + q * 128:512 + (q + 1) * 128], eng="act")

            def drain(gs):
                gs = list(gs)
                while gs:
                    for g_ in list(gs):
                        try:
                            next(g_)
                        except StopIteration:
                            gs.remove(g_)

            drain([prep_gen(0)])
            for j in range(4):
                gs = [main_gen(j)]
                if j + 1 < 4:
                    gs.append(prep_gen(j + 1))
                if j >= 1:
                    gs.append(cg)
                drain(gs)
            for _ in cg:
                pass
            kb.dma(c.yT[256:512, g * 512:(g + 1) * 512].rearrange("(q p) t -> p q t", p=128), ys_[:])


def phase_D(c, l):
    kb, S, NG = c.kb, c.S, c.NG
    with kb.scope() as es:
        def cst(nm, src):
            t = kb.sb("D_" + nm, [128, 128], F32, es)
            kb.dma(t[:], src[:, :])
            return t
        SU, SL, IU, bones = cst("SU", c.mSU), cst("SL", c.mSL), cst("IU", c.mIU), cst("bones", c.bones)
        rmask = kb.sb("D_rmask", [128, 512], F32, es)
        kb.dma(rmask[:], c.rmask[:, :])
        hsel = kb.sb("D_hsel", [128, 2], F32, es)
        kb.dma(hsel[:], c.hsel[:, :])
        onec = kb.sb("D_one", [128, 1], F32, es)
        kb.memset(onec[:], 1.0)
        mhalf = kb.sb("D_mhalf", [128, 1], F32, es)
        kb.memset(mhalf[:], -0.5)
        eps12 = kb.sb("D_eps12", [128, 1], F32, es)
        kb.memset(eps12[:], 1e-12)
        epsln = kb.sb("D_epsln", [128, 1], F32, es)
        kb.memset(epsln[:], 64e-5)
        mu = kb.sb("D_mu", [128, 7], F32, es)
        omm = kb.sb("D_omm", [128, 7], F32, es)
        kb.dma(mu[:], c.rwkv_mu[l, :].rearrange("(cc p) -> p cc", p=128), allow_slow_non_contiguous=True)
        kb.ts(omm[:], mu[:], -1.0, ALU.mult, 1.0, ALU.add)
        pp = kb.sb("D_pp", [128, 5, 2], F32, es)
        for i, t in enumerate((c.rwkv_w0, c.rwkv_a0, c.rwkv_k_k, c.rwkv_k_a)):
            kb.dma(pp[:, i, :], t[l, :].rearrange("(q p) -> p q", p=128), allow_slow_non_contiguous=True)
        kb.dma(pp[:, 4, :], c.rwkv_r_k[l, :, :].rearrange("(q hh) d -> (hh d) q", q=2), allow_slow_non_contiguous=True)
        kb.ts(pp[:, 0, :], pp[:, 0, :], -1.0, ALU.mult)
        w2t = kb.sb("D_w2", [32, 256], F32, es)
        kb.dma(w2t[:], c.rwkv_w2[l, :, :])
        a2t = kb.sb("D_a2", [64, 256], F32, es)
        kb.dma(a2t[32:64, :], c.rwkv_a2[l, :, :])
        g2t = kb.sb("D_g2", [128, 256], F32, es)
        kb.dma(g2t[64:128, :], c.rwkv_g2[l, :, :])
        lnw = kb.sb("D_lnw", [128, 256], F32, es)
        lnb = kb.sb("D_lnb", [128, 256], F32, es)
        kb.dma(lnw[:], c.rwkv_ln_w[l:l + 1, :].partition_broadcast(128))
        kb.dma(lnb[:], c.rwkv_ln_b[l:l + 1, :].partition_broadcast(128))
        Hs = [[kb.sb("D_H%d_%d" % (h, i), [128, 64], F32, es) for i in range(3)] for h in range(4)]
        for h in range(4):
            kb.memset(Hs[h][0][:], 0.0)
        hidx = [0, 0, 0, 0]

        def garr(nm, n=2, cols=512, dt=F32):
            return [kb.sb("D_%s%d" % (nm, i), [128, cols], dt, es) for i in range(n)]
        buf = garr("buf", 2, 513)
        m1 = garr("m1", 2)
        rr, kk_, vv = garr("rr"), garr("kk"), garr("vv")
        xx = kb.sb("D_xx", [128, 512], F32, es)
        txw = kb.sb("D_txw", [32, 512], F32, es)
        sxg = kb.sb("D_sxg", [128, 512], F32, es)
        aa, nlw, ncl = garr("aa"), garr("nlw"), garr("ncl")
        pc = garr("pc")
        kat, kt, bt, rt = garr("kat", dt=BF16), garr("kt", dt=BF16), garr("bt", dt=BF16), garr("rt", dt=BF16)
        t1, t2, prod = garr("t1"), garr("t2"), garr("prod")
        gtok = kb.sb("D_gtok", [128, 4, 256], F32, es)
        vtok = garr("vtok", 2, 256)
        vtokb = garr("vtokb", 2, 256, BF16)
        otok = garr("otok", 2, 256)
        rkt = garr("rkt", 2, 4)
        ymid = garr("ymid", 2, 256)
        ybf = garr("ybf", 2, 256, BF16)
        st6 = garr("st6", 2, 32)
        yst = [kb.sb("D_yst%d" % i, [128, 2, 512], BF16, es) for i in range(2)]
        NS = 4
        def slot(nm, cols=128, dt=BF16):
            return [kb.sb("D_s_%s%d" % (nm, i), [128, cols], dt, es) for i in range(NS)]
        BAa, BAb, IA, Xa, Xb = slot("BAa", 256), slot("BAb", 256), slot("IA"), slot("Xa"), slot("Xb")
        A3 = slot("A3", 384)
        mSUSL = kb.sb("D_mSUSL", [128, 256], F32, es)
        mSUIUIU = kb.sb("D_mSUIUIU", [128, 384], F32, es)
        kb.cp(mSUSL[:, 0:128], SU[:], eng="pool")
        kb.cp(mSUSL[:, 128:256], SL[:], eng="pool")
        kb.cp(mSUIUIU[:, 0:128], SU[:], eng="pool")
        kb.cp(mSUIUIU[:, 128:256], IU[:], eng="pool")
        kb.cp(mSUIUIU[:, 256:384], IU[:], eng="pool")
        RH, WU, nU0 = slot("RH"), slot("WU"), slot("nU0", 64)
        Wpad, btTp, ktTp, R1, R2 = slot("Wpad"), slot("btTp"), slot("ktTp"), slot("R1"), slot("R2")
        MpT, Npp = slot("MpT", dt=F32), slot("Npp", 64, dt=F32)
        WpadF, btTpF = slot("WpadF", dt=F32), slot("btTpF", dt=F32)
        Hb = slot("Hb", 64)
        for s_ in range(NS):
            for t_ in (Wpad, btTp, ktTp, R1, R2, WpadF, btTpF):
                kb.memset(t_[s_][:], 0.0)
        P0, P1, P2, P3, P4, P5, P6 = c.psum
        for g in range(NG):
            def load_mix(cc, dst):
                b_ = buf[cc % 2]
                rows = slice(cc * 128, (cc + 1) * 128)
                if g == 0:
                    kb.memset(b_[:, 0:1], 0.0)
                    kb.dma(b_[:, 1:513], c.rp[rows, 0:512])
                else:
                    kb.dma(b_[:, 0:513], c.rp[rows, g * 512 - 1:(g + 1) * 512])
                m_ = m1[cc % 2]
                kb.ts(m_[:], b_[:, 1:513], omm[:, cc:cc + 1], ALU.mult, eng="pool")
                kb.stt(dst, b_[:, 0:512], mu[:, cc:cc + 1], m_[:], ALU.mult, ALU.add)
            for q in range(2):
                load_mix(q, rr[q][:])
                load_mix(2 + q, kk_[q][:])
                load_mix(4 + q, vv[q][:])
            load_mix(6, xx[:])
            kb.act(txw[:], xx[0:32, :], AF.Tanh)
            kb.act(sxg[64:128, :], xx[64:128, :], AF.Sigmoid)
            for q in range(2):
                qs = slice(q * 128, (q + 1) * 128)
                kb.mm(P0[:, :], a2t[32:64, qs], xx[32:64, :], start=True, stop=True)
                kb.act(aa[q][:], P0[:, :], AF.Sigmoid, bias=pp[:, 1, q:q + 1], scale=1.0)
                kb.mm(P1[:, :], w2t[:, qs], txw[:], start=True, stop=True)
                kb.act(t1[q][:], P1[:, :], AF.Exp, bias=pp[:, 0, q:q + 1], scale=-1.0)
                kb.act(t1[q][:], t1[q][:], AF.Ln, bias=onec[:, 0:1], scale=1.0)
                kb.act(nlw[q][:], t1[q][:], AF.Exp, bias=mhalf[:, 0:1], scale=-1.0)
                kb.scan(ncl[q][:], rmask[:], nlw[q][:], 0.0, ALU.mult, ALU.add)
                kb.ts(t1[q][:], kk_[q][:], pp[:, 2, q:q + 1], ALU.mult, eng="pool")
                kb.tt(t2[q][:], t1[q][:], t1[q][:], ALU.mult, eng="pool")
                kb.mm(P2[:, :], bones[:], t2[q][:], start=True, stop=True)
                kb.act(t2[q][:], P2[:, :], AF.Ln, bias=eps12[:, 0:1], scale=1.0)
                kb.act(t2[q][:], t2[q][:], AF.Exp, scale=-0.5)
                kb.tt(t1[q][:], t1[q][:], t2[q][:], ALU.mult)
                kb.tt(t2[q][:], nlw[q][:], ncl[q][:], ALU.subtract, eng="pool")
                kb.act(t2[q][:], t2[q][:], AF.Exp)
                kb.tt(kat[q][:], t1[q][:], t2[q][:], ALU.mult)
                kb.act(t2[q][:], ncl[q][:], AF.Exp)
                kb.tt(t1[q][:], t1[q][:], aa[q][:], ALU.mult, eng="pool")
                kb.tt(bt[q][:], t1[q][:], t2[q][:], ALU.mult)
                kb.ts(t1[q][:], aa[q][:], -1.0, ALU.add, pp[:, 3, q:q + 1], ALU.mult, eng="pool")
                kb.stt(t1[q][:], t1[q][:], 1.0, kk_[q][:], ALU.add, ALU.mult)
                kb.tt(kt[q][:], t1[q][:], t2[q][:], ALU.mult)
                kb.stt(prod[q][:], rr[q][:], pp[:, 4, q:q + 1], t1[q][:], ALU.mult, ALU.mult)
                kb.act(pc[q][:], ncl[q][:], AF.Exp, scale=-1.0)
                kb.tt(rt[q][:], rr[q][:], pc[q][:], ALU.mult)
            ys_ = yst[g % 2]
            for j4 in range(4):
                pg_ = c.psum[j4]
                kb.mm(pg_[:, 0:256], sxg[64:128, j4 * 128:(j4 + 1) * 128], g2t[64:128, :], start=True, stop=True)
                kb.cp(gtok[:, j4, :], pg_[:, 0:256], eng=("act", "dve")[j4 % 2])

            def prep_gen(j):
                cs_ = slice(j * 128, (j + 1) * 128)
                ti = j % 2
                vt_, rk_, vb_ = vtok[ti], rkt[ti], vtokb[ti]
                for q in range(2):
                    kb.tr(P4[:, q * 128:(q + 1) * 128], vv[q][:, cs_], c.identf[:])
                for q in range(2):
                    kb.mm(P4[:, 256 + 2 * q:258 + 2 * q], prod[q][:, cs_], hsel[:], start=True, stop=True)
                yield
                kb.cp(vt_[:], P4[:, 0:256], eng="act")
                kb.cp(vb_[:], P4[:, 0:256], eng="dve")
                kb.cp(rk_[:], P4[:, 256:260], eng="act")
                yield

            def post_gen(j, ys_=ys_):
                cs_ = slice(j * 128, (j + 1) * 128)
                ti = j % 2
                vt_, ot_, rk_ = vtok[ti], otok[ti], rkt[ti]
                s6, ym, yb = st6[ti], ymid[ti], ybf[ti]
                for h in range(4):
                    hs = slice(h * 64, (h + 1) * 64)
                    kb.op("dve", lambda: c.nc.vector.bn_stats(s6[:, h * 6:h * 6 + 6].ap, ot_[:, hs].ap),
                          [ot_[:, hs]], [s6[:, h * 6:h * 6 + 6]])
                yield
                for h in range(4):
                    kb.op("dve", lambda: c.nc.vector.bn_aggr(s6[:, 24 + 2 * h:26 + 2 * h].ap, s6[:, h * 6:h * 6 + 6].ap),
                          [s6[:, h * 6:h * 6 + 6]], [s6[:, 24 + 2 * h:26 + 2 * h]])
                yield
                for h in range(4):
                    var = s6[:, 25 + 2 * h:26 + 2 * h]
                    kb.ts(var, var, 64e-5, ALU.add)
                    kb.tt(var, var, mhalf[:, 0:1], ALU.pow, eng="pool")
                yield
                for h in range(4):
                    hs = slice(h * 64, (h + 1) * 64)
                    var = s6[:, 25 + 2 * h:26 + 2 * h]
                    kb.ts(ym[:, hs], ot_[:, hs], s6[:, 24 + 2 * h:25 + 2 * h], ALU.subtract, var, ALU.mult)
                yield
                kb.tt(ym[:], ym[:], lnw[:], ALU.mult, eng="pool")
                kb.tt(ym[:], ym[:], lnb[:], ALU.add, eng="pool")
                yield
                for h in range(4):
                    hs = slice(h * 64, (h + 1) * 64)
                    kb.stt(ym[:, hs], vt_[:, hs], rk_[:, h:h + 1], ym[:, hs], ALU.mult, ALU.add)
                yield
                kb.tt(yb[:], ym[:], gtok[:, j, :], ALU.mult, eng="pool")
                yield
                for q in range(2):
                    kb.tr(c.psT[:, q * 128:(q + 1) * 128], yb[:, q * 128:(q + 1) * 128], c.identb[:])
                yield
                for q in range(2):
                    kb.cp(ys_[:, q, cs_], c.psT[:, q * 128:(q + 1) * 128], eng="act")

            for _ in prep_gen(0):
                pass
            for j in range(4):
                cs_ = slice(j * 128, (j + 1) * 128)
                ti = j % 2
                vb_, ot_ = vtokb[ti], otok[ti]
                def head_gen(h, j=j, cs_=cs_, vt_=vb_, ot_=ot_):
                    q, s_ = h // 2, h
                    hp = slice((h % 2) * 64, (h % 2) * 64 + 64)
                    hs = slice(h * 64, (h + 1) * 64)
                    kat_, kt_, bt_, rt_ = kat[q][hp, cs_], kt[q][hp, cs_], bt[q][hp, cs_], rt[q][hp, cs_]
                    V_ = vt_[:, hs]
                    PH = c.psum[h]
                    PY = (P6, P5)[h % 2]
                    PT = c.psT if h % 2 == 0 else c.psum[4][:, :].bitcast(BF16)
                    kb.mm(PH[:, 0:128], bt_, kat_, start=True, stop=True)
                    kb.mm(PH[:, 128:256], kat_, bt_, start=True, stop=True)
                    yield
                    kb.tt(BAa[s_][:], PH[:, 0:256], mSUSL[:], ALU.mult)
                    yield
                    kb.tt(Xa[s_][:], c.identf[:], BAa[s_][:, 0:128], ALU.subtract, eng="pool")
                    BAc, BAn, Xc, Xn = BAa[s_], BAb[s_], Xa[s_], Xb[s_]
                    PN = PH
                    for lev in range(5):
                        kb.mm(PN[:, 128:256], BAc[:, 0:128], BAc[:, 128:256], start=True, stop=True)
                        if lev < 4:
                            kb.mm(PN[:, 0:128], BAc[:, 128:256], BAc[:, 0:128], start=True, stop=True)
                        yield
                        kb.tt(IA[s_][:], PN[:, 128:256], c.identf[:], ALU.add)
                        if lev < 4:
                            kb.cp(BAn[:], PN[:, 0:256], eng="act")
                        yield
                        kb.mm(PN[:, 256:384], IA[s_][:], Xc[:], start=True, stop=True)
                        yield
                        kb.cp(Xn[:], PN[:, 256:384], eng="dve")
                        BAc, BAn = BAn, BAc
                        Xc, Xn = Xn, Xc
                        yield
                    TT = Xc
                    kb.mm(PH[:, 0:128], kt_, kat_, start=True, stop=True)
                    kb.mm(PH[:, 128:256], bt_, rt_, start=True, stop=True)
                    kb.mm(PH[:, 256:384], kt_, rt_, start=True, stop=True)
                    yield
                    kb.tt(A3[s_][:], PH[:, 0:384], mSUIUIU[:], ALU.mult)
                    yield
                    idb = c.identb[hp, hp]
                    TB = (256 + h * 192) if h % 2 == 0 else (520 + (h // 2) * 192)
                    kb.tr(PT[:, TB:TB + 64], kat_, idb)
                    kb.tr(PT[:, TB + 64:TB + 128], bt_, idb)
                    kb.tr(PT[:, TB + 128:TB + 192], kt_, idb)
                    yield
                    kb.cp(RH[s_][:, 0:64], PT[:, TB:TB + 64], eng="act")
                    kb.cp(btTp[s_][:, hp], PT[:, TB + 64:TB + 128], eng="act")
                    kb.cp(btTpF[s_][:, hp], PT[:, TB + 64:TB + 128], eng="dve")
                    kb.cp(ktTp[s_][:, hp], PT[:, TB + 128:TB + 192], eng="act")
                    yield
                    kb.mm(PH[:, 256:320], A3[s_][:, 0:128], V_, start=True, stop=True)
                    yield
                    kb.cp(RH[s_][:, 64:128], PH[:, 256:320], eng="dve")
                    yield
                    kb.mm(PH[:, 0:128], TT[:], RH[s_][:], start=True, stop=True)
                    yield
                    kb.cp(Wpad[s_][:, hp], PH[:, 0:64], eng="dve")
                    kb.cp(WpadF[s_][:, hp], PH[:, 0:64], eng="dve")
                    kb.ts(nU0[s_][:], PH[:, 64:128], -1.0, ALU.mult)
                    yield
                    kb.mm(PH[:, 128:256], Wpad[s_][:], A3[s_][:, 128:256], start=True, stop=True)
                    yield
                    kb.tt(R1[s_][hp, 0:64], rt_[:, 0:64], PH[hp, 128:192], ALU.subtract)
                    kb.tt(R2[s_][hp, 64:128], rt_[:, 64:128], PH[hp, 192:256], ALU.subtract)
                    yield
                    kb.mm(PY[:, hs], A3[s_][:, 256:384], V_, start=(h < 2), stop=False, skip_group_check=True)
                    kb.mm(PY[:, hs], A3[s_][:, 128:256], nU0[s_][:], start=False, stop=False, skip_group_check=True)
                    yield
                    for ci in range(2):
                        cr = slice(ci * 64, ci * 64 + 64)
                        Hc = Hs[h][hidx[h] % 3]
                        Hn = Hs[h][(hidx[h] + 1) % 3]
                        hidx[h] += 1
                        Rm = (R1, R2)[ci][s_]
                        kb.cp(Hb[s_][hp, :], Hc[hp, :], eng="pool")
                        kb.mm(PY[:, hs], Rm[hp, :], Hb[s_][hp, :], start=False, stop=(ci == 1), skip_group_check=True)
                        kb.mm(PH[:, 256:320], ktTp[s_][cr, :], V_[cr, :], start=True, stop=False)
                        kb.mm(PH[:, 256:320], btTp[s_][cr, :], nU0[s_][cr, :], start=False, stop=True)
                        kb.mm(PH[:, 384:512], WpadF[s_][cr, :], btTpF[s_][cr, :], start=True, stop=True)
                        yield
                        PC = pc[q][hp, j * 128 + ci * 64 + 63:j * 128 + ci * 64 + 64]
                        kb.ts(Npp[s_][hp, :], PH[hp, 256:320], PC, ALU.mult)
                        kb.tt(MpT[s_][hp, :], c.identf[hp, :], PH[hp, 384:512], ALU.subtract)
                        yield
                        kb.mm(PH[:, 0:64], MpT[s_][hp, :], Hc[hp, :], start=True, stop=True)
                        yield
                        kb.stt(Hn[hp, :], PH[hp, 0:64], PC, Npp[s_][hp, :], ALU.mult, ALU.add)
                        yield
                    kb.cp(ot_[:, hs], PY[:, hs], eng="act")
                def pp_gen(j=j):
                    if j > 0:
                        yield from post_gen(j - 1)
                    if j + 1 < 4:
                        yield from prep_gen(j + 1)

                def allg(j=j):
                    yield pp_gen()
                    for h in range(4):
                        yield head_gen(h)
                run_pipelined(allg(), 5)
            for _ in post_gen(3):
                pass
            kb.dma(c.yT[768:1024, g * 512:(g + 1) * 512].rearrange("(q p) t -> p q t", p=128), ys_[:])


def consts(S):
    bf = ml_dtypes.bfloat16
    d = {}
    d["ident"] = np.eye(128, dtype=np.float32)
    s = np.arange(128)
    d["maskneg"] = np.where(s[:, None] > s[None, :], -30000.0, 0.0).astype(bf)
    d["triu"] = (s[:, None] <= s[None, :]).astype(np.float32)
    d["strl"] = (s[:, None] > s[None, :]).astype(np.float32)
    same = (s[:, None] // 64) == (s[None, :] // 64)
    d["mSU"] = ((s[:, None] < s[None, :]) & same).astype(np.float32)
    d["mSL"] = ((s[:, None] > s[None, :]) & same).astype(np.float32)
    d["mIU"] = ((s[:, None] <= s[None, :]) & same).astype(np.float32)
    d["bones"] = same.astype(np.float32)
    d["rmask"] = np.tile((np.arange(512) % 64 != 0).astype(np.float32)[None, :], (128, 1))
    d["hsel"] = np.stack([(s < 64), (s >= 64)], 1).astype(np.float32)
    pos = np.arange(S)
    hi = ((pos // 64) * 64).astype(np.float32)
    lo = (pos % 64).astype(np.float32)
    ak = np.zeros((4, 4, S), np.float32)
    aq = np.zeros((4, 4, S), np.float32)
    for h in range(4):
        sl = 2.0 ** (-8.0 * (h + 1) / 4)
        ak[h, 0] = sl * hi; ak[h, 1] = sl * lo; ak[h, 2] = 1; ak[h, 3] = 1
        aq[h, 0] = 1; aq[h, 1] = 1; aq[h, 2] = -sl * hi; aq[h, 3] = -sl * lo
    d["alibi_k"] = ak.astype(bf); d["alibi_q"] = aq.astype(bf)
    return d


SEQ = 8192
DEPTH = 2
_CACHE = {}


def kernel(**inputs):
    S, L = SEQ, DEPTH
    if "nc" not in _CACHE:
        _CACHE["nc"] = build(S, L)
        _CACHE["consts"] = consts(S)
    nc = _CACHE["nc"]
    x = np.ascontiguousarray(np.asarray(inputs["x"], dtype=np.float32))
    B = x.shape[0]
    shared = {k: np.ascontiguousarray(np.asarray(v, dtype=np.float32)) for k, v in inputs.items() if k != "x"}
    shared.update(_CACHE["consts"])
    in_maps = []
    for b in range(B):
        m = dict(shared)
        m["x"] = x[b]
        in_maps.append(m)
    res = run_bass_kernel_spmd(nc, in_maps, core_ids=list(range(B)))
    return np.stack([np.asarray(r["out"]) for r in res.results], axis=0).astype(np.float32)
```
